# Optimizing a Trainium2 kernel written in Bass

```python
import math
import jax, jax.numpy as jnp
from jax import lax
import numpy as np

D_MODEL = 1024
BATCH = 4
SEQ = 8192
DEPTH = 1

CHUNK = 64
Q_BLOCK = 128
EPS = 1e-6

FOX_WIDTH = D_MODEL // 2
FOX_HEAD_DIM = 64
FOX_HEADS = FOX_WIDTH // FOX_HEAD_DIM

S5_WIDTH = D_MODEL // 2
S5_GROUP = 16
S5_GROUPS = S5_WIDTH // S5_GROUP
S5_STATE = 64
DT_MIN = 1e-3
DT_MAX = 1e-1

PROJ_SIZES = (FOX_WIDTH, FOX_WIDTH, FOX_WIDTH, FOX_HEADS, FOX_WIDTH,
              S5_WIDTH, S5_WIDTH,
              D_MODEL, D_MODEL)
PROJ_WIDTH = sum(PROJ_SIZES)

kernel_name = "fox_s5_gated_hybrid_block"


def rmsnorm(x, g):
    xf = x.astype(jnp.float32)
    y = xf * lax.rsqrt(jnp.mean(xf * xf, axis=-1, keepdims=True) + EPS)
    return (y * g.astype(jnp.float32)).astype(x.dtype)


def forgetting_attention(q, k, v, log_f):
    S = q.shape[1]
    Dh = q.shape[-1]
    F = jnp.cumsum(log_f.astype(jnp.float32), axis=1).transpose(0, 2, 1)
    scale = Dh ** -0.5
    neg = jnp.finfo(jnp.float32).min
    outs = []
    for i in range(S // Q_BLOCK):
        q0 = i * Q_BLOCK
        kend = q0 + Q_BLOCK
        qb = q[:, q0:kend]
        kb = k[:, :kend]
        vb = v[:, :kend]
        s = jnp.einsum('bqhd,bkhd->bhqk', qb, kb).astype(jnp.float32) * scale
        s = s + F[:, :, q0:kend, None] - F[:, :, None, :kend]
        t_idx = q0 + jnp.arange(Q_BLOCK)
        s_idx = jnp.arange(kend)
        s = jnp.where(s_idx[None, :] <= t_idx[:, None], s, neg)
        p = jax.nn.softmax(s, axis=-1).astype(v.dtype)
        outs.append(jnp.einsum('bhqk,bkhd->bqhd', p, vb))
    return jnp.concatenate(outs, axis=1)


def s5_ssm(u, a_re, a_im, log_dt, b_re, b_im, c_re, c_im, d_skip):
    Bsz, S, W = u.shape
    uf = u.astype(jnp.float32)
    ug = uf.reshape(Bsz, S, S5_GROUPS, S5_GROUP).astype(jnp.complex64)
    lam = lax.complex(a_re.astype(jnp.float32), a_im.astype(jnp.float32))
    dt = jnp.exp(log_dt.astype(jnp.float32))[:, None]
    a_bar = jnp.exp(lam * dt)
    bmat = lax.complex(b_re.astype(jnp.float32), b_im.astype(jnp.float32))
    b_bar = ((a_bar - 1.0) / lam)[:, :, None] * bmat
    bu = jnp.einsum('bsgc,gpc->sbgp', ug, b_bar)
    a_seq = jnp.broadcast_to(a_bar, bu.shape)

    def combine(left, right):
        return (right[0] * left[0], right[0] * left[1] + right[1])

    _, states = lax.associative_scan(combine, (a_seq, bu), axis=0)
    cmat = lax.complex(c_re.astype(jnp.float32), c_im.astype(jnp.float32))
    y = jnp.einsum('sbgp,gcp->bsgc', states, cmat).real.reshape(Bsz, S, W)
    y = y + d_skip.astype(jnp.float32) * uf
    return y.astype(u.dtype)


def setup_inputs(seed: int = 0) -> dict:
    key = jax.random.key(seed)
    ks = jax.random.split(key, 24)
    D, L, G, P, Cg = D_MODEL, DEPTH, S5_GROUPS, S5_STATE, S5_GROUP
    f32 = jnp.float32

    def nrm(k, shape, fan_in, s=1.0):
        return s * jax.random.normal(k, shape, f32) * fan_in ** -0.5

    x = jax.random.normal(ks[0], (BATCH, SEQ, D), f32)
    c = jax.random.normal(ks[1], (BATCH, D), f32)
    w_ada = nrm(ks[2], (L, D, 3 * D), D, 0.3)
    b_ada = 0.02 * jax.random.normal(ks[3], (L, 3 * D), f32)
    g_norm = 1.0 + 0.05 * jax.random.normal(ks[4], (L, D), f32)
    w_in = nrm(ks[5], (L, D, PROJ_WIDTH), D)
    b_f = 3.0 + 0.5 * jax.random.normal(ks[6], (L, FOX_HEADS), f32)
    n_idx = jnp.arange(P, dtype=f32)
    a_re = -0.5 + 0.01 * jax.random.normal(ks[7], (L, G, P), f32)
    a_im = jnp.broadcast_to(math.pi * n_idx, (L, G, P)) + 0.01 * jax.random.normal(ks[8], (L, G, P), f32)
    log_dt = jax.random.uniform(ks[9], (L, G), f32, math.log(DT_MIN), math.log(DT_MAX))
    b_re = nrm(ks[10], (L, G, P, Cg), 2 * Cg)
    b_im = nrm(ks[11], (L, G, P, Cg), 2 * Cg)
    c_re = nrm(ks[12], (L, G, Cg, P), 2 * P)
    c_im = nrm(ks[13], (L, G, Cg, P), 2 * P)
    d_skip = jax.random.normal(ks[14], (L, S5_WIDTH), f32)
    w_glu = nrm(ks[15], (L, S5_WIDTH, S5_WIDTH), S5_WIDTH)
    b_glu = 0.02 * jax.random.normal(ks[16], (L, S5_WIDTH), f32)
    w_up_a = nrm(ks[17], (L, FOX_WIDTH, D), FOX_WIDTH)
    w_up_b = nrm(ks[18], (L, S5_WIDTH, D), S5_WIDTH)
    w_out = nrm(ks[19], (L, D, D), D)
    g_final = 1.0 + 0.05 * jax.random.normal(ks[20], (D,), f32)
    return {"x": x, "c": c, "w_ada": w_ada, "b_ada": b_ada, "g_norm": g_norm,
            "w_in": w_in, "b_f": b_f, "a_re": a_re, "a_im": a_im, "log_dt": log_dt,
            "b_re": b_re, "b_im": b_im, "c_re": c_re, "c_im": c_im, "d_skip": d_skip,
            "w_glu": w_glu, "b_glu": b_glu, "w_up_a": w_up_a, "w_up_b": w_up_b,
            "w_out": w_out, "g_final": g_final}


def reference(x, c, w_ada, b_ada, g_norm, w_in, b_f, a_re, a_im, log_dt, b_re, b_im,
              c_re, c_im, d_skip, w_glu, b_glu, w_up_a, w_up_b, w_out, g_final):
    Bsz, S, D = x.shape
    offsets = []
    acc = 0
    for sz in PROJ_SIZES[:-1]:
        acc += sz
        offsets.append(acc)
    for l in range(DEPTH):
        mod = c @ w_ada[l] + b_ada[l]
        shift, scale, gate = jnp.split(mod, 3, axis=-1)
        h = rmsnorm(x, g_norm[l]) * (1.0 + scale[:, None, :]) + shift[:, None, :]

        proj = h @ w_in[l]
        q, k, v, f_logit, z_a, u, z_b, g_a, g_b = jnp.split(proj, offsets, axis=-1)

        hs = (Bsz, S, FOX_HEADS, FOX_HEAD_DIM)
        log_f = jax.nn.log_sigmoid((f_logit + b_f[l]).astype(jnp.float32))
        y_a = forgetting_attention(q.reshape(hs), k.reshape(hs), v.reshape(hs), log_f)
        y_a = y_a.reshape(Bsz, S, FOX_WIDTH) * jax.nn.silu(z_a)

        y_b = s5_ssm(u, a_re[l], a_im[l], log_dt[l], b_re[l], b_im[l], c_re[l], c_im[l], d_skip[l])
        y_b = jax.nn.gelu(y_b)
        y_b = y_b * jax.nn.sigmoid(y_b @ w_glu[l] + b_glu[l])
        y_b = y_b * jax.nn.silu(z_b)

        merged = jax.nn.sigmoid(g_a) * (y_a @ w_up_a[l]) + jax.nn.sigmoid(g_b) * (y_b @ w_up_b[l])
        x = x + gate[:, None, :] * (merged @ w_out[l])
    return rmsnorm(x, g_final)
```

```python
import math
import numpy as np
import ml_dtypes
from contextlib import ExitStack
import concourse.bass as bass
import concourse.mybir as mybir
from concourse.bass_utils import run_bass_kernel_spmd

F32 = mybir.dt.float32
BF16 = mybir.dt.bfloat16
AF = mybir.ActivationFunctionType
ALU = mybir.AluOpType

D = 1024
EPS = 1e-6
NCORES = 8
COMPUTE = ("pe", "act", "dve", "pool")
QUEUES = ("sp", "gq", "aq")
RING = 8


class Buf:
    __slots__ = ("name", "lw", "rd")

    def __init__(self, name):
        self.name = name
        self.lw = None
        self.rd = []


class Op:
    __slots__ = ("eng", "fn", "deps", "sig", "cnt", "ring", "seg")

    def __init__(self, eng, fn, seg):
        self.eng = eng
        self.fn = fn
        self.deps = []
        self.sig = False
        self.cnt = 0
        self.ring = None
        self.seg = seg


class Sched:
    def __init__(self, nc, sems):
        self.nc = nc
        self.sems = sems
        self.ops = []
        self.seg = 0
        self.eng_obj = {"pe": nc.tensor, "act": nc.scalar, "dve": nc.vector,
                        "pool": nc.gpsimd, "sp": nc.sync, "gq": nc.gpsimd, "aq": nc.scalar}
        self.cnt = {e: 0 for e in COMPUTE}
        self.qcnt = {q: 0 for q in QUEUES}
        self.waited = {}
        self.nops = 0

    def buf(self, name):
        return Buf(name)

    def add(self, eng, fn, r=(), w=()):
        op = Op(eng, fn, self.seg)
        seg = self.seg
        deps = op.deps
        for b in r:
            p = b.lw
            if p is not None and p.seg == seg:
                deps.append(p)
        for b in w:
            p = b.lw
            if p is not None and p.seg == seg:
                deps.append(p)
            for x in b.rd:
                if x.seg == seg:
                    deps.append(x)
        for b in r:
            b.rd.append(op)
        for b in w:
            b.lw = op
            b.rd = []
        self.ops.append(op)
        return op

    def pe(self, fn, r=(), w=()):
        return self.add("pe", fn, r, w)

    def act(self, fn, r=(), w=()):
        return self.add("act", fn, r, w)

    def dve(self, fn, r=(), w=()):
        return self.add("dve", fn, r, w)

    def pool(self, fn, r=(), w=()):
        return self.add("pool", fn, r, w)

    def dma(self, fn, r=(), w=(), q="sp"):
        return self.add(q, fn, r, w)

    @staticmethod
    def _stream(e):
        return "pool" if e == "gq" else ("act" if e == "aq" else e)

    def _wait(self, st, eng, key, val):
        wk = (st, key)
        if self.waited.get(wk, 0) >= val:
            return
        self.waited[wk] = val
        sem = self.sems[key[1]] if key[0] == "c" else self.sems[(key[1], key[2])]
        eng.wait_ge(sem, val)

    def flush(self):
        ops = self.ops
        last = {}
        for op in ops:
            if op.eng in COMPUTE:
                last[op.eng] = op
            for p in op.deps:
                if p.eng == "pe" and op.eng == "pe":
                    continue
                p.sig = True
        for op in last.values():
            op.sig = True
        for op in ops:
            if op.eng in COMPUTE:
                if op.sig:
                    self.cnt[op.eng] += 1
                    op.cnt = self.cnt[op.eng]
            else:
                k = self.qcnt[op.eng]
                self.qcnt[op.eng] += 1
                op.ring = (k % RING, 16 * (k // RING + 1), k)
        for op in ops:
            eng = self.eng_obj[op.eng]
            st = self._stream(op.eng)
            need = {}
            for p in op.deps:
                if p is op or (p.eng == "pe" and op.eng == "pe"):
                    continue
                if p.eng in COMPUTE:
                    key = ("c", p.eng)
                    val = p.cnt
                else:
                    key = ("q", p.eng, p.ring[0])
                    val = p.ring[1]
                if need.get(key, 0) < val:
                    need[key] = val
            if op.eng in QUEUES:
                r, v, k = op.ring
                if k >= RING:
                    key = ("q", op.eng, r)
                    if need.get(key, 0) < v - 16:
                        need[key] = v - 16
            for key, val in need.items():
                self._wait(st, eng, key, val)
            ins = op.fn(eng)
            if op.eng in COMPUTE:
                if op.sig:
                    ins.then_inc(self.sems[op.eng], 1)
            else:
                ins.then_inc(self.sems[(op.eng, op.ring[0])], 16)
        self.nops += len(ops)
        for st in ("pe", "act", "dve", "pool", "sp"):
            eng = self.eng_obj[st]
            for e in COMPUTE:
                if e != st and self.cnt[e] > 0:
                    self._wait(st, eng, ("c", e), self.cnt[e])
            for q in QUEUES:
                k = self.qcnt[q]
                for r in range(RING):
                    n = (k - r + RING - 1) // RING if k > r else 0
                    if n > 0:
                        self._wait(st, eng, ("q", q, r), 16 * n)
        self.ops = []
        self.seg += 1
        print("flush seg", self.seg, "nops", self.nops, "cnt", self.cnt, "qcnt", self.qcnt)


C_Q, C_K, C_V, C_F, C_ZA, C_U, C_ZB, C_GA, C_GB = 0, 512, 1024, 1536, 1544, 2056, 2568, 3080, 4104
PW = 5128


class Ctx:
    pass


def build(NB=8, dbg=None, flags=()):
    T = NB * 1024
    nc = bass.Bass("TRN2", target_bir_lowering=False)
    K = Ctx()
    K.nc = nc
    K.NB = NB
    K.T = T
    K.dbg = dbg or ()
    K.flags = flags

    def din(name, shape, dt=F32):
        return nc.dram_tensor(name, list(shape), dt, kind="ExternalInput").ap()

    def dscr(name, shape, dt):
        return nc.dram_tensor(name, list(shape), dt, kind="Internal").ap()

    K.x = din("x", [T, D])
    K.cT = din("cT", [128, 8])
    K.w_ada = din("w_ada", [D, 3 * D])
    K.bada_col = din("bada_col", [128, 24])
    K.bgate_rep = din("bgate_rep", [128, D])
    K.gn_col = din("gn_col", [128, 8])
    K.w_in = din("w_in", [D, PW])
    K.bf_rep = din("bf_rep", [128, 64])
    K.w_glu = din("w_glu", [512, 512])
    K.bglu_col = din("bglu_col", [128, 4])
    K.w_up_a = din("w_up_a", [512, D])
    K.w_up_b = din("w_up_b", [512, D])
    K.w_out = din("w_out", [D, D])
    K.gfin_rep = din("gfin_rep", [128, D])
    K.s5_lam = din("s5_lam", [32, 2, 2, 64])
    K.s5_ldt = din("s5_ldt", [128, 32])
    K.s5_b = din("s5_b", [128, 32, 16])
    K.s5_c = din("s5_c", [512, 2, 2, 64])
    K.s5_d = din("s5_d", [128, 32])
    K.s5_bs = din("s5_bs", [128, 32, 16])
    K.s5_zrep = din("s5_zrep", [128, 3, 2048])
    K.ident_b = din("ident_b", [128, 128], BF16)
    K.ident_f = din("ident_f", [128, 128])
    K.tri_f = din("tri_f", [128, 128])
    K.cm5 = din("cm5", [128, 5, 512], BF16)
    K.tmask = din("tmask", [128, 128])
    K.nvec = din("nvec", [128, 4])
    K.ecol_in = din("ecol", [128, 64], BF16)
    K.erow_in = din("erowsel", [8, 1024], BF16)
    K.wj_in = din("wj", [128, 2])
    K.mB_in = din("mB", [128, 512], BF16)
    K.out = nc.dram_tensor("out", [T // 2, D], F32, kind="ExternalOutput").ap()
    K.hT_d = dscr("hT_d", [NB, 128, 8, 1024], BF16)
    K.yaT_d = dscr("yaT_d", [NB // 2, 128, 4, 1024], BF16)
    K.c_d = dscr("c_d", [NB // 2, 8, 1024], BF16)
    K.wb_in = dscr("wb_in", [128, 8, PW], BF16)
    K.wb_glu = dscr("wb_glu", [128, 4, 512], BF16)
    K.wb_upa = dscr("wb_upa", [128, 4, D], BF16)
    K.wb_upb = dscr("wb_upb", [128, 4, D], BF16)
    K.wb_out = dscr("wb_out", [128, 8, D], BF16)
    K.tab_d = dscr("tab_d", [4, 128, 4096], BF16)
    K.tab_e = dscr("tab_e", [2, 4096], F32)
    K.ut_d = dscr("ut_d", [NB, 128, 32, 128], BF16)
    K.dbg_out = {}
    for name, shape, dt in (dbg or ()):
        K.dbg_out[name] = nc.dram_tensor("dbg_" + name, list(shape), dt, kind="ExternalOutput").ap()

    with ExitStack() as es:
        sems = {}
        for e in COMPUTE:
            sems[e] = es.enter_context(nc.semaphore("s_" + e))
        for q in QUEUES:
            for r in range(RING):
                sems[(q, r)] = es.enter_context(nc.semaphore(f"s_{q}{r}"))
        S = Sched(nc, sems)
        K.S = S
        K.es = es

        def sb(name, shape, dt, stack=es):
            t = stack.enter_context(nc.sbuf_tensor(name, list(shape), dt))
            return t, S.buf(name)

        def ps(name, shape, dt, stack=es):
            t = stack.enter_context(nc.psum_tensor(name, list(shape), dt))
            return t, S.buf(name)
        K.sb = sb
        K.ps = ps

        K.cst, K.b_cst = sb("cst", [128, 8], F32)
        K.id_b, K.b_idb = sb("id_b", [128, 128], BF16)
        K.id_f, K.b_idf = sb("id_f", [128, 128], F32)
        K.tri_fs, K.b_trif = sb("tri_fs", [128, 128], F32)
        K.tri_bs, K.b_trib = sb("tri_bs", [128, 128], BF16)
        K.ones_f, K.b_onesf = sb("ones_f", [128, 128], F32)
        K.ones_b, K.b_onesb = sb("ones_b", [128, 128], BF16)
        K.modc, K.b_modc = sb("modc", [128, 24], F32)
        K.acol, K.b_acol = sb("acol", [128, 8], F32)
        K.gate_bc, K.b_gate = sb("gate_bc", [128, D], F32)
        K.G, K.b_G = sb("G", [128, NB * 8, 8], F32)
        K.Gend, K.b_Gend = sb("Gend", [128, NB, 8], F32)
        K.Gcar, K.b_Gcar = sb("Gcar", [128, 8], F32)
        K.wj, K.b_wj = sb("wj_s", [128, 2], F32)
        K.T5, K.b_T5 = sb("T5", [128, 32, 128], BF16)
        K.BA5, K.b_BA5 = sb("BA5", [128, 32, 128], BF16)
        K.CA5, K.b_CA5 = sb("CA5", [128, 32, 128], BF16)

        S.dve(lambda e: e.memset(K.cst[:, 0:1], EPS), w=[K.b_cst])
        S.dve(lambda e: e.memset(K.cst[:, 1:2], 1.0), w=[K.b_cst])
        S.dve(lambda e: e.memset(K.cst[:, 2:3], 0.0), w=[K.b_cst])
        S.dve(lambda e: e.memset(K.ones_f[:], 1.0), w=[K.b_onesf])
        S.dve(lambda e: e.memset(K.ones_b[:], 1.0), w=[K.b_onesb])
        S.dve(lambda e: e.memset(K.Gcar[:], 0.0), w=[K.b_Gcar])
        S.dma(lambda e: e.dma_start(out=K.wj[:], in_=K.wj_in), w=[K.b_wj])
        S.dma(lambda e: e.dma_start(out=K.id_b[:], in_=K.ident_b), w=[K.b_idb])
        S.dma(lambda e: e.dma_start(out=K.id_f[:], in_=K.ident_f), w=[K.b_idf])
        S.dma(lambda e: e.dma_start(out=K.tri_fs[:], in_=K.tri_f), w=[K.b_trif])
        S.dve(lambda e: e.tensor_copy(out=K.tri_bs[:], in_=K.tri_fs[:]), r=[K.b_trif], w=[K.b_trib])

        S.flush()
        stage_setup(K)
        if "skip3" not in K.flags or "s5tab" in K.flags:
            s5_setup_b(K)
        if "no12" not in K.flags:
            phase1(K)
            if "no2" not in K.flags:
                phase2(K)
        if "skip3" not in K.flags:
            phase3(K)
    return nc


def stage_setup(K):
    nc, S = K.nc, K.S
    with ExitStack() as st:
        def sb(name, shape, dt):
            return K.sb(name, shape, dt, st)

        def ps(name, shape, dt):
            return K.ps(name, shape, dt, st)
        cT_s, b_cT = sb("cT_s", [128, 8], F32)
        bada_s, b_bada = sb("bada_s", [128, 24], F32)
        gn_s, b_gn = sb("gn_s", [128, 8], F32)
        bg_s, b_bg = sb("bg_s", [128, D], F32)
        cTrep, b_cTrep = sb("cTrep", [128, 8, 128], F32)
        w32 = [sb(f"w32_{i}", [128, 8, 512], F32) for i in range(2)]
        w16 = [sb(f"w16_{i}", [128, 8, 512], BF16) for i in range(2)]
        pmod, b_pmod = ps("pmod", [128, 24], F32)
        pgate = [ps(f"pgate{i}", [128, 512], F32) for i in range(2)]

        S.dma(lambda e: e.dma_start(out=cT_s[:], in_=K.cT), w=[b_cT])
        S.dma(lambda e: e.dma_start(out=bada_s[:], in_=K.bada_col), w=[b_bada])
        S.dma(lambda e: e.dma_start(out=gn_s[:], in_=K.gn_col), w=[b_gn])
        S.dma(lambda e: e.dma_start(out=bg_s[:], in_=K.bgate_rep), w=[b_bg])
        for kc in range(8):
            S.dve(lambda e, kc=kc: e.tensor_copy(out=cTrep[:, kc, :],
                                                 in_=cT_s[:, kc:kc + 1].to_broadcast([128, 128])),
                  r=[b_cT], w=[b_cTrep])
        for piece in range(6):
            wt, b_wt = w32[piece % 2]
            S.dma(lambda e, piece=piece, wt=wt: e.dma_start(
                out=wt[:], in_=K.w_ada[:, piece * 512:(piece + 1) * 512].rearrange("(kc p) n -> p kc n", p=128)),
                w=[b_wt])
            if piece < 4:
                for fc in range(4):
                    col = piece * 4 + fc
                    for kc in range(8):
                        S.pe(lambda e, fc=fc, kc=kc, col=col, wt=wt: e.matmul(
                            pmod[:, col:col + 1], lhsT=wt[:, kc, fc * 128:(fc + 1) * 128],
                            rhs=cT_s[:, kc:kc + 1], start=(kc == 0), stop=(kc == 7)),
                            r=[b_wt, b_cT], w=[b_pmod])
            else:
                pg, b_pg = pgate[piece - 4]
                for kc in range(8):
                    S.pe(lambda e, kc=kc, wt=wt, pg=pg: e.matmul(
                        pg[:], lhsT=cTrep[:, kc, :], rhs=wt[:, kc, :], start=(kc == 0), stop=(kc == 7)),
                        r=[b_wt, b_cTrep], w=[b_pg])
                half = piece - 4
                S.dve(lambda e, pg=pg, half=half: e.tensor_tensor(
                    out=K.gate_bc[:, half * 512:(half + 1) * 512], in0=pg[:],
                    in1=bg_s[:, half * 512:(half + 1) * 512], op=ALU.add),
                    r=[b_pg, b_bg], w=[K.b_gate])
        S.dve(lambda e: e.tensor_scalar(out=K.gate_bc[:], in0=K.gate_bc[:], scalar1=0.0625, scalar2=None,
                                        op0=ALU.mult), r=[K.b_gate], w=[K.b_gate])
        S.dve(lambda e: e.tensor_tensor(out=K.modc[:, 0:16], in0=pmod[:, 0:16], in1=bada_s[:, 0:16], op=ALU.add),
              r=[b_pmod, b_bada], w=[K.b_modc])
        S.dve(lambda e: e.scalar_tensor_tensor(out=K.acol[:], in0=K.modc[:, 8:16], scalar=1.0, in1=gn_s[:],
                                               op0=ALU.add, op1=ALU.mult), r=[K.b_modc, b_gn], w=[K.b_acol])

        if "skip3" not in K.flags or "s5tab" in K.flags:
            s5_setup_a(K, st)
        jobs = []
        c0 = 0
        while c0 < PW:
            c1 = min(c0 + 512, PW)
            jobs.append((K.w_in[:, c0:c1].rearrange("(kc p) n -> p kc n", p=128), K.wb_in[:, :, c0:c1], 8, c1 - c0))
            c0 = c1
        jobs.append((K.w_glu.rearrange("(kc p) n -> p kc n", p=128), K.wb_glu, 4, 512))
        for h in range(2):
            jobs.append((K.w_up_a[:, h * 512:(h + 1) * 512].rearrange("(kc p) n -> p kc n", p=128),
                         K.wb_upa[:, :, h * 512:(h + 1) * 512], 4, 512))
            jobs.append((K.w_up_b[:, h * 512:(h + 1) * 512].rearrange("(kc p) n -> p kc n", p=128),
                         K.wb_upb[:, :, h * 512:(h + 1) * 512], 4, 512))
            jobs.append((K.w_out[:, h * 512:(h + 1) * 512].rearrange("(kc p) n -> p kc n", p=128),
                         K.wb_out[:, :, h * 512:(h + 1) * 512], 8, 512, h))
        for i, job in enumerate(jobs):
            src, dst, nk, w = job[:4]
            wt, b_wt = w32[i % 2]
            wo, b_wo = w16[i % 2]
            S.dma(lambda e, src=src, wt=wt, nk=nk, w=w: e.dma_start(out=wt[:, 0:nk, 0:w], in_=src), w=[b_wt])
            if len(job) == 5:
                gh = job[4]
                S.dve(lambda e, wt=wt, gh=gh: e.tensor_tensor(
                    out=wt[:], in0=wt[:], in1=K.gate_bc[:, gh * 512:(gh + 1) * 512].unsqueeze(1).to_broadcast([128, 8, 512]),
                    op=ALU.mult), r=[b_wt, K.b_gate], w=[b_wt])
            S.act(lambda e, wt=wt, wo=wo, nk=nk, w=w: e.activation(out=wo[:, 0:nk, 0:w], in_=wt[:, 0:nk, 0:w], func=AF.Copy),
                  r=[b_wt], w=[b_wo])
            S.dma(lambda e, dst=dst, wo=wo, nk=nk, w=w: e.dma_start(out=dst, in_=wo[:, 0:nk, 0:w]), r=[b_wo], q="gq")
        S.flush()


PI = math.pi


def sin_eval(K, out_ap, x_ap, scr, r, w, shift=0.0):
    S = K.S
    t, ti, tf, b_s = scr
    lim = 3.14159
    S.dve(lambda e: e.tensor_scalar(out=t, in0=x_ap, scalar1=1.0 / (2 * PI), scalar2=64.5 + shift / (2 * PI),
                                    op0=ALU.mult, op1=ALU.add), r=r, w=[b_s])
    S.dve(lambda e: e.tensor_copy(out=ti, in_=t), r=[b_s], w=[b_s])
    S.dve(lambda e: e.tensor_copy(out=tf, in_=ti), r=[b_s], w=[b_s])
    S.dve(lambda e: e.tensor_tensor(out=t, in0=t, in1=tf, op=ALU.subtract), r=[b_s], w=[b_s])
    S.dve(lambda e: e.tensor_scalar(out=t, in0=t, scalar1=0.0, scalar2=None, op0=ALU.is_lt), r=[b_s], w=[b_s])
    S.dve(lambda e: e.tensor_tensor(out=tf, in0=tf, in1=t, op=ALU.subtract), r=[b_s], w=[b_s])
    S.dve(lambda e: e.tensor_scalar(out=tf, in0=tf, scalar1=-2 * PI, scalar2=128 * PI + shift,
                                    op0=ALU.mult, op1=ALU.add), r=[b_s], w=[b_s])
    S.dve(lambda e: e.tensor_tensor(out=t, in0=x_ap, in1=tf, op=ALU.add), r=list(r) + [b_s], w=[b_s])
    S.dve(lambda e: e.tensor_scalar(out=t, in0=t, scalar1=lim, scalar2=-lim, op0=ALU.min, op1=ALU.max),
          r=[b_s], w=[b_s])
    S.act(lambda e: e.activation(out=out_ap, in_=t, func=AF.Sin), r=[b_s], w=w)


def s5_setup_a(K, st):
    nc, S = K.nc, K.S
    I32 = mybir.dt.int32
    if True:
        def sb(name, shape, dt):
            return K.sb(name, shape, dt, st)

        def ps(name, shape, dt):
            return K.ps(name, shape, dt, st)
        lam_in, b_lam = sb("s5_lam_in", [32, 256], F32)
        ar, b_ar = sb("s5_ar", [128, 32], F32)
        ai, b_ai = sb("s5_ai", [128, 32], F32)
        dt_, b_dt = sb("s5_dt", [128, 32], F32)
        zr, b_zr = sb("s5_zr", [128, 32], F32)
        zi, b_zi = sb("s5_zi", [128, 32], F32)
        sm = [sb(f"s5_sm{i}", [128, 32], F32) for i in range(8)]
        smi, _ = sb("s5_smi", [128, 32], I32)
        ZR, b_ZR = sb("s5_ZR", [128, 16, 32], F32)
        ZI, b_ZI = sb("s5_ZI", [128, 16, 32], F32)
        MAG, b_MAG = sb("s5_MAG", [128, 16, 32], F32)
        SN, b_SN = sb("s5_SN", [128, 16, 32], F32)
        CS, b_CS = sb("s5_CS", [128, 16, 32], F32)
        ER, b_ER = sb("s5_ER", [128, 16, 32], F32)
        EI, b_EI = sb("s5_EI", [128, 16, 32], F32)
        ALR, b_ALR = sb("s5_ALR", [128, 16, 32], F32)
        ALI, b_ALI = sb("s5_ALI", [128, 16, 32], F32)
        CA16, b_CA16 = sb("s5_CA16", [128, 16, 32], F32)
        CB16, b_CB16 = sb("s5_CB16", [128, 16, 32], F32)
        w1, b_w1 = sb("s5_w1", [128, 16, 32], F32)
        w2, b_w2 = sb("s5_w2", [128, 16, 32], F32)
        w3, b_w3 = sb("s5_w3", [128, 16, 32], F32)
        wi, _ = sb("s5_wi", [128, 16, 32], I32)
        sgn, b_sgn = sb("s5_sgn", [128, 1], F32)
        Bn, b_Bn = sb("s5_Bn", [128, 32, 16], F32)
        Bs, b_Bs = sb("s5_Bs", [128, 32, 16], F32)
        Pm, b_Pm = sb("s5_Pm", [128, 32, 128], BF16)
        Pp, b_Pp = sb("s5_Pp", [128, 32, 128], F32)
        Qm, b_Qm = sb("s5_Qm", [128, 32, 128], BF16)
        c_in, b_cin = sb("s5_cin", [128, 4, 256], F32)
        CrT, b_CrT = sb("s5_CrT", [128, 32, 16], F32)
        CiT, b_CiT = sb("s5_CiT", [128, 32, 16], F32)
        u1, b_u1 = sb("s5_u1", [128, 32, 16], F32)
        u2, b_u2 = sb("s5_u2", [128, 32, 16], F32)
        Dcol, b_Dcol = sb("s5_Dcol", [128, 32], F32)
        tmask, b_tmask = sb("s5_tmask", [128, 128], F32)
        tt, b_tt = sb("s5_tt", [128, 128], F32)
        ptp = [ps(f"s5_ptp{i}", [128, 128], F32) for i in range(2)]

        S.dma(lambda e: e.dma_start(out=lam_in[:], in_=K.s5_lam.rearrange("g a d p -> g (a d p)")), w=[b_lam])
        S.dma(lambda e: e.dma_start(out=dt_[:], in_=K.s5_ldt), w=[b_dt])
        S.dma(lambda e: e.dma_start(out=Bn[:], in_=K.s5_b), w=[b_Bn])
        S.dma(lambda e: e.dma_start(out=Bs[:], in_=K.s5_bs), w=[b_Bs])
        S.dma(lambda e: e.dma_start(out=c_in[:], in_=K.s5_c.rearrange("(s q) a d p -> q s (a d p)", q=128)), w=[b_cin])
        S.dma(lambda e: e.dma_start(out=Dcol[:], in_=K.s5_d), w=[b_Dcol])
        S.dma(lambda e: e.dma_start(out=tmask[:], in_=K.tmask), w=[b_tmask])
        S.dve(lambda e: e.memset(sgn[0:64, :], -1.0), w=[b_sgn])
        S.dve(lambda e: e.memset(sgn[64:128, :], 1.0), w=[b_sgn])
        for which, (dst, b_dst) in enumerate(((ar, b_ar), (ai, b_ai))):
            pt, b_pt = ptp[which]
            S.pe(lambda e, which=which, pt=pt: e.transpose(out=pt[:, 0:32], in_=lam_in[:, which * 128:(which + 1) * 128],
                                                           identity=K.id_f[0:32, 0:32]), r=[b_lam, K.b_idf], w=[b_pt])
            S.dve(lambda e, pt=pt, dst=dst: e.tensor_copy(out=dst[:], in_=pt[:, 0:32]), r=[b_pt], w=[b_dst])
        S.act(lambda e: e.activation(out=dt_[:], in_=dt_[:], func=AF.Exp), r=[b_dt], w=[b_dt])
        S.dve(lambda e: e.tensor_tensor(out=zr[:], in0=ar[:], in1=dt_[:], op=ALU.mult), r=[b_ar, b_dt], w=[b_zr])
        S.dve(lambda e: e.tensor_tensor(out=zi[:], in0=ai[:], in1=dt_[:], op=ALU.mult), r=[b_ai, b_dt], w=[b_zi])
        for ti_, tau in enumerate(range(-7, 9)):
            S.dve(lambda e, ti_=ti_, tau=tau: e.tensor_scalar(out=ZR[:, ti_, :], in0=zr[:], scalar1=float(tau), scalar2=None,
                                                              op0=ALU.mult), r=[b_zr], w=[b_ZR])
            S.dve(lambda e, ti_=ti_, tau=tau: e.tensor_scalar(out=ZI[:, ti_, :], in0=zi[:], scalar1=float(tau), scalar2=None,
                                                              op0=ALU.mult), r=[b_zi], w=[b_ZI])
        S.act(lambda e: e.activation(out=MAG[:], in_=ZR[:], func=AF.Exp), r=[b_ZR], w=[b_MAG])
        scr = (w1[:], wi[:], w2[:], b_w1)
        sin_eval(K, SN[:], ZI[:], scr, [b_ZI], [b_SN])
        sin_eval(K, CS[:], ZI[:], scr, [b_ZI], [b_CS], shift=PI / 2)
        S.dve(lambda e: e.tensor_tensor(out=ER[:], in0=MAG[:], in1=CS[:], op=ALU.mult), r=[b_MAG, b_CS], w=[b_ER])
        S.dve(lambda e: e.tensor_tensor(out=EI[:], in0=MAG[:], in1=SN[:], op=ALU.mult), r=[b_MAG, b_SN], w=[b_EI])
        (th, b_th), (sh, b_sh), (am1r, b_am1r), (am1i, b_am1i), (rl2, b_rl2), (kr, b_kr), (ki, b_ki), (tq, b_tq) = sm
        S.act(lambda e: e.activation(out=th[:], in_=zr[:], func=AF.Tanh, scale=0.5), r=[b_zr], w=[b_th])
        S.dve(lambda e: e.tensor_scalar(out=tq[:], in0=zi[:], scalar1=0.5, scalar2=None, op0=ALU.mult), r=[b_zi], w=[b_tq])
        scr2 = (am1r[:], smi[:], am1i[:], b_am1r)
        sin_eval(K, sh[:], tq[:], scr2, [b_tq], [b_sh])
        i1 = 8
        S.dve(lambda e: e.scalar_tensor_tensor(out=am1r[:], in0=MAG[:, i1, :], scalar=1.0, in1=th[:], op0=ALU.add, op1=ALU.mult),
              r=[b_MAG, b_th, b_sh], w=[b_am1r])
        S.dve(lambda e: e.tensor_tensor(out=am1r[:], in0=am1r[:], in1=CS[:, i1, :], op=ALU.mult), r=[b_am1r, b_CS], w=[b_am1r])
        S.dve(lambda e: e.tensor_tensor(out=tq[:], in0=sh[:], in1=sh[:], op=ALU.mult), r=[b_sh], w=[b_tq])
        S.dve(lambda e: e.scalar_tensor_tensor(out=am1r[:], in0=tq[:], scalar=-2.0, in1=am1r[:], op0=ALU.mult, op1=ALU.add),
              r=[b_tq, b_am1r], w=[b_am1r])
        S.dve(lambda e: e.tensor_copy(out=am1i[:], in_=EI[:, i1, :]), r=[b_EI, b_am1r], w=[b_am1i])
        S.dve(lambda e: e.tensor_tensor(out=rl2[:], in0=ar[:], in1=ar[:], op=ALU.mult), r=[b_ar], w=[b_rl2])
        S.dve(lambda e: e.tensor_tensor(out=tq[:], in0=ai[:], in1=ai[:], op=ALU.mult), r=[b_ai], w=[b_tq])
        S.dve(lambda e: e.tensor_tensor(out=rl2[:], in0=rl2[:], in1=tq[:], op=ALU.add), r=[b_rl2, b_tq], w=[b_rl2])
        S.dve(lambda e: e.reciprocal(out=rl2[:], in_=rl2[:]), r=[b_rl2], w=[b_rl2])
        S.dve(lambda e: e.tensor_tensor(out=kr[:], in0=am1r[:], in1=ar[:], op=ALU.mult), r=[b_am1r, b_ar], w=[b_kr])
        S.dve(lambda e: e.tensor_tensor(out=tq[:], in0=am1i[:], in1=ai[:], op=ALU.mult), r=[b_am1i, b_ai], w=[b_tq])
        S.dve(lambda e: e.tensor_tensor(out=kr[:], in0=kr[:], in1=tq[:], op=ALU.add), r=[b_kr, b_tq], w=[b_kr])
        S.dve(lambda e: e.tensor_tensor(out=kr[:], in0=kr[:], in1=rl2[:], op=ALU.mult), r=[b_kr, b_rl2], w=[b_kr])
        S.dve(lambda e: e.tensor_tensor(out=ki[:], in0=am1i[:], in1=ar[:], op=ALU.mult), r=[b_am1i, b_ar], w=[b_ki])
        S.dve(lambda e: e.tensor_tensor(out=tq[:], in0=am1r[:], in1=ai[:], op=ALU.mult), r=[b_am1r, b_ai], w=[b_tq])
        S.dve(lambda e: e.tensor_tensor(out=ki[:], in0=ki[:], in1=tq[:], op=ALU.subtract), r=[b_ki, b_tq], w=[b_ki])
        S.dve(lambda e: e.tensor_tensor(out=ki[:], in0=ki[:], in1=rl2[:], op=ALU.mult), r=[b_ki, b_rl2], w=[b_ki])
        krb = kr[:].unsqueeze(1).to_broadcast([128, 16, 32])
        kib = ki[:].unsqueeze(1).to_broadcast([128, 16, 32])
        S.dve(lambda e: e.tensor_tensor(out=ALR[:], in0=ER[:], in1=krb, op=ALU.mult), r=[b_ER, b_kr], w=[b_ALR])
        S.dve(lambda e: e.tensor_tensor(out=w3[:], in0=EI[:], in1=kib, op=ALU.mult), r=[b_EI, b_ki], w=[b_w3])
        S.dve(lambda e: e.tensor_tensor(out=ALR[:], in0=ALR[:], in1=w3[:], op=ALU.subtract), r=[b_ALR, b_w3], w=[b_ALR])
        S.dve(lambda e: e.tensor_tensor(out=ALI[:], in0=ER[:], in1=kib, op=ALU.mult), r=[b_ER, b_ki], w=[b_ALI])
        S.dve(lambda e: e.tensor_tensor(out=w3[:], in0=EI[:], in1=krb, op=ALU.mult), r=[b_EI, b_kr], w=[b_w3])
        S.dve(lambda e: e.tensor_tensor(out=ALI[:], in0=ALI[:], in1=w3[:], op=ALU.add), r=[b_ALI, b_w3], w=[b_ALI])
        S.dve(lambda e: e.tensor_scalar(out=ALI[:], in0=ALI[:], scalar1=sgn[:, 0:1], scalar2=None, op0=ALU.mult),
              r=[b_ALI, b_sgn], w=[b_ALI])
        for fam in range(2):
            dst, b_dst = (Pm, b_Pm) if fam == 0 else (Pp, b_Pp)
            for i in range(8):
                ti_ = (7 - i) if fam == 0 else (14 - i)
                ca = ALR[:, ti_, :].unsqueeze(2).to_broadcast([128, 32, 16])
                cb = ALI[:, ti_, :].unsqueeze(2).to_broadcast([128, 32, 16])
                S.dve(lambda e, ca=ca: e.tensor_tensor(out=u1[:], in0=Bn[:], in1=ca, op=ALU.mult), r=[b_Bn, b_ALR], w=[b_u1])
                S.dve(lambda e, cb=cb: e.tensor_tensor(out=u2[:], in0=Bs[:], in1=cb, op=ALU.mult), r=[b_Bs, b_ALI], w=[b_u2])
                S.dve(lambda e, dst=dst, i=i: e.tensor_tensor(out=dst[:, :, i * 16:(i + 1) * 16], in0=u1[:], in1=u2[:], op=ALU.add),
                      r=[b_u1, b_u2], w=[b_dst])
        for g in range(32):
            pt, b_pt = ptp[g % 2]
            S.pe(lambda e, g=g, pt=pt: e.transpose(out=pt[:], in_=Pp[:, g, :], identity=K.id_f[:]), r=[b_Pp, K.b_idf], w=[b_pt])
            S.dve(lambda e, g=g, pt=pt: e.tensor_copy(out=K.BA5[:, g, :], in_=pt[:]), r=[b_pt], w=[K.b_BA5])
        for sl in range(4):
            for part, (dst, b_dst) in enumerate(((CrT, b_CrT), (CiT, b_CiT))):
                pt, b_pt = ptp[(sl * 2 + part) % 2]
                S.pe(lambda e, sl=sl, part=part, pt=pt: e.transpose(out=pt[:], in_=c_in[:, sl, part * 128:(part + 1) * 128],
                                                                    identity=K.id_f[:]), r=[b_cin, K.b_idf], w=[b_pt])
                S.dve(lambda e, sl=sl, pt=pt, dst=dst: e.tensor_copy(
                    out=dst[:, sl * 8:(sl + 1) * 8, :], in_=pt[:].rearrange("p (g c) -> p g c", g=8)), r=[b_pt], w=[b_dst])
        S.dve(lambda e: e.tensor_copy(out=CA16[0:64], in_=ER[0:64]), r=[b_ER], w=[b_CA16])
        S.dve(lambda e: e.tensor_scalar(out=CA16[64:128], in0=EI[64:128], scalar1=-1.0, scalar2=None, op0=ALU.mult), r=[b_EI], w=[b_CA16])
        S.dve(lambda e: e.tensor_scalar(out=CB16[0:64], in0=EI[0:64], scalar1=-1.0, scalar2=None, op0=ALU.mult), r=[b_EI], w=[b_CB16])
        S.dve(lambda e: e.tensor_scalar(out=CB16[64:128], in0=ER[64:128], scalar1=-1.0, scalar2=None, op0=ALU.mult), r=[b_ER], w=[b_CB16])
        for fam in range(2):
            dst, b_dst = (Qm, b_Qm) if fam == 0 else (K.CA5, K.b_CA5)
            for j in range(8):
                ti_ = (7 + j) if fam == 0 else (8 + j)
                ca = CA16[:, ti_, :].unsqueeze(2).to_broadcast([128, 32, 16])
                cb = CB16[:, ti_, :].unsqueeze(2).to_broadcast([128, 32, 16])
                S.dve(lambda e, ca=ca: e.tensor_tensor(out=u1[:], in0=CrT[:], in1=ca, op=ALU.mult), r=[b_CrT, b_CA16], w=[b_u1])
                S.dve(lambda e, cb=cb: e.tensor_tensor(out=u2[:], in0=CiT[:], in1=cb, op=ALU.mult), r=[b_CiT, b_CB16], w=[b_u2])
                S.dve(lambda e, dst=dst, j=j: e.tensor_tensor(out=dst[:, :, j * 16:(j + 1) * 16], in0=u1[:], in1=u2[:], op=ALU.add),
                      r=[b_u1, b_u2], w=[b_dst])
        for g in range(32):
            pt, b_pt = ptp[g % 2]
            S.pe(lambda e, g=g, pt=pt: e.matmul(pt[:], lhsT=Pm[:, g, :], rhs=Qm[:, g, :], start=True, stop=True),
                 r=[b_Pm, b_Qm], w=[b_pt])
            S.dve(lambda e, pt=pt: e.tensor_tensor(out=tt[:], in0=pt[:], in1=tmask[:], op=ALU.mult), r=[b_pt, b_tmask], w=[b_tt])
            S.dve(lambda e, g=g: e.scalar_tensor_tensor(out=K.T5[:, g, :], in0=K.id_f[:], scalar=Dcol[:, g:g + 1], in1=tt[:],
                                                        op0=ALU.mult, op1=ALU.add), r=[K.b_idf, b_Dcol, b_tt], w=[K.b_T5])


def s5_setup_b(K):
    nc, S = K.nc, K.S
    I32 = mybir.dt.int32
    with ExitStack() as st:
        def sb(name, shape, dt):
            return K.sb(name, shape, dt, st)
        zin_, b_zin = sb("s5c_zin", [128, 3, 2048], F32)
        nv, b_nv = sb("s5c_nv", [128, 4], F32)
        sc, b_sc = sb("s5c_sc", [128, 4], F32)
        zrn, b_zrn = sb("s5c_zrn", [128, 2048], F32)
        phi, b_phi = sb("s5c_phi", [128, 2048], F32)
        mag, b_mag = sb("s5c_mag", [128, 2048], F32)
        th_, b_th_ = sb("s5c_th", [128, 2048], F32)
        sn, b_sn = sb("s5c_sn", [128, 2048], F32)
        cs, b_cs = sb("s5c_cs", [128, 2048], F32)
        x1, b_x1 = sb("s5c_x1", [128, 2048], F32)
        x2, b_x2 = sb("s5c_x2", [128, 2048], F32)
        xi, _ = sb("s5c_xi", [128, 2048], I32)
        tA, b_tA = sb("s5c_tA", [128, 32, 2, 64], BF16)
        tB, b_tB = sb("s5c_tB", [128, 32, 2, 64], BF16)
        eA, b_eA = sb("s5c_eA", [1, 32, 2, 64], F32)
        eB, b_eB = sb("s5c_eB", [1, 32, 2, 64], F32)
        S.dma(lambda e: e.dma_start(out=zin_[:], in_=K.s5_zrep), w=[b_zin])
        S.dma(lambda e: e.dma_start(out=nv[:], in_=K.nvec), w=[b_nv])
        S.dve(lambda e: e.tensor_scalar(out=sc[:, 0:1], in0=nv[:, 1:2], scalar1=-8.0, scalar2=None, op0=ALU.mult), r=[b_nv], w=[b_sc])
        S.dve(lambda e: e.tensor_scalar(out=sc[:, 1:2], in0=nv[:, 0:1], scalar1=8.0, scalar2=None, op0=ALU.mult), r=[b_nv], w=[b_sc])
        S.dve(lambda e: e.tensor_scalar(out=sc[:, 2:3], in0=nv[:, 1:2], scalar1=-1.0, scalar2=None, op0=ALU.mult), r=[b_nv], w=[b_sc])
        S.dve(lambda e: e.tensor_copy(out=sc[:, 3:4], in_=nv[:, 0:1]), r=[b_nv], w=[b_sc])
        S.act(lambda e: e.activation(out=zin_[:, 2, :], in_=zin_[:, 2, :], func=AF.Exp), r=[b_zin], w=[b_zin])
        S.dve(lambda e: e.tensor_tensor(out=zrn[:], in0=zin_[:, 0, :], in1=zin_[:, 2, :], op=ALU.mult), r=[b_zin], w=[b_zrn])
        S.dve(lambda e: e.scalar_tensor_tensor(out=x1[:], in0=zin_[:, 1, :], scalar=8.0, in1=zin_[:, 2, :], op0=ALU.mult, op1=ALU.mult),
              r=[b_zin], w=[b_x1])
        S.dve(lambda e: e.tensor_scalar(out=x2[:], in0=x1[:], scalar1=1.0 / (2 * PI), scalar2=64.5, op0=ALU.mult, op1=ALU.add), r=[b_x1], w=[b_x2])
        S.dve(lambda e: e.tensor_copy(out=xi[:], in_=x2[:]), r=[b_x2], w=[b_x2])
        S.dve(lambda e: e.tensor_copy(out=phi[:], in_=xi[:]), r=[b_x2], w=[b_phi])
        S.dve(lambda e: e.tensor_tensor(out=x2[:], in0=x2[:], in1=phi[:], op=ALU.subtract), r=[b_x2, b_phi], w=[b_x2])
        S.dve(lambda e: e.tensor_scalar(out=x2[:], in0=x2[:], scalar1=0.0, scalar2=None, op0=ALU.is_lt), r=[b_x2], w=[b_x2])
        S.dve(lambda e: e.tensor_tensor(out=phi[:], in0=phi[:], in1=x2[:], op=ALU.subtract), r=[b_x2, b_phi], w=[b_phi])
        S.dve(lambda e: e.tensor_scalar(out=phi[:], in0=phi[:], scalar1=-2 * PI, scalar2=128 * PI, op0=ALU.mult, op1=ALU.add), r=[b_phi], w=[b_phi])
        S.dve(lambda e: e.tensor_tensor(out=phi[:], in0=phi[:], in1=x1[:], op=ALU.add), r=[b_phi, b_x1], w=[b_phi])
        scr = (x1[:], xi[:], x2[:], b_x1)
        for which in range(2):
            S.act(lambda e, which=which: e.activation(out=mag[:], in_=zrn[:], func=AF.Exp, scale=sc[:, which:which + 1]),
                  r=[b_zrn, b_sc], w=[b_mag])
            S.dve(lambda e, which=which: e.tensor_scalar(out=th_[:], in0=phi[:], scalar1=sc[:, 2 + which:3 + which], scalar2=None,
                                                         op0=ALU.mult), r=[b_phi, b_sc], w=[b_th_])
            sin_eval(K, sn[:], th_[:], scr, [b_th_], [b_sn])
            sin_eval(K, cs[:], th_[:], scr, [b_th_], [b_cs], shift=PI / 2)
            S.dve(lambda e: e.tensor_tensor(out=cs[:], in0=cs[:], in1=mag[:], op=ALU.mult), r=[b_cs, b_mag], w=[b_cs])
            S.dve(lambda e: e.tensor_tensor(out=sn[:], in0=sn[:], in1=mag[:], op=ALU.mult), r=[b_sn, b_mag], w=[b_sn])
            cs3 = cs[:].rearrange("p (g q) -> p g q", g=32)
            sn3 = sn[:].rearrange("p (g q) -> p g q", g=32)
            S.dve(lambda e, cs3=cs3: e.tensor_copy(out=tA[:, :, 0, :], in_=cs3), r=[b_cs], w=[b_tA])
            S.dve(lambda e, cs3=cs3: e.tensor_copy(out=tA[:, :, 1, :], in_=cs3), r=[b_cs], w=[b_tA])
            S.dve(lambda e, sn3=sn3: e.tensor_scalar(out=tB[:, :, 0, :], in0=sn3, scalar1=-1.0, scalar2=None, op0=ALU.mult), r=[b_sn], w=[b_tB])
            S.dve(lambda e, sn3=sn3: e.tensor_copy(out=tB[:, :, 1, :], in_=sn3), r=[b_sn], w=[b_tB])
            S.dma(lambda e, which=which: e.dma_start(out=K.tab_d[2 * which], in_=tA[:].rearrange("p g r q -> p (g r q)")), r=[b_tA], q="gq")
            S.dma(lambda e, which=which: e.dma_start(out=K.tab_d[2 * which + 1], in_=tB[:].rearrange("p g r q -> p (g r q)")), r=[b_tB], q="gq")
        S.act(lambda e: e.activation(out=mag[0:1, :], in_=zrn[0:1, :], func=AF.Exp, scale=1024.0), r=[b_zrn], w=[b_mag])
        S.dve(lambda e: e.tensor_scalar(out=th_[0:1, :], in0=phi[0:1, :], scalar1=128.0, scalar2=None, op0=ALU.mult), r=[b_phi], w=[b_th_])
        scr1 = (x1[0:1, :], xi[0:1, :], x2[0:1, :], b_x1)
        sin_eval(K, sn[0:1, :], th_[0:1, :], scr1, [b_th_], [b_sn])
        sin_eval(K, cs[0:1, :], th_[0:1, :], scr1, [b_th_], [b_cs], shift=PI / 2)
        S.dve(lambda e: e.tensor_tensor(out=cs[0:1, :], in0=cs[0:1, :], in1=mag[0:1, :], op=ALU.mult), r=[b_cs, b_mag], w=[b_cs])
        S.dve(lambda e: e.tensor_tensor(out=sn[0:1, :], in0=sn[0:1, :], in1=mag[0:1, :], op=ALU.mult), r=[b_sn, b_mag], w=[b_sn])
        cs3 = cs[0:1, :].rearrange("p (g q) -> p g q", g=32)
        sn3 = sn[0:1, :].rearrange("p (g q) -> p g q", g=32)
        S.dve(lambda e: e.tensor_copy(out=eA[:, :, 0, :], in_=cs3), r=[b_cs], w=[b_eA])
        S.dve(lambda e: e.tensor_copy(out=eA[:, :, 1, :], in_=cs3), r=[b_cs], w=[b_eA])
        S.dve(lambda e: e.tensor_scalar(out=eB[:, :, 0, :], in0=sn3, scalar1=-1.0, scalar2=None, op0=ALU.mult), r=[b_sn], w=[b_eB])
        S.dve(lambda e: e.tensor_copy(out=eB[:, :, 1, :], in_=sn3), r=[b_sn], w=[b_eB])
        S.dma(lambda e: e.dma_start(out=K.tab_e[0:1, :], in_=eA[:].rearrange("p g r q -> p (g r q)")), r=[b_eA], q="gq")
        S.dma(lambda e: e.dma_start(out=K.tab_e[1:2, :], in_=eB[:].rearrange("p g r q -> p (g r q)")), r=[b_eB], q="gq")
        S.flush()
        for nm, src in (("T5", K.T5), ("BA5", K.BA5), ("CA5", K.CA5)):
            if nm in K.dbg_out:
                S.dma(lambda e, nm=nm, src=src: e.dma_start(out=K.dbg_out[nm], in_=src[:]), q="gq")
        if "tab" in K.dbg_out:
            S.dma(lambda e: e.dma_start(out=K.dbg_out["tab"], in_=K.tab_d), q="gq")
        if "tabe" in K.dbg_out:
            S.dma(lambda e: e.dma_start(out=K.dbg_out["tabe"], in_=K.tab_e), q="gq")
        S.flush()


def phase1(K):
    nc, S, NB = K.nc, K.S, K.NB
    with ExitStack() as st:
        def sb(name, shape, dt):
            return K.sb(name, shape, dt, st)

        def ps(name, shape, dt):
            return K.ps(name, shape, dt, st)
        xt = [sb(f"p1_x{i}", [128, 8, D], F32) for i in range(2)]
        hns = [sb(f"p1_hn{i}", [128, 8, D], BF16) for i in range(2)]
        hT = [sb(f"p1_hT{i}", [128, 8, 1024], BF16) for i in range(2)]
        junk, b_junk = sb("p1_junk", [128, D], BF16)
        ss, b_ss = sb("p1_ss", [128, 8], F32)
        rstd, b_rstd = sb("p1_rstd", [128, 8], F32)
        wf, b_wf = sb("p1_wf", [128, 8, 8], BF16)
        wu1, b_wu1 = sb("p1_wu", [128, 8, 512], BF16)
        Ut1 = [sb(f"p1_Ut{i}", [128, 32, 128], BF16) for i in range(2)]
        pu1 = [ps(f"p1_pu{i}", [128, 512], F32) for i in range(2)]
        bf_s, b_bf = sb("p1_bf", [128, 64], F32)
        lgs = [sb(f"p1_lg{i}", [128, 8, 8], F32) for i in range(2)]
        ppfs = [sb(f"p1_ppfs{i}", [128, 16], F32) for i in range(2)]
        car = [sb(f"p1_car{i}", [128, 8], F32) for i in range(2)]
        cT32, b_cT32 = sb("p1_cT32", [8, 1024], F32)
        cT16, b_cT16 = sb("p1_cT16", [8, 1024], BF16)
        gcol, b_gcol = sb("p1_gcol", [8, 1], F32)
        ptr = [ps(f"p1_ptr{i}", [128, 1024], BF16) for i in range(2)]
        pf, b_pf = ps("p1_pf", [128, 8, 8], F32)
        ppf, b_ppf = ps("p1_ppf", [128, 16], F32)
        pcT, b_pcT = ps("p1_pcT", [8, 1024], F32)

        S.dma(lambda e: e.dma_start(out=wf[:], in_=K.wb_in[:, :, C_F:C_F + 8]), w=[b_wf])
        S.dma(lambda e: e.dma_start(out=wu1[:], in_=K.wb_in[:, :, C_U:C_U + 512]), w=[b_wu1])
        S.dma(lambda e: e.dma_start(out=bf_s[:], in_=K.bf_rep), w=[b_bf])
        def stageA(s):
            x_t, b_x = xt[s % 2]
            hn, b_hn = hns[s % 2]
            S.dma(lambda e, s=s, x_t=x_t: e.dma_start(
                out=x_t[:], in_=K.x[s * 1024:(s + 1) * 1024, :].rearrange("(n j) d -> n j d", j=8)), w=[b_x])
            for j in range(8):
                S.act(lambda e, j=j, x_t=x_t: e.activation(out=junk[:], in_=x_t[:, j, :], func=AF.Square,
                                                           accum_out=ss[:, j:j + 1]), r=[b_x], w=[b_junk, b_ss])
            S.act(lambda e: e.activation(out=rstd[:], in_=ss[:], func=AF.Sqrt, scale=1.0 / D, bias=K.cst[:, 0:1]),
                  r=[b_ss, K.b_cst], w=[b_rstd])
            S.dve(lambda e: e.reciprocal(out=rstd[:], in_=rstd[:]), r=[b_rstd], w=[b_rstd])
            for j in range(8):
                if j % 2 == 0:
                    S.dve(lambda e, j=j, x_t=x_t: e.tensor_scalar(
                        out=hn[:, j, :], in0=x_t[:, j, :], scalar1=rstd[:, j:j + 1], scalar2=None, op0=ALU.mult),
                        r=[b_x, b_rstd], w=[b_hn])
                else:
                    S.act(lambda e, j=j, x_t=x_t: e.activation(
                        out=hn[:, j, :], in_=x_t[:, j, :], func=AF.Copy, scale=rstd[:, j:j + 1]),
                        r=[b_x, b_rstd], w=[b_hn])

        def stageB(s):
            hn, b_hn = hns[s % 2]
            h_t, b_h = hT[s % 2]
            for kc in range(8):
                pt, b_pt = ptr[kc % 2]
                for j in range(8):
                    S.pe(lambda e, kc=kc, j=j, pt=pt: e.transpose(
                        out=pt[:, j * 128:(j + 1) * 128], in_=hn[:, j, kc * 128:(kc + 1) * 128],
                        identity=K.id_b[:]), r=[b_hn, K.b_idb], w=[b_pt])
                S.act(lambda e, kc=kc, pt=pt, h_t=h_t: e.activation(
                    out=h_t[:, kc, :], in_=pt[:], func=AF.Identity,
                    bias=K.modc[:, kc:kc + 1], scale=K.acol[:, kc:kc + 1]),
                    r=[b_pt, K.b_modc, K.b_acol], w=[b_h])
            S.dma(lambda e, s=s, h_t=h_t: e.dma_start(out=K.hT_d[s], in_=h_t[:]), r=[b_h], q="gq")
            if "skip3" not in K.flags:
                U_t, b_U = Ut1[s % 2]
                for j in range(8):
                    pu_, b_pu = pu1[j % 2]
                    for kc in range(8):
                        S.pe(lambda e, j=j, kc=kc, h_t=h_t, pu_=pu_: e.matmul(
                            pu_[:], lhsT=h_t[:, kc, j * 128:(j + 1) * 128], rhs=wu1[:, kc, :],
                            start=(kc == 0), stop=(kc == 7)), r=[b_h, b_wu1], w=[b_pu])
                    pu3 = pu_[:].rearrange("p (g c) -> p g c", g=32)
                    if j % 2 == 0:
                        S.dve(lambda e, j=j, pu3=pu3, U_t=U_t: e.tensor_copy(out=U_t[:, :, j * 16:(j + 1) * 16], in_=pu3),
                              r=[b_pu], w=[b_U])
                    else:
                        S.act(lambda e, j=j, pu3=pu3, U_t=U_t: e.activation(out=U_t[:, :, j * 16:(j + 1) * 16], in_=pu3, func=AF.Copy),
                              r=[b_pu], w=[b_U])
                S.dma(lambda e, s=s, U_t=U_t: e.dma_start(out=K.ut_d[s], in_=U_t[:]), r=[b_U], q="gq")
            for j in range(8):
                for kc in range(8):
                    S.pe(lambda e, j=j, kc=kc, h_t=h_t: e.matmul(
                        pf[:, j, :], lhsT=h_t[:, kc, j * 128:(j + 1) * 128], rhs=wf[:, kc, :],
                        start=(kc == 0), stop=(kc == 7)), r=[b_h, b_wf], w=[b_pf])
            lg, b_lg = lgs[s % 2]
            pfs, b_pfs = ppfs[s % 2]
            pf2 = pf[:].rearrange("p j h -> p (j h)")
            lg2 = lg[:].rearrange("p j h -> p (j h)")
            S.dve(lambda e, lg2=lg2, pf2=pf2: e.tensor_tensor(out=lg2, in0=pf2, in1=bf_s[:], op=ALU.add), r=[b_pf, b_bf], w=[b_lg])
            S.act(lambda e, lg2=lg2: e.activation(out=lg2, in_=lg2, func=AF.Exp, scale=-1.0), r=[b_lg], w=[b_lg])
            S.act(lambda e, lg2=lg2: e.activation(out=lg2, in_=lg2, func=AF.Ln, bias=K.cst[:, 1:2], scale=1.0),
                  r=[b_lg, K.b_cst], w=[b_lg])
            for j in range(1, 8):
                S.dve(lambda e, j=j, lg=lg: e.tensor_tensor(out=lg[:, j, :], in0=lg[:, j, :], in1=lg[:, j - 1, :], op=ALU.add),
                      r=[b_lg], w=[b_lg])
            S.pe(lambda e, lg=lg: e.matmul(ppf[:, 0:8], lhsT=K.tri_fs[:], rhs=lg[:, 7, :], start=True, stop=True),
                 r=[K.b_trif, b_lg], w=[b_ppf])
            S.pe(lambda e, lg=lg: e.matmul(ppf[:, 8:16], lhsT=K.ones_f[:], rhs=lg[:, 7, :], start=True, stop=True),
                 r=[K.b_onesf, b_lg], w=[b_ppf])
            S.dve(lambda e, pfs=pfs: e.tensor_copy(out=pfs[:], in_=ppf[:]), r=[b_ppf], w=[b_pfs])
            if s % 2 == 1:
                A, B = s - 1, s
                (lgA, b_lgA), (lgB, b_lgB) = lgs
                (pA, b_pA), (pB, b_pB) = ppfs
                (cA, b_cA), (cB, b_cB) = car
                S.dve(lambda e: e.scalar_tensor_tensor(out=cA[:], in0=pB[:, 8:16], scalar=K.wj[:, 0:1], in1=K.Gcar[:],
                                                       op0=ALU.mult, op1=ALU.add), r=[b_pB, K.b_wj, K.b_Gcar], w=[b_cA])
                S.dve(lambda e: e.scalar_tensor_tensor(out=cB[:], in0=pA[:, 8:16], scalar=K.wj[:, 1:2], in1=K.Gcar[:],
                                                       op0=ALU.mult, op1=ALU.add), r=[b_pA, K.b_wj, K.b_Gcar], w=[b_cB])
                for (P_, lgX, b_lgX, pX, b_pX, cX, b_cX) in ((A, lgA, b_lgA, pA, b_pA, cA, b_cA), (B, lgB, b_lgB, pB, b_pB, cB, b_cB)):
                    S.dve(lambda e, P_=P_, pX=pX, cX=cX: e.tensor_tensor(out=K.Gend[:, P_, :], in0=pX[:, 8:16], in1=cX[:], op=ALU.add),
                          r=[b_pX, b_cX], w=[K.b_Gend])
                    S.dve(lambda e, pX=pX, cX=cX: e.tensor_tensor(out=pX[:, 0:8], in0=pX[:, 0:8], in1=cX[:], op=ALU.add),
                          r=[b_pX, b_cX], w=[b_pX])
                    S.dve(lambda e, P_=P_, lgX=lgX, pX=pX: e.tensor_tensor(
                        out=K.G[:, P_ * 8:(P_ + 1) * 8, :], in0=lgX[:], in1=pX[:, 0:8].unsqueeze(1).to_broadcast([128, 8, 8]),
                        op=ALU.add), r=[b_lgX, b_pX], w=[K.b_G])
                S.dve(lambda e: e.tensor_tensor(out=K.Gcar[:], in0=K.Gcar[:], in1=pA[:, 8:16], op=ALU.add),
                      r=[K.b_Gcar, b_pA], w=[K.b_Gcar])
                S.dve(lambda e: e.tensor_tensor(out=K.Gcar[:], in0=K.Gcar[:], in1=pB[:, 8:16], op=ALU.add),
                      r=[K.b_Gcar, b_pB], w=[K.b_Gcar])
                for j in range(8):
                    S.pe(lambda e, A=A, j=j: e.transpose(out=pcT[:, j * 128:(j + 1) * 128], in_=K.G[:, A * 8 + j, :],
                                                         identity=K.id_f[:]), r=[K.b_G, K.b_idf], w=[b_pcT])
                S.dve(lambda e: e.tensor_copy(out=cT32[:], in_=pcT[:]), r=[b_pcT], w=[b_cT32])
                S.dve(lambda e: e.tensor_copy(out=gcol[:], in_=cT32[:, 1023:1024]), r=[b_cT32], w=[b_gcol])
                S.dve(lambda e: e.tensor_scalar(out=cT16[:], in0=cT32[:], scalar1=gcol[:, 0:1], scalar2=-1.0,
                                                op0=ALU.subtract, op1=ALU.mult), r=[b_cT32, b_gcol], w=[b_cT16])
                S.dma(lambda e, A=A: e.dma_start(out=K.c_d[A // 2], in_=cT16[:]), r=[b_cT16], q="gq")
            if s == 0 and "hT" in K.dbg_out:
                S.dma(lambda e, h_t=h_t: e.dma_start(out=K.dbg_out["hT"], in_=h_t[:]), r=[b_h], q="gq")
        stageA(0)
        for s in range(NB):
            if s + 1 < NB:
                stageA(s + 1)
            stageB(s)
        if "G" in K.dbg_out:
            S.dma(lambda e: e.dma_start(out=K.dbg_out["G"], in_=K.G[:]), r=[K.b_G], q="gq")
        S.flush()


def phase2(K):
    nc, S, NB = K.nc, K.S, K.NB
    T = K.T
    with ExitStack() as st:
        def sb(name, shape, dt):
            return K.sb(name, shape, dt, st)

        def ps(name, shape, dt):
            return K.ps(name, shape, dt, st)
        NR = NB // 2
        QA, _ = sb("QA", [128, T // 2], BF16)
        QB, _ = sb("QB", [128, T // 2], BF16)
        KA, _ = sb("KA", [128, T], BF16)
        KB, _ = sb("KB", [128, T], BF16)
        V2, _ = sb("V2", [128, NB * 8, 192], BF16)
        b_Q = [S.buf(f"Q{s}") for s in range(NB // 2)]
        b_K = [S.buf(f"K{s}") for s in range(NB)]
        b_V = [S.buf(f"V{s}") for s in range(NB)]
        hTb = [sb(f"p2_hT{i}", [128, 8, 1024], BF16) for i in range(2)]
        wq, b_wq = sb("p2_wq", [128, 8, 128], BF16)
        wk, b_wk = sb("p2_wk", [128, 8, 128], BF16)
        wv, b_wv = sb("p2_wv", [128, 8, 128], BF16)
        cm5, b_cm5 = sb("p2_cm5", [128, 5, 512], BF16)
        mB, b_mB = sb("p2_mB", [128, 512], BF16)
        biasq = [sb(f"p2_biasq{i}", [128, NB * 8, 8], F32) for i in range(2)]
        pT = [sb(f"p2_pT{i}", [128, 512], BF16) for i in range(4)]
        rden = [sb(f"p2_rden{i}", [128, 512], F32) for i in range(2)]
        num = [sb(f"p2_num{i}", [128, 512], F32) for i in range(2)]
        yo = [sb(f"p2_yo{i}", [128, 512], BF16) for i in range(2)]
        pss = [ps(f"p2_s{i}", [128, 512], F32) for i in range(3)]
        po = [ps(f"p2_o{i}", [128, 512], F32) for i in range(2)]
        pb, b_pb = ps("p2_b", [128, 512], F32)
        pp = [ps(f"p2_p{i}", [128, 512], F32) for i in range(2)]

        for i, tile_ in enumerate((QA, QB, KA, KB)):
            for s in range(NB // 2 if i < 2 else NB):
                bb = b_Q[s] if i < 2 else b_K[s]
                eng = S.pool if (i + s) % 2 else S.dve
                eng(lambda e, tile_=tile_, s=s: e.memset(tile_[:, s * 1024:(s + 1) * 1024], 0.0), w=[bb])
        for s in range(NB):
            S.dve(lambda e, s=s: e.memset(KA[64:65, s * 1024:(s + 1) * 1024], 1.0), w=[b_K[s]])
            S.dve(lambda e, s=s: e.memset(KB[0:1, s * 1024:(s + 1) * 1024], 1.0), w=[b_K[s]])
            S.pool(lambda e, s=s: e.memset(V2[:, s * 8:(s + 1) * 8, 64:128], 0.0), w=[b_V[s]])
            S.pool(lambda e, s=s: e.memset(V2[:, s * 8:(s + 1) * 8, 64:65], 1.0), w=[b_V[s]])
        S.dma(lambda e: e.dma_start(out=cm5[:], in_=K.cm5), w=[b_cm5])
        S.dma(lambda e: e.dma_start(out=mB[:], in_=K.mB_in), w=[b_mB])
        mBc, _ = sb("p2_mBc", [128, 1], F32)
        S.dve(lambda e: e.tensor_copy(out=mBc[:], in_=mB[:, 0:1]), r=[b_mB], w=[b_mB])

        ti = 0
        for hp in range(4):
            S.dma(lambda e, hp=hp: e.dma_start(out=wq[:], in_=K.wb_in[:, :, C_Q + hp * 128:C_Q + (hp + 1) * 128]), w=[b_wq])
            S.dma(lambda e, hp=hp: e.dma_start(out=wk[:], in_=K.wb_in[:, :, C_K + hp * 128:C_K + (hp + 1) * 128]), w=[b_wk])
            S.dma(lambda e, hp=hp: e.dma_start(out=wv[:], in_=K.wb_in[:, :, C_V + hp * 128:C_V + (hp + 1) * 128]), w=[b_wv])
            for s in range(NB):
                h_t, b_h = hTb[s % 2]
                S.dma(lambda e, s=s, h_t=h_t: e.dma_start(out=h_t[:], in_=K.hT_d[s]), w=[b_h])
                own = (s % 2 == 0)
                rr = s // 2
                if own:
                    S.dma(lambda e, rr=rr, hp=hp: e.dma_start(out=QA[64:65, rr * 1024:(rr + 1) * 1024],
                                                              in_=K.c_d[rr, 2 * hp:2 * hp + 1, :]), w=[b_Q[rr]])
                    S.dma(lambda e, rr=rr, hp=hp: e.dma_start(out=QB[0:1, rr * 1024:(rr + 1) * 1024],
                                                              in_=K.c_d[rr, 2 * hp + 1:2 * hp + 2, :]), w=[b_Q[rr]])
                for which in ((0, 1) if own else (1,)):
                    wt, b_wt = (wq, b_wq) if which == 0 else (wk, b_wk)
                    tA, tB = (QA, QB) if which == 0 else (KA, KB)
                    bb = b_Q[rr] if which == 0 else b_K[s]
                    scl = 0.125 if which == 0 else 1.0
                    for half in range(2):
                        p_t, b_p = pp[ti % 2]
                        ti += 1
                        for kc in range(8):
                            S.pe(lambda e, kc=kc, half=half, p_t=p_t, wt=wt, h_t=h_t: e.matmul(
                                p_t[:], lhsT=wt[:, kc, :], rhs=h_t[:, kc, half * 512:(half + 1) * 512],
                                start=(kc == 0), stop=(kc == 7)), r=[b_wt, b_h], w=[b_p])
                        c0 = (rr if which == 0 else s) * 1024 + half * 512
                        S.act(lambda e, p_t=p_t, tA=tA, c0=c0, scl=scl: e.activation(
                            out=tA[0:64, c0:c0 + 512], in_=p_t[0:64, :], func=AF.Copy, scale=scl),
                            r=[b_p], w=[bb])
                        S.dve(lambda e, p_t=p_t, tB=tB, c0=c0, scl=scl: e.tensor_scalar(
                            out=tB[64:128, c0:c0 + 512], in0=p_t[64:128, :], scalar1=scl, scalar2=None, op0=ALU.mult),
                            r=[b_p], w=[bb])
                for jh in range(2):
                    p_t, b_p = pp[ti % 2]
                    ti += 1
                    for jj in range(4):
                        j = jh * 4 + jj
                        for kc in range(8):
                            S.pe(lambda e, kc=kc, j=j, jj=jj, p_t=p_t, h_t=h_t: e.matmul(
                                p_t[:, jj * 128:(jj + 1) * 128], lhsT=h_t[:, kc, j * 128:(j + 1) * 128], rhs=wv[:, kc, :],
                                start=(kc == 0), stop=(kc == 7)), r=[b_wv, b_h], w=[b_p])
                    kt0 = s * 8 + jh * 4
                    pv = p_t[:].rearrange("p (j c) -> p j c", j=4)
                    S.dve(lambda e, pv=pv, kt0=kt0: e.tensor_copy(out=V2[:, kt0:kt0 + 4, 0:64], in_=pv[:, :, 0:64]),
                          r=[b_p], w=[b_V[s]])
                    S.act(lambda e, pv=pv, kt0=kt0: e.activation(out=V2[:, kt0:kt0 + 4, 128:192], in_=pv[:, :, 64:128],
                                                                 func=AF.Copy), r=[b_p], w=[b_V[s]])
            tiles = []
            for sq in range(NB // 2):
                nkt = (2 * sq + 2) * 8
                for hq in range(2):
                    for hl in range(2):
                        for kt in range(nkt):
                            tiles.append((sq, hq, hl, kt, nkt))
            n_t_ = len(tiles)
            LA = 1 if "la1" in K.flags else 2
            bias_done = set()
            pending = []
            gctr = [0]

            def emit_qk(i):
                sq, hq, hl, kt, nkt = tiles[i]
                if sq not in bias_done:
                    bias_done.add(sq)
                    bq, b_bq = biasq[sq % 2]
                    S.dve(lambda e, sq=sq, nkt=nkt, bq=bq: e.tensor_tensor(
                        out=bq[:, 0:nkt, :], in0=K.G[:, 0:nkt, :],
                        in1=K.Gend[:, 2 * sq, :].unsqueeze(1).to_broadcast([128, nkt, 8]), op=ALU.subtract),
                        r=[K.b_G, K.b_Gend], w=[b_bq])
                    k0 = (2 * sq + 1) * 8
                    S.dve(lambda e, bq=bq, k0=k0: e.tensor_scalar(out=bq[:, k0:k0 + 8, :], in0=bq[:, k0:k0 + 8, :], scalar1=mBc[:, 0:1],
                                                                 scalar2=None, op0=ALU.add), r=[b_bq, b_mB], w=[b_bq])
                sk, jk = kt // 8, kt % 8
                q0 = sq * 1024 + hq * 512
                Qt = QA if hl == 0 else QB
                Kt = KA if hl == 0 else KB
                s_t, b_s = pss[i % 3]
                diag = (sk == 2 * sq)
                partner = (sk == 2 * sq + 1)
                S.pe(lambda e: e.matmul(s_t[:], lhsT=Kt[:, kt * 128:(kt + 1) * 128], rhs=Qt[:, q0:q0 + 512],
                                        start=True, stop=(not diag)), r=[b_K[sk], b_Q[sq]], w=[b_s])
                if diag:
                    a = min(max(jk - 4 * hq, 0), 4)
                    S.pe(lambda e: e.matmul(s_t[:], lhsT=K.id_b[:], rhs=cm5[:, a, :], start=False, stop=True),
                         r=[K.b_idb, b_cm5], w=[b_s])


            def emit_exp(i):
                sq, hq, hl, kt, nkt = tiles[i]
                h = 2 * hp + hl
                s_t, b_s = pss[i % 3]
                p_t, b_p = pT[i % 4]
                bq, b_bq = biasq[sq % 2]
                S.act(lambda e: e.activation(out=p_t[:], in_=s_t[:], func=AF.Exp, bias=bq[:, kt, h:h + 1], scale=1.0),
                      r=[b_s, b_bq], w=[b_p])

            def emit_pv(i):
                sq, hq, hl, kt, nkt = tiles[i]
                sk = kt // 8
                p_t, b_p = pT[i % 4]
                if kt == 0:
                    gctr[0] += 1
                g = gctr[0]
                o_t, b_o = po[g % 2]
                if hl == 0:
                    S.pe(lambda e: e.matmul(o_t[0:65, :], lhsT=V2[:, kt, 0:65], rhs=p_t[:],
                                            start=(kt == 0), stop=(kt == nkt - 1)), r=[b_V[sk], b_p], w=[b_o])
                else:
                    S.pe(lambda e: e.matmul(o_t[:], lhsT=V2[:, kt, 64:192], rhs=p_t[:],
                                            start=(kt == 0), stop=(kt == nkt - 1)), r=[b_V[sk], b_p], w=[b_o])
                if kt == nkt - 1:
                    n_t, b_n = num[g % 2]
                    rd, b_rd = rden[g % 2]
                    y_t, b_y = yo[hq]
                    if hl == 0:
                        S.dve(lambda e: e.reciprocal(out=rd[64:65, :], in_=o_t[64:65, :]), r=[b_o], w=[b_rd])
                        S.dve(lambda e: e.tensor_copy(out=n_t[0:64, :], in_=o_t[0:64, :]), r=[b_o], w=[b_n])
                    else:
                        S.dve(lambda e: e.reciprocal(out=rd[0:1, :], in_=o_t[0:1, :]), r=[b_o], w=[b_rd])
                        S.dve(lambda e: e.tensor_copy(out=n_t[64:128, :], in_=o_t[64:128, :]), r=[b_o], w=[b_n])

                    def stage2():
                        if hl == 0:
                            S.pe(lambda e: e.matmul(pb[0:64, :], lhsT=K.ones_f[64:65, 0:64], rhs=rd[64:65, :],
                                                    start=True, stop=True), r=[K.b_onesf, b_rd], w=[b_pb])
                            S.dve(lambda e: e.tensor_tensor(out=y_t[0:64, :], in0=n_t[0:64, :], in1=pb[0:64, :], op=ALU.mult),
                                  r=[b_n, b_pb], w=[b_y])
                        else:
                            S.pe(lambda e: e.matmul(pb[:], lhsT=K.ones_f[0:1, :], rhs=rd[0:1, :],
                                                    start=True, stop=True), r=[K.b_onesf, b_rd], w=[b_pb])
                            S.dve(lambda e: e.tensor_tensor(out=y_t[64:128, :], in0=n_t[64:128, :], in1=pb[64:128, :], op=ALU.mult),
                                  r=[b_n, b_pb], w=[b_y])
                            S.dma(lambda e, hp=hp: e.dma_start(out=K.yaT_d[sq, :, hp, hq * 512:(hq + 1) * 512], in_=y_t[:]),
                                  r=[b_y], q="sp")
                    pending.append((i + (0 if "nodefer" in K.flags else 3), stage2))

            for i in range(min(LA, n_t_)):
                emit_qk(i)
            for i in range(n_t_):
                emit_exp(i)
                if i + LA < n_t_:
                    emit_qk(i + LA)
                emit_pv(i)
                while pending and pending[0][0] <= i:
                    pending.pop(0)[1]()
            while pending:
                pending.pop(0)[1]()
        S.flush()
        if "yaT" in K.dbg_out:
            S.dma(lambda e: e.dma_start(out=K.dbg_out["yaT"], in_=K.yaT_d), q="gq")
            S.flush()


def phase3(K):
    nc, S, NB = K.nc, K.S, K.NB
    GC = 0.7978845608028654
    with ExitStack() as st:
        def sb(name, shape, dt):
            return K.sb(name, shape, dt, st)

        def ps(name, shape, dt):
            return K.ps(name, shape, dt, st)
        hTb, b_h = sb("p3_hT", [128, 8, 1024], BF16)
        yaT, b_ya = sb("p3_yaT", [128, 4, 512], BF16)
        wua4, b_wua = sb("p3_wupa", [128, 4, 1024], BF16)
        wub4, b_wub = sb("p3_wupb", [128, 4, 1024], BF16)
        wgl, b_wgl = sb("p3_wglu", [128, 4, 512], BF16)
        wbuf = [sb(f"p3_w{i}", [128, 8, 512], BF16) for i in range(4)]
        tabs = [sb(f"p3_tab{i}", [128, 4, 512], BF16) for i in range(2)]
        UtXp, b_UtXp = sb("p3_UtXp", [128, 32, 128], BF16)
        UT, b_UT = sb("p3_UT", [128, 32, 128], BF16)
        SY, b_SY = sb("p3_SY", [128, 32, 128], BF16)
        XpT, b_XpT = sb("p3_XpT", [128, 32, 128], BF16)
        ybT, b_ybT = sb("p3_ybT", [128, 4, 1024], BF16)
        tmp = [sb(f"p3_t{i}", [128, 512], F32) for i in range(6)]
        thb = [sb(f"p3_th{i}", [128, 512], BF16) for i in range(4)]
        Xin8, b_Xin = sb("p3_Xin8", [8, 512], BF16)
        totB8, b_totB = sb("p3_totB8", [8, 512], F32)
        rows = [sb(f"p3_row{i}", [8, 512], F32) for i in range(4)]
        xown8, b_xown = sb("p3_xown8", [8, 512], BF16)
        e128, b_e128 = sb("p3_e128", [8, 2, 512], F32)
        ecol, b_ecol = sb("p3_ecol", [128, 64], BF16)
        erws, b_erws = sb("p3_erws", [8, 1024], BF16)
        xj = [sb(f"p3_x{i}", [128, D], F32) for i in range(2)]
        junk, b_junk = sb("p3_junk", [128, D], BF16)
        gfin, b_gfin = sb("p3_gfin", [128, D], F32)
        bgl, b_bgl = sb("p3_bgl", [128, 4], F32)
        ss2, b_ss2 = sb("p3_ss2", [128, 2], F32)
        PB = [ps(f"p3_pb{i}", [128, 1024], BF16) for i in range(2)]
        PF = [ps(f"p3_pf{i}", [128, 512], F32) for i in range(5)]
        P8 = ps("p3_p8", [128, 512], F32)
        merged = UT[:].rearrange("p g q -> p (g q)").rearrange("p (k n) -> p k n", k=8)
        yag = XpT[:].rearrange("p g q -> p (g q)")[:, 0:2048].rearrange("p (k n) -> p k n", k=4)
        ybg = XpT[:].rearrange("p g q -> p (g q)")[:, 2048:4096].rearrange("p (k n) -> p k n", k=4)
        cnt = {"w": 0, "f": 0, "b": 0, "t": 0, "h": 0, "tab": 0, "x": 0}

        def nxt(key, pool_):
            i = cnt[key]
            cnt[key] += 1
            return pool_[i % len(pool_)]

        wsrc = {"u": K.wb_in[:, :, C_U:C_U + 512], "za": K.wb_in[:, :, C_ZA:C_ZA + 512], "zb": K.wb_in[:, :, C_ZB:C_ZB + 512],
                "ga0": K.wb_in[:, :, C_GA:C_GA + 512], "gb0": K.wb_in[:, :, C_GB:C_GB + 512],
                "ga1": K.wb_in[:, :, C_GA + 512:C_GA + 1024], "gb1": K.wb_in[:, :, C_GB + 512:C_GB + 1024],
                "o0": K.wb_out[:, :, 0:512], "o1": K.wb_out[:, :, 512:1024]}
        stages = []
        for r_ in range(NB // 2):
            for h_ in range(2):
                stages += [["za"], ["zb"], ["ga0", "gb0"], ["ga1", "gb1"], ["o0", "o1"]]
        wstate = {"stage": -1, "issued": 0, "tiles": {}}

        def _issue_stage(k):
            if k >= len(stages) or k < wstate["issued"]:
                return
            for kk in range(wstate["issued"], k + 1):
                for key in stages[kk]:
                    wt, b_wt = nxt("w", wbuf)
                    S.dma(lambda e, wt=wt, key=key: e.dma_start(out=wt[:], in_=wsrc[key]), w=[b_wt])
                    wstate["tiles"][(kk, key)] = (wt, b_wt)
            wstate["issued"] = k + 1

        def wstage(expect):
            wstate["stage"] += 1
            k = wstate["stage"]
            assert stages[k] == expect, (k, stages[k], expect)
            _issue_stage(k)
            _issue_stage(k + 1)
            return [wstate["tiles"].pop((k, key)) for key in stages[k]]

        S.dma(lambda e: e.dma_start(out=gfin[:], in_=K.gfin_rep), w=[b_gfin])
        S.dma(lambda e: e.dma_start(out=bgl[:], in_=K.bglu_col), w=[b_bgl])
        S.dma(lambda e: e.dma_start(out=wua4[:], in_=K.wb_upa), w=[b_wua])
        S.dve(lambda e: e.tensor_scalar(out=wua4[:], in0=wua4[:], scalar1=4.0, scalar2=None, op0=ALU.mult), r=[b_wua], w=[b_wua])
        S.dma(lambda e: e.dma_start(out=wub4[:], in_=K.wb_upb), w=[b_wub])
        S.dma(lambda e: e.dma_start(out=wgl[:], in_=K.wb_glu), w=[b_wgl])
        print("phase3 sbuf bytes remaining/partition:", nc.sbuf_bytes_remaining)
        S.dve(lambda e: e.tensor_scalar(out=bgl[:], in0=bgl[:], scalar1=0.5, scalar2=None, op0=ALU.mult), r=[b_bgl], w=[b_bgl])
        S.dve(lambda e: e.memset(Xin8[:], 0.0), w=[b_Xin])
        S.dma(lambda e: e.dma_start(out=e128[:], in_=K.tab_e.rearrange("t (s q) -> s t q", s=8)), w=[b_e128])
        S.dma(lambda e: e.dma_start(out=ecol[:], in_=K.ecol_in), w=[b_ecol])
        S.dma(lambda e: e.dma_start(out=erws[:], in_=K.erow_in), w=[b_erws])

        def cscale(psrc, b_psrc, tabA, tabB, b_tab, dst, b_dst, extra_r=()):
            t1, b_t1 = nxt("t", tmp)
            t2, b_t2 = nxt("t", tmp)
            npart = dst.shape[0]
            S.dve(lambda e: e.tensor_tensor(out=t1[0:npart, :], in0=psrc, in1=tabA, op=ALU.mult),
                  r=[b_psrc, b_tab] + list(extra_r), w=[b_t1])
            p4 = psrc.rearrange("p (g r q) -> p g r q", g=4, r=2)
            b4 = tabB.rearrange("p (g r q) -> p g r q", g=4, r=2)
            t4 = t2[0:npart, :].rearrange("p (g r q) -> p g r q", g=4, r=2)
            S.dve(lambda e: e.tensor_tensor(out=t4[:, :, 0, :], in0=p4[:, :, 1, :], in1=b4[:, :, 0, :], op=ALU.mult),
                  r=[b_psrc, b_tab], w=[b_t2])
            S.dve(lambda e: e.tensor_tensor(out=t4[:, :, 1, :], in0=p4[:, :, 0, :], in1=b4[:, :, 1, :], op=ALU.mult),
                  r=[b_psrc, b_tab], w=[b_t2])
            S.dve(lambda e: e.tensor_tensor(out=dst, in0=t1[0:npart, :], in1=t2[0:npart, :], op=ALU.add),
                  r=[b_t1, b_t2], w=[b_dst])

        SY2 = SY[:].rearrange("p g q -> p (g q)")
        Xp2 = UtXp[:].rearrange("p g q -> p (g q)")
        w_ = K.wj[0:1, 0:1]
        v_ = K.wj[0:1, 1:2]

        def s5_front(pos):
            S.dma(lambda e: e.dma_start(out=UtXp[:], in_=K.ut_d[pos]), w=[b_UtXp])
            for g8 in range(4):
                pb_, b_pb = nxt("b", PB)
                for gg in range(8):
                    g = g8 * 8 + gg
                    S.pe(lambda e, g=g, gg=gg, pb_=pb_: e.transpose(out=pb_[:, gg * 128:(gg + 1) * 128], in_=UtXp[:, g, :],
                                                                    identity=K.id_b[:]), r=[b_UtXp, K.b_idb], w=[b_pb])
                src = pb_[:].rearrange("p (g q) -> p g q", g=8)
                if g8 % 2 == 0:
                    S.act(lambda e, g8=g8, src=src: e.activation(out=UT[:, g8 * 8:(g8 + 1) * 8, :], in_=src, func=AF.Copy),
                          r=[b_pb], w=[b_UT])
                else:
                    S.dve(lambda e, g8=g8, src=src: e.tensor_copy(out=UT[:, g8 * 8:(g8 + 1) * 8, :], in_=src), r=[b_pb], w=[b_UT])

        def s5_sums(p8, b_p8):
            def emit_pS(sl):
                pS, b_pS = nxt("f", PF)
                for gg in range(4):
                    g = sl * 4 + gg
                    S.pe(lambda e, g=g, gg=gg: e.matmul(pS[:, gg * 128:(gg + 1) * 128], lhsT=UT[:, g, :], rhs=K.BA5[:, g, :],
                                                        start=True, stop=True), r=[b_UT, K.b_BA5], w=[b_pS])
                return pS, b_pS
            cur = emit_pS(0)
            for sl in range(8):
                tb, b_tb = nxt("tab", tabs)
                S.dma(lambda e, sl=sl, tb=tb: e.dma_start(
                    out=tb[:, 0:2, :], in_=K.tab_d[0:2, :, sl * 512:(sl + 1) * 512].rearrange("t p q -> p t q")), w=[b_tb])
                nxt_ = emit_pS(sl + 1) if sl + 1 < 8 else None
                pS, b_pS = cur
                cscale(pS[:], b_pS, tb[:, 0, :], tb[:, 1, :], b_tb, SY2[:, sl * 512:(sl + 1) * 512], b_SY)
                S.pe(lambda e, sl=sl: e.matmul(p8[0:8, :], lhsT=ecol[:, sl * 8:(sl + 1) * 8], rhs=SY2[:, sl * 512:(sl + 1) * 512],
                                               start=(sl == 0), stop=(sl == 7)), r=[b_ecol, b_SY], w=[b_p8])
                cur = nxt_

        def block(r):
            A, Bp = 2 * r, 2 * r + 1
            S.dma(lambda e: e.dma_start(out=hTb[:], in_=K.hT_d[A]), w=[b_h])
            s5_front(Bp)
            p8, b_p8 = P8
            s5_sums(p8, b_p8)
            S.dve(lambda e, p8=p8: e.tensor_copy(out=totB8[:], in_=p8[0:8, :]), r=[b_p8], w=[b_totB])
            s5_front(A)
            if "p3a" in K.flags:
                return
            p8, b_p8 = P8
            s5_sums(p8, b_p8)
            (r1, b_r1), (r2, b_r2), (r3, b_r3), (r4, b_r4) = rows
            w8 = K.wj[0:8, 0:1]
            v8 = K.wj[0:8, 1:2]
            S.dve(lambda e, p8=p8: e.tensor_scalar(out=r1[:], in0=p8[0:8, :], scalar1=v8, scalar2=None, op0=ALU.mult),
                  r=[b_p8, K.b_wj], w=[b_r1])
            S.dve(lambda e: e.scalar_tensor_tensor(out=r1[:], in0=totB8[:], scalar=w8, in1=r1[:], op0=ALU.mult, op1=ALU.add),
                  r=[b_totB, b_r1, K.b_wj], w=[b_r1])
            S.dve(lambda e: e.tensor_scalar(out=r2[:], in0=totB8[:], scalar1=v8, scalar2=None, op0=ALU.mult),
                  r=[b_totB, K.b_wj], w=[b_r2])
            S.dve(lambda e, p8=p8: e.scalar_tensor_tensor(out=r2[:], in0=p8[0:8, :], scalar=w8, in1=r2[:], op0=ALU.mult, op1=ALU.add),
                  r=[b_p8, b_r2, K.b_wj], w=[b_r2])
            S.dve(lambda e: e.tensor_tensor(out=r1[:], in0=r1[:], in1=Xin8[:], op=ALU.add), r=[b_r1, b_Xin], w=[b_r1])
            cscale(r1[:], b_r1, e128[:, 0, :], e128[:, 1, :], b_e128, r3[:], b_r3)
            S.dve(lambda e: e.tensor_scalar(out=r4[:], in0=Xin8[:], scalar1=v8, scalar2=None, op0=ALU.mult),
                  r=[b_Xin, K.b_wj], w=[b_r4])
            S.dve(lambda e: e.scalar_tensor_tensor(out=xown8[:], in0=r3[:], scalar=w8, in1=r4[:], op0=ALU.mult, op1=ALU.add),
                  r=[b_r3, b_r4, K.b_wj], w=[b_xown])
            S.dve(lambda e: e.tensor_tensor(out=r2[:], in0=r2[:], in1=r3[:], op=ALU.add), r=[b_r2, b_r3], w=[b_r2])
            cscale(r2[:], b_r2, e128[:, 0, :], e128[:, 1, :], b_e128, Xin8[:], b_Xin)
            for sl in range(8):
                tb, b_tb = nxt("tab", tabs)
                S.dma(lambda e, sl=sl, tb=tb: e.dma_start(
                    out=tb[:, 2:4, :], in_=K.tab_d[2:4, :, sl * 512:(sl + 1) * 512].rearrange("t p q -> p t q")), w=[b_tb])
                pZ, b_pZ = nxt("f", PF)
                S.pe(lambda e, sl=sl, pZ=pZ: e.matmul(pZ[:], lhsT=K.tri_bs[:], rhs=SY2[:, sl * 512:(sl + 1) * 512],
                                                      start=True, stop=False), r=[K.b_trib, b_SY], w=[b_pZ])
                S.pe(lambda e, sl=sl, pZ=pZ: e.matmul(pZ[:], lhsT=erws[0:8, sl * 128:(sl + 1) * 128], rhs=xown8[0:8, :],
                                                      start=False, stop=True), r=[b_erws, b_xown], w=[b_pZ])
                cscale(pZ[:], b_pZ, tb[:, 2, :], tb[:, 3, :], b_tb, Xp2[:, sl * 512:(sl + 1) * 512], b_UtXp)
                if sl % 2 == 1 and "p3b" not in K.flags:
                    g8 = sl // 2
                    pb_, b_pb = nxt("b", PB)
                    for gg in range(8):
                        g = g8 * 8 + gg
                        S.pe(lambda e, g=g, gg=gg, pb_=pb_: e.transpose(out=pb_[:, gg * 128:(gg + 1) * 128], in_=UtXp[:, g, :],
                                                                        identity=K.id_b[:]), r=[b_UtXp, K.b_idb], w=[b_pb])
                    src = pb_[:].rearrange("p (g q) -> p g q", g=8)
                    if g8 % 2 == 0:
                        S.act(lambda e, g8=g8, src=src: e.activation(out=XpT[:, g8 * 8:(g8 + 1) * 8, :], in_=src, func=AF.Copy),
                              r=[b_pb], w=[b_XpT])
                    else:
                        S.dve(lambda e, g8=g8, src=src: e.tensor_copy(out=XpT[:, g8 * 8:(g8 + 1) * 8, :], in_=src), r=[b_pb], w=[b_XpT])
            if "p3b" in K.flags:
                return
            for sl in range(8):
                pY, b_pY = nxt("f", PF)
                for gg in range(4):
                    g = sl * 4 + gg
                    S.pe(lambda e, g=g, gg=gg, pY=pY: e.matmul(pY[:, gg * 128:(gg + 1) * 128], lhsT=UT[:, g, :], rhs=K.T5[:, g, :],
                                                               start=True, stop=False), r=[b_UT, K.b_T5], w=[b_pY])
                    S.pe(lambda e, g=g, gg=gg, pY=pY: e.matmul(pY[:, gg * 128:(gg + 1) * 128], lhsT=XpT[:, g, :], rhs=K.CA5[:, g, :],
                                                               start=False, stop=True), r=[b_XpT, K.b_CA5], w=[b_pY])
                xs, b_xs = nxt("t", tmp)
                x2, b_x2 = nxt("t", tmp)
                S.act(lambda e, pY=pY, xs=xs: e.activation(out=xs[:], in_=pY[:], func=AF.Copy), r=[b_pY], w=[b_xs])
                S.pool(lambda e, xs=xs, x2=x2: e.tensor_tensor(out=x2[:], in0=xs[:], in1=xs[:], op=ALU.mult), r=[b_xs], w=[b_x2])
                S.dve(lambda e, x2=x2: e.tensor_scalar(out=x2[:], in0=x2[:], scalar1=0.044715, scalar2=1.0, op0=ALU.mult, op1=ALU.add),
                      r=[b_x2], w=[b_x2])
                S.dve(lambda e, xs=xs, x2=x2: e.tensor_tensor(out=x2[:], in0=x2[:], in1=xs[:], op=ALU.mult), r=[b_x2, b_xs], w=[b_x2])
                S.act(lambda e, x2=x2: e.activation(out=x2[:], in_=x2[:], func=AF.Tanh, scale=GC), r=[b_x2], w=[b_x2])
                yg4 = SY2.rearrange("p (j g c) -> p j g c", j=8, g=32)
                for gg in range(4):
                    ygo = yg4[:, :, sl * 4 + gg, :]
                    S.dve(lambda e, ygo=ygo, xs=xs, x2=x2, gg=gg: e.scalar_tensor_tensor(
                        out=ygo, in0=x2[:, gg * 128:(gg + 1) * 128].rearrange("p (j c) -> p j c", j=8), scalar=1.0,
                        in1=xs[:, gg * 128:(gg + 1) * 128].rearrange("p (j c) -> p j c", j=8), op0=ALU.add, op1=ALU.mult),
                        r=[b_x2, b_xs], w=[b_SY])
            def emit_ybT():
                for fc in range(4):
                    pb_, b_pb = nxt("b", PB)
                    for j in range(8):
                        S.pe(lambda e, fc=fc, j=j, pb_=pb_: e.transpose(
                            out=pb_[:, j * 128:(j + 1) * 128], in_=SY2[:, j * 512 + fc * 128:j * 512 + (fc + 1) * 128],
                            identity=K.id_b[:]), r=[b_SY, K.b_idb], w=[b_pb])
                    if fc % 2 == 0:
                        S.act(lambda e, fc=fc, pb_=pb_: e.activation(out=ybT[:, fc, :], in_=pb_[:], func=AF.Copy), r=[b_pb], w=[b_ybT])
                    else:
                        S.dve(lambda e, fc=fc, pb_=pb_: e.tensor_copy(out=ybT[:, fc, :], in_=pb_[:]), r=[b_pb], w=[b_ybT])
                if r == 0 and "ybT" in K.dbg_out:
                    S.dma(lambda e: e.dma_start(out=K.dbg_out["ybT"], in_=ybT[:]), r=[b_ybT])
            if "p3c" in K.flags:
                emit_ybT()
                return
            def do_half(half):
                hs = slice(half * 512, (half + 1) * 512)
                S.dma(lambda e, r=r, hs=hs: e.dma_start(out=yaT[:], in_=K.yaT_d[r, :, :, hs]), w=[b_ya])
                ((wza, b_wza),) = wstage(["za"])
                for fc in range(4):
                    pz, b_pz = nxt("f", PF)
                    for kc in range(8):
                        S.pe(lambda e, fc=fc, kc=kc, pz=pz: e.matmul(pz[:], lhsT=wza[:, kc, fc * 128:(fc + 1) * 128], rhs=hTb[:, kc, hs],
                                                                     start=(kc == 0), stop=(kc == 7)), r=[b_wza, b_h], w=[b_pz])
                    th, b_th = nxt("h", thb)
                    t1, b_t1 = nxt("t", tmp)
                    S.act(lambda e, pz=pz, th=th: e.activation(out=th[:], in_=pz[:], func=AF.Tanh, scale=0.5), r=[b_pz], w=[b_th])
                    S.dve(lambda e, pz=pz, th=th, t1=t1: e.scalar_tensor_tensor(out=t1[:], in0=th[:], scalar=1.0, in1=pz[:],
                                                                                op0=ALU.add, op1=ALU.mult), r=[b_th, b_pz], w=[b_t1])
                    S.dve(lambda e, fc=fc, t1=t1: e.tensor_tensor(out=yag[:, fc, :], in0=t1[:], in1=yaT[:, fc, :], op=ALU.mult),
                          r=[b_t1, b_ya], w=[b_XpT])
                if half == 0:
                    emit_ybT()
                ((wzb, b_wzb),) = wstage(["zb"])
                for fc in range(4):
                    pg, b_pg = nxt("f", PF)
                    for kc in range(4):
                        S.pe(lambda e, fc=fc, kc=kc, pg=pg: e.matmul(pg[:], lhsT=wgl[:, kc, fc * 128:(fc + 1) * 128], rhs=ybT[:, kc, hs],
                                                                     start=(kc == 0), stop=(kc == 3)), r=[b_wgl, b_ybT], w=[b_pg])
                    pz, b_pz = nxt("f", PF)
                    for kc in range(8):
                        S.pe(lambda e, fc=fc, kc=kc, pz=pz: e.matmul(pz[:], lhsT=wzb[:, kc, fc * 128:(fc + 1) * 128], rhs=hTb[:, kc, hs],
                                                                     start=(kc == 0), stop=(kc == 7)), r=[b_wzb, b_h], w=[b_pz])
                    thg, b_thg = nxt("h", thb)
                    thz, b_thz = nxt("h", thb)
                    t1, b_t1 = nxt("t", tmp)
                    t2, b_t2 = nxt("t", tmp)
                    S.act(lambda e, fc=fc, pg=pg, thg=thg: e.activation(out=thg[:], in_=pg[:], func=AF.Tanh, scale=0.25,
                                                                        bias=bgl[:, fc:fc + 1]), r=[b_pg, b_bgl], w=[b_thg])
                    S.act(lambda e, pz=pz, thz=thz: e.activation(out=thz[:], in_=pz[:], func=AF.Tanh, scale=0.5), r=[b_pz], w=[b_thz])
                    S.dve(lambda e, pz=pz, thz=thz, t1=t1: e.scalar_tensor_tensor(out=t1[:], in0=thz[:], scalar=1.0, in1=pz[:],
                                                                                  op0=ALU.add, op1=ALU.mult), r=[b_thz, b_pz], w=[b_t1])
                    S.pool(lambda e, thg=thg, t2=t2: e.tensor_scalar(out=t2[:], in0=thg[:], scalar1=1.0, scalar2=None, op0=ALU.add),
                           r=[b_thg], w=[b_t2])
                    S.dve(lambda e, fc=fc, t2=t2: e.tensor_tensor(out=t2[:], in0=t2[:], in1=ybT[:, fc, hs], op=ALU.mult),
                          r=[b_t2, b_ybT], w=[b_t2])
                    S.dve(lambda e, fc=fc, t1=t1, t2=t2: e.tensor_tensor(out=ybg[:, fc, :], in0=t1[:], in1=t2[:], op=ALU.mult),
                          r=[b_t1, b_t2], w=[b_XpT])
                wg = {}

                def emit_G(fc):
                    gh, f4 = fc // 4, fc % 4
                    if f4 == 0:
                        wg[gh] = tuple(wstage(["ga%d" % gh, "gb%d" % gh]))
                    (wga, b_wga), (wgb, b_wgb) = wg[gh]
                    pga, b_pga = nxt("f", PF)
                    for kc in range(8):
                        S.pe(lambda e, kc=kc: e.matmul(pga[:], lhsT=wga[:, kc, f4 * 128:(f4 + 1) * 128], rhs=hTb[:, kc, hs],
                                                       start=(kc == 0), stop=(kc == 7)), r=[b_wga, b_h], w=[b_pga])
                    pgb, b_pgb = nxt("f", PF)
                    for kc in range(8):
                        S.pe(lambda e, kc=kc: e.matmul(pgb[:], lhsT=wgb[:, kc, f4 * 128:(f4 + 1) * 128], rhs=hTb[:, kc, hs],
                                                       start=(kc == 0), stop=(kc == 7)), r=[b_wgb, b_h], w=[b_pgb])
                    tha, b_tha = nxt("h", thb)
                    thb_, b_thb = nxt("h", thb)
                    S.act(lambda e: e.activation(out=tha[:], in_=pga[:], func=AF.Tanh, scale=0.5), r=[b_pga], w=[b_tha])
                    S.act(lambda e: e.activation(out=thb_[:], in_=pgb[:], func=AF.Tanh, scale=0.5), r=[b_pgb], w=[b_thb])
                    return tha, b_tha, thb_, b_thb

                def emit_U(fc, ths):
                    tha, b_tha, thb_, b_thb = ths
                    pua, b_pua = nxt("f", PF)
                    for kc in range(4):
                        S.pe(lambda e, kc=kc: e.matmul(pua[:], lhsT=wua4[:, kc, fc * 128:(fc + 1) * 128], rhs=yag[:, kc, :],
                                                       start=(kc == 0), stop=(kc == 3)), r=[b_wua, b_XpT], w=[b_pua])
                    pub, b_pub = nxt("f", PF)
                    for kc in range(4):
                        S.pe(lambda e, kc=kc: e.matmul(pub[:], lhsT=wub4[:, kc, fc * 128:(fc + 1) * 128], rhs=ybg[:, kc, :],
                                                       start=(kc == 0), stop=(kc == 3)), r=[b_wub, b_XpT], w=[b_pub])
                    m1, b_m1 = nxt("t", tmp)
                    m2, b_m2 = nxt("t", tmp)
                    S.dve(lambda e: e.scalar_tensor_tensor(out=m1[:], in0=tha[:], scalar=1.0, in1=pua[:],
                                                           op0=ALU.add, op1=ALU.mult), r=[b_tha, b_pua], w=[b_m1])
                    S.dve(lambda e: e.scalar_tensor_tensor(out=m2[:], in0=thb_[:], scalar=1.0, in1=pub[:],
                                                           op0=ALU.add, op1=ALU.mult), r=[b_thb, b_pub], w=[b_m2])
                    S.pool(lambda e: e.tensor_tensor(out=merged[:, fc, :], in0=m1[:], in1=m2[:], op=ALU.add),
                           r=[b_m1, b_m2], w=[b_UT])

                ths = {0: emit_G(0)}
                for fc in range(8):
                    if fc + 1 < 8:
                        ths[fc + 1] = emit_G(fc + 1)
                    emit_U(fc, ths.pop(fc))
                if "p3d" in K.flags:
                    return
                if r == 0 and half == 0:
                    if "merged" in K.dbg_out:
                        S.dma(lambda e: e.dma_start(out=K.dbg_out["merged"], in_=merged), r=[b_UT])
                    if "yag" in K.dbg_out:
                        S.dma(lambda e: e.dma_start(out=K.dbg_out["yag"], in_=yag), r=[b_XpT])
                    if "ybg" in K.dbg_out:
                        S.dma(lambda e: e.dma_start(out=K.dbg_out["ybg"], in_=ybg), r=[b_XpT])
                (wo0, b_wo0), (wo1, b_wo1) = wstage(["o0", "o1"])
                for jj in range(4):
                    j = half * 4 + jj
                    x_t, b_x = nxt("x", xj)
                    S.dma(lambda e, A=A, j=j, x_t=x_t: e.dma_start(
                        out=x_t[:], in_=K.x[A * 1024:(A + 1) * 1024, :].rearrange("(n j) d -> n j d", j=8)[:, j, :]), w=[b_x])
                    for fh in range(2):
                        wo, b_wo = (wo0, b_wo0) if fh == 0 else (wo1, b_wo1)
                        po, b_po = nxt("f", PF)
                        for kc in range(8):
                            S.pe(lambda e, jj=jj, kc=kc, po=po, wo=wo: e.matmul(po[:], lhsT=merged[:, kc, jj * 128:(jj + 1) * 128], rhs=wo[:, kc, :],
                                                                               start=(kc == 0), stop=(kc == 7)), r=[b_UT, b_wo], w=[b_po])
                        fs = slice(fh * 512, (fh + 1) * 512)
                        S.dve(lambda e, po=po, x_t=x_t, fs=fs: e.tensor_tensor(out=x_t[:, fs], in0=x_t[:, fs], in1=po[:], op=ALU.add),
                              r=[b_po, b_x], w=[b_x])
                    if "p3e" in K.flags:
                        continue
                    S.act(lambda e, x_t=x_t: e.activation(out=junk[:], in_=x_t[:], func=AF.Square, accum_out=ss2[:, 0:1]),
                          r=[b_x], w=[b_junk, b_ss2])
                    S.act(lambda e: e.activation(out=ss2[:, 1:2], in_=ss2[:, 0:1], func=AF.Sqrt, scale=1.0 / D, bias=K.cst[:, 0:1]),
                          r=[b_ss2, K.b_cst], w=[b_ss2])
                    S.dve(lambda e: e.reciprocal(out=ss2[:, 1:2], in_=ss2[:, 1:2]), r=[b_ss2], w=[b_ss2])
                    S.dve(lambda e, x_t=x_t: e.scalar_tensor_tensor(out=x_t[:], in0=x_t[:], scalar=ss2[:, 1:2], in1=gfin[:],
                                                                    op0=ALU.mult, op1=ALU.mult), r=[b_x, b_ss2, b_gfin], w=[b_x])
                    if "p3f" in K.flags:
                        continue
                    S.dma(lambda e, r=r, j=j, x_t=x_t: e.dma_start(
                        out=K.out[r * 1024:(r + 1) * 1024, :].rearrange("(n j) d -> n j d", j=8)[:, j, :], in_=x_t[:]),
                        r=[b_x], q="aq")
            for half in range(2):
                do_half(half)

        for r in range(NB // 2):
            block(r)
        S.flush()


def host_consts():
    bf = ml_dtypes.bfloat16
    n = np.arange(128)
    tri = (n[:, None] < n[None, :]).astype(np.float32)
    mle = (n[:, None] <= n[None, :]).astype(np.float32)
    mlt = tri
    cm5 = np.zeros((128, 5, 512), np.float32)
    for a in range(5):
        for i in range(4):
            cm5[:, a, i * 128:(i + 1) * 128] = -30000.0 * (1.0 - (mlt if i < a else mle))
    ic = np.arange(128) // 16
    tmask = (ic[None, :] >= ic[:, None]).astype(np.float32)
    nvec = np.stack([n, n + 1, n + 2, n + 3], axis=1).astype(np.float32)
    ecol = np.zeros((128, 8, 8), np.float32)
    erow = np.zeros((8, 8, 128), np.float32)
    for sl in range(8):
        ecol[:, sl, sl] = 1.0
        erow[sl, sl, :] = 1.0
    return {"ecol": ecol.reshape(128, 64).astype(bf), "erowsel": erow.reshape(8, 1024).astype(bf),
            "ident_b": np.eye(128, dtype=np.float32).astype(bf), "ident_f": np.eye(128, dtype=np.float32),
            "tri_f": tri, "cm5": cm5.astype(bf), "tmask": tmask, "nvec": nvec}


def core_inputs(inp, b, NB=8, jc=0):
    T = NB * 1024
    f = np.float32
    col = lambda v, k: np.ascontiguousarray(np.asarray(v, f).reshape(k, 128).T)
    rep = lambda v: np.ascontiguousarray(np.broadcast_to(np.asarray(v, f)[None, :], (128, v.shape[-1])))
    b_ada = np.asarray(inp["b_ada"][0], f)
    a_re, a_im = np.asarray(inp["a_re"][0], f), np.asarray(inp["a_im"][0], f)
    lam = np.stack([np.stack([a_re, a_re], 1), np.stack([a_im, a_im], 1)], 1)
    c_re, c_im = np.asarray(inp["c_re"][0], f).reshape(512, 64), np.asarray(inp["c_im"][0], f).reshape(512, 64)
    s5c = np.stack([np.stack([c_re, c_re], 1), np.stack([c_im, c_im], 1)], 1)
    b_re, b_im = np.asarray(inp["b_re"][0], f), np.asarray(inp["b_im"][0], f)
    s5b = np.concatenate([b_re.transpose(1, 0, 2), b_im.transpose(1, 0, 2)], 0)
    d = np.asarray(inp["d_skip"][0], f).reshape(32, 16)
    s5d = np.tile(d.T[None, :, :], (8, 1, 1)).reshape(128, 32)
    m = {
        "x": np.ascontiguousarray(np.asarray(inp["x"][b, :T], f).reshape(NB // 2, 2, 1024, D)[:, ::(-1 if jc else 1)].reshape(T, D)),
        "wj": np.ascontiguousarray(np.broadcast_to(np.array([float(jc), 1.0 - jc], f)[None, :], (128, 2))),
        "mB": np.full((128, 512), 0.0 if jc else -30000.0, f).astype(ml_dtypes.bfloat16),
        "cT": col(inp["c"][b], 8),
        "w_ada": np.ascontiguousarray(np.asarray(inp["w_ada"][0], f)),
        "bada_col": col(b_ada, 24),
        "bgate_rep": rep(b_ada[2 * D:]),
        "gn_col": col(inp["g_norm"][0], 8),
        "w_in": np.ascontiguousarray(np.asarray(inp["w_in"][0], f)),
        "bf_rep": np.ascontiguousarray(np.tile(np.asarray(inp["b_f"][0], f), 8)[None, :].repeat(128, 0)),
        "w_glu": np.ascontiguousarray(np.asarray(inp["w_glu"][0], f)),
        "bglu_col": col(inp["b_glu"][0], 4),
        "w_up_a": np.ascontiguousarray(np.asarray(inp["w_up_a"][0], f)),
        "w_up_b": np.ascontiguousarray(np.asarray(inp["w_up_b"][0], f)),
        "w_out": np.ascontiguousarray(np.asarray(inp["w_out"][0], f)),
        "gfin_rep": rep(np.asarray(inp["g_final"], f)),
        "s5_lam": np.ascontiguousarray(lam),
        "s5_ldt": rep(np.asarray(inp["log_dt"][0], f)),
        "s5_b": np.ascontiguousarray(s5b),
        "s5_c": np.ascontiguousarray(s5c),
        "s5_d": np.ascontiguousarray(s5d),
        "s5_bs": np.ascontiguousarray(np.concatenate([s5b[64:], s5b[:64]], 0)),
        "s5_zrep": np.ascontiguousarray(np.broadcast_to(np.stack([
            a_re.reshape(-1), a_im.reshape(-1),
            np.repeat(np.asarray(inp["log_dt"][0], f), 64)], 0)[None], (128, 3, 2048))),
    }
    m.update(host_consts())
    return m


_NC_CACHE = {}


def kernel(**inputs):
    NB = 8
    if NB not in _NC_CACHE:
        _NC_CACHE[NB] = build(NB)
    nc = _NC_CACHE[NB]
    in_maps = [core_inputs(inputs, c // 2, NB, c % 2) for c in range(NCORES)]
    res = run_bass_kernel_spmd(nc, in_maps, core_ids=list(range(NCORES)))
    out = np.empty((4, NB // 2, 2, 1024, D), np.float32)
    for c in range(NCORES):
        out[c // 2, :, c % 2] = np.asarray(res.results[c]["out"], np.float32).reshape(NB // 2, 1024, D)
    return out.reshape(4, NB * 1024, D)
```

```python
import math
import numpy as np
import ml_dtypes
from contextlib import ExitStack
import concourse.bass as bass
import concourse.mybir as mybir
from concourse.bass_utils import run_bass_kernel_spmd

F32 = mybir.dt.float32
BF16 = mybir.dt.bfloat16
AF = mybir.ActivationFunctionType
ALU = mybir.AluOpType

D = 1024
EPS = 1e-6
NCORES = 8
COMPUTE = ("pe", "act", "dve", "pool")
QUEUES = ("sp", "gq", "aq")
RING = 8


class Buf:
    __slots__ = ("name", "lw", "rd")

    def __init__(self, name):
        self.name = name
        self.lw = None
        self.rd = []


class Op:
    __slots__ = ("eng", "fn", "deps", "sig", "cnt", "ring", "seg")

    def __init__(self, eng, fn, seg):
        self.eng = eng
        self.fn = fn
        self.deps = []
        self.sig = False
        self.cnt = 0
        self.ring = None
        self.seg = seg


class Sched:
    def __init__(self, nc, sems):
        self.nc = nc
        self.sems = sems
        self.ops = []
        self.seg = 0
        self.eng_obj = {"pe": nc.tensor, "act": nc.scalar, "dve": nc.vector,
                        "pool": nc.gpsimd, "sp": nc.sync, "gq": nc.gpsimd, "aq": nc.scalar}
        self.cnt = {e: 0 for e in COMPUTE}
        self.qcnt = {q: 0 for q in QUEUES}
        self.waited = {}
        self.nops = 0

    def buf(self, name):
        return Buf(name)

    def add(self, eng, fn, r=(), w=()):
        op = Op(eng, fn, self.seg)
        seg = self.seg
        deps = op.deps
        for b in r:
            p = b.lw
            if p is not None and p.seg == seg:
                deps.append(p)
        for b in w:
            p = b.lw
            if p is not None and p.seg == seg:
                deps.append(p)
            for x in b.rd:
                if x.seg == seg:
                    deps.append(x)
        for b in r:
            b.rd.append(op)
        for b in w:
            b.lw = op
            b.rd = []
        self.ops.append(op)
        return op

    def pe(self, fn, r=(), w=()):
        return self.add("pe", fn, r, w)

    def act(self, fn, r=(), w=()):
        return self.add("act", fn, r, w)

    def dve(self, fn, r=(), w=()):
        return self.add("dve", fn, r, w)

    def pool(self, fn, r=(), w=()):
        return self.add("pool", fn, r, w)

    def dma(self, fn, r=(), w=(), q="sp"):
        return self.add(q, fn, r, w)

    @staticmethod
    def _stream(e):
        return "pool" if e == "gq" else ("act" if e == "aq" else e)

    def _wait(self, st, eng, key, val):
        wk = (st, key)
        if self.waited.get(wk, 0) >= val:
            return
        self.waited[wk] = val
        sem = self.sems[key[1]] if key[0] == "c" else self.sems[(key[1], key[2])]
        eng.wait_ge(sem, val)

    def flush(self):
        ops = self.ops
        last = {}
        for op in ops:
            if op.eng in COMPUTE:
                last[op.eng] = op
            for p in op.deps:
                if p.eng == "pe" and op.eng == "pe":
                    continue
                p.sig = True
        for op in last.values():
            op.sig = True
        for op in ops:
            if op.eng in COMPUTE:
                if op.sig:
                    self.cnt[op.eng] += 1
                    op.cnt = self.cnt[op.eng]
            else:
                k = self.qcnt[op.eng]
                self.qcnt[op.eng] += 1
                op.ring = (k % RING, 16 * (k // RING + 1), k)
        for op in ops:
            eng = self.eng_obj[op.eng]
            st = self._stream(op.eng)
            need = {}
            for p in op.deps:
                if p is op or (p.eng == "pe" and op.eng == "pe"):
                    continue
                if p.eng in COMPUTE:
                    key = ("c", p.eng)
                    val = p.cnt
                else:
                    key = ("q", p.eng, p.ring[0])
                    val = p.ring[1]
                if need.get(key, 0) < val:
                    need[key] = val
            if op.eng in QUEUES:
                r, v, k = op.ring
                if k >= RING:
                    key = ("q", op.eng, r)
                    if need.get(key, 0) < v - 16:
                        need[key] = v - 16
            for key, val in need.items():
                self._wait(st, eng, key, val)
            ins = op.fn(eng)
            if op.eng in COMPUTE:
                if op.sig:
                    ins.then_inc(self.sems[op.eng], 1)
            else:
                ins.then_inc(self.sems[(op.eng, op.ring[0])], 16)
        self.nops += len(ops)
        for st in ("pe", "act", "dve", "pool", "sp"):
            eng = self.eng_obj[st]
            for e in COMPUTE:
                if e != st and self.cnt[e] > 0:
                    self._wait(st, eng, ("c", e), self.cnt[e])
            for q in QUEUES:
                k = self.qcnt[q]
                for r in range(RING):
                    n = (k - r + RING - 1) // RING if k > r else 0
                    if n > 0:
                        self._wait(st, eng, ("q", q, r), 16 * n)
        self.ops = []
        self.seg += 1


C_Q, C_K, C_V, C_F, C_ZA, C_U, C_ZB, C_GA, C_GB = 0, 512, 1024, 1536, 1544, 2056, 2568, 3080, 4104
PW = 5128


class Ctx:
    pass


def build(NB=8, dbg=None, flags=()):
    T = NB * 1024
    nc = bass.Bass("TRN2", target_bir_lowering=False)
    K = Ctx()
    K.nc = nc
    K.NB = NB
    K.T = T
    K.dbg = dbg or ()
    K.flags = flags

    def din(name, shape, dt=F32):
        return nc.dram_tensor(name, list(shape), dt, kind="ExternalInput").ap()

    def dscr(name, shape, dt):
        return nc.dram_tensor(name, list(shape), dt, kind="Internal").ap()

    K.x = din("x", [T, D])
    K.cT = din("cT", [128, 8])
    K.w_ada = din("w_ada", [D, 3 * D])
    K.bada_col = din("bada_col", [128, 24])
    K.bgate_rep = din("bgate_rep", [128, D])
    K.gn_col = din("gn_col", [128, 8])
    K.w_in = din("w_in", [D, PW])
    K.bf_rep = din("bf_rep", [128, 64])
    K.w_glu = din("w_glu", [512, 512])
    K.bglu_col = din("bglu_col", [128, 4])
    K.w_up_a = din("w_up_a", [512, D])
    K.w_up_b = din("w_up_b", [512, D])
    K.w_out = din("w_out", [D, D])
    K.gfin_rep = din("gfin_rep", [128, D])
    K.s5_lam = din("s5_lam", [32, 2, 2, 64])
    K.s5_ldt = din("s5_ldt", [128, 32])
    K.s5_b = din("s5_b", [128, 32, 16])
    K.s5_c = din("s5_c", [512, 2, 2, 64])
    K.s5_d = din("s5_d", [128, 32])
    K.s5_bs = din("s5_bs", [128, 32, 16])
    K.s5_zrep = din("s5_zrep", [128, 3, 2048])
    K.ident_b = din("ident_b", [128, 128], BF16)
    K.ident_f = din("ident_f", [128, 128])
    K.tri_f = din("tri_f", [128, 128])
    K.cm5 = din("cm5", [128, 5, 512], BF16)
    K.tmask = din("tmask", [128, 128])
    K.nvec = din("nvec", [128, 4])
    K.ecol_in = din("ecol", [128, 64], BF16)
    K.erow_in = din("erowsel", [8, 1024], BF16)
    K.wj_in = din("wj", [128, 2])
    K.mB_in = din("mB", [128, 512], BF16)
    K.out = nc.dram_tensor("out", [T // 2, D], F32, kind="ExternalOutput").ap()
    K.hT_d = dscr("hT_d", [NB, 128, 8, 1024], BF16)
    K.yaT_d = dscr("yaT_d", [NB // 2, 128, 4, 1024], BF16)
    K.c_d = dscr("c_d", [NB // 2, 8, 1024], BF16)
    K.wb_in = dscr("wb_in", [128, 8, PW], BF16)
    K.wb_glu = dscr("wb_glu", [128, 4, 512], BF16)
    K.wb_upa = dscr("wb_upa", [128, 4, D], BF16)
    K.wb_upb = dscr("wb_upb", [128, 4, D], BF16)
    K.wb_out = dscr("wb_out", [128, 8, D], BF16)
    K.tab_d = dscr("tab_d", [4, 128, 4096], BF16)
    K.tab_e = dscr("tab_e", [2, 4096], F32)
    K.ut_d = dscr("ut_d", [NB, 128, 32, 128], BF16)
    K.dbg_out = {}
    for name, shape, dt in (dbg or ()):
        K.dbg_out[name] = nc.dram_tensor("dbg_" + name, list(shape), dt, kind="ExternalOutput").ap()

    with ExitStack() as es:
        sems = {}
        for e in COMPUTE:
            sems[e] = es.enter_context(nc.semaphore("s_" + e))
        for q in QUEUES:
            for r in range(RING):
                sems[(q, r)] = es.enter_context(nc.semaphore(f"s_{q}{r}"))
        S = Sched(nc, sems)
        K.S = S
        K.es = es

        def sb(name, shape, dt, stack=es):
            t = stack.enter_context(nc.sbuf_tensor(name, list(shape), dt))
            return t, S.buf(name)

        def ps(name, shape, dt, stack=es):
            t = stack.enter_context(nc.psum_tensor(name, list(shape), dt))
            return t, S.buf(name)
        K.sb = sb
        K.ps = ps

        K.cst, K.b_cst = sb("cst", [128, 8], F32)
        K.id_b, K.b_idb = sb("id_b", [128, 128], BF16)
        K.id_f, K.b_idf = sb("id_f", [128, 128], F32)
        K.tri_fs, K.b_trif = sb("tri_fs", [128, 128], F32)
        K.tri_bs, K.b_trib = sb("tri_bs", [128, 128], BF16)
        K.ones_f, K.b_onesf = sb("ones_f", [128, 128], F32)
        K.ones_b, K.b_onesb = sb("ones_b", [128, 128], BF16)
        K.modc, K.b_modc = sb("modc", [128, 24], F32)
        K.acol, K.b_acol = sb("acol", [128, 8], F32)
        K.gate_bc, K.b_gate = sb("gate_bc", [128, D], F32)
        K.G, K.b_G = sb("G", [128, NB * 8, 8], F32)
        K.Gend, K.b_Gend = sb("Gend", [128, NB, 8], F32)
        K.Gcar, K.b_Gcar = sb("Gcar", [128, 8], F32)
        K.wj, K.b_wj = sb("wj_s", [128, 2], F32)
        K.T5, K.b_T5 = sb("T5", [128, 32, 128], BF16)
        K.BA5, K.b_BA5 = sb("BA5", [128, 32, 128], BF16)
        K.CA5, K.b_CA5 = sb("CA5", [128, 32, 128], BF16)

        S.dve(lambda e: e.memset(K.cst[:, 0:1], EPS), w=[K.b_cst])
        S.dve(lambda e: e.memset(K.cst[:, 1:2], 1.0), w=[K.b_cst])
        S.dve(lambda e: e.memset(K.cst[:, 2:3], 0.0), w=[K.b_cst])
        S.dve(lambda e: e.memset(K.ones_f[:], 1.0), w=[K.b_onesf])
        S.dve(lambda e: e.memset(K.ones_b[:], 1.0), w=[K.b_onesb])
        S.dve(lambda e: e.memset(K.Gcar[:], 0.0), w=[K.b_Gcar])
        S.dma(lambda e: e.dma_start(out=K.wj[:], in_=K.wj_in), w=[K.b_wj])
        S.dma(lambda e: e.dma_start(out=K.id_b[:], in_=K.ident_b), w=[K.b_idb])
        S.dma(lambda e: e.dma_start(out=K.id_f[:], in_=K.ident_f), w=[K.b_idf])
        S.dma(lambda e: e.dma_start(out=K.tri_fs[:], in_=K.tri_f), w=[K.b_trif])
        S.dve(lambda e: e.tensor_copy(out=K.tri_bs[:], in_=K.tri_fs[:]), r=[K.b_trif], w=[K.b_trib])

        S.flush()
        stage_setup(K)
        if "skip3" not in K.flags or "s5tab" in K.flags:
            s5_setup_b(K)
        if "no12" not in K.flags:
            phase1(K)
            if "no2" not in K.flags:
                phase2(K)
        if "skip3" not in K.flags:
            phase3(K)
    return nc


def stage_setup(K):
    nc, S = K.nc, K.S
    with ExitStack() as st:
        def sb(name, shape, dt):
            return K.sb(name, shape, dt, st)

        def ps(name, shape, dt):
            return K.ps(name, shape, dt, st)
        cT_s, b_cT = sb("cT_s", [128, 8], F32)
        bada_s, b_bada = sb("bada_s", [128, 24], F32)
        gn_s, b_gn = sb("gn_s", [128, 8], F32)
        bg_s, b_bg = sb("bg_s", [128, D], F32)
        cTrep, b_cTrep = sb("cTrep", [128, 8, 128], F32)
        w32 = [sb(f"w32_{i}", [128, 8, 512], F32) for i in range(2)]
        w16 = [sb(f"w16_{i}", [128, 8, 512], BF16) for i in range(2)]
        pmod, b_pmod = ps("pmod", [128, 24], F32)
        pgate = [ps(f"pgate{i}", [128, 512], F32) for i in range(2)]

        S.dma(lambda e: e.dma_start(out=cT_s[:], in_=K.cT), w=[b_cT])
        S.dma(lambda e: e.dma_start(out=bada_s[:], in_=K.bada_col), w=[b_bada])
        S.dma(lambda e: e.dma_start(out=gn_s[:], in_=K.gn_col), w=[b_gn])
        S.dma(lambda e: e.dma_start(out=bg_s[:], in_=K.bgate_rep), w=[b_bg])
        for kc in range(8):
            S.dve(lambda e, kc=kc: e.tensor_copy(out=cTrep[:, kc, :],
                                                 in_=cT_s[:, kc:kc + 1].to_broadcast([128, 128])),
                  r=[b_cT], w=[b_cTrep])
        for piece in range(6):
            wt, b_wt = w32[piece % 2]
            S.dma(lambda e, piece=piece, wt=wt: e.dma_start(
                out=wt[:], in_=K.w_ada[:, piece * 512:(piece + 1) * 512].rearrange("(kc p) n -> p kc n", p=128)),
                w=[b_wt])
            if piece < 4:
                for fc in range(4):
                    col = piece * 4 + fc
                    for kc in range(8):
                        S.pe(lambda e, fc=fc, kc=kc, col=col, wt=wt: e.matmul(
                            pmod[:, col:col + 1], lhsT=wt[:, kc, fc * 128:(fc + 1) * 128],
                            rhs=cT_s[:, kc:kc + 1], start=(kc == 0), stop=(kc == 7)),
                            r=[b_wt, b_cT], w=[b_pmod])
            else:
                pg, b_pg = pgate[piece - 4]
                for kc in range(8):
                    S.pe(lambda e, kc=kc, wt=wt, pg=pg: e.matmul(
                        pg[:], lhsT=cTrep[:, kc, :], rhs=wt[:, kc, :], start=(kc == 0), stop=(kc == 7)),
                        r=[b_wt, b_cTrep], w=[b_pg])
                half = piece - 4
                S.dve(lambda e, pg=pg, half=half: e.tensor_tensor(
                    out=K.gate_bc[:, half * 512:(half + 1) * 512], in0=pg[:],
                    in1=bg_s[:, half * 512:(half + 1) * 512], op=ALU.add),
                    r=[b_pg, b_bg], w=[K.b_gate])
        S.dve(lambda e: e.tensor_scalar(out=K.gate_bc[:], in0=K.gate_bc[:], scalar1=0.0625, scalar2=None,
                                        op0=ALU.mult), r=[K.b_gate], w=[K.b_gate])
        S.dve(lambda e: e.tensor_tensor(out=K.modc[:, 0:16], in0=pmod[:, 0:16], in1=bada_s[:, 0:16], op=ALU.add),
              r=[b_pmod, b_bada], w=[K.b_modc])
        S.dve(lambda e: e.scalar_tensor_tensor(out=K.acol[:], in0=K.modc[:, 8:16], scalar=1.0, in1=gn_s[:],
                                               op0=ALU.add, op1=ALU.mult), r=[K.b_modc, b_gn], w=[K.b_acol])

        if "skip3" not in K.flags or "s5tab" in K.flags:
            s5_setup_a(K, st)
        jobs = []
        c0 = 0
        while c0 < PW:
            c1 = min(c0 + 512, PW)
            jobs.append((K.w_in[:, c0:c1].rearrange("(kc p) n -> p kc n", p=128), K.wb_in[:, :, c0:c1], 8, c1 - c0))
            c0 = c1
        jobs.append((K.w_glu.rearrange("(kc p) n -> p kc n", p=128), K.wb_glu, 4, 512))
        for h in range(2):
            jobs.append((K.w_up_a[:, h * 512:(h + 1) * 512].rearrange("(kc p) n -> p kc n", p=128),
                         K.wb_upa[:, :, h * 512:(h + 1) * 512], 4, 512))
            jobs.append((K.w_up_b[:, h * 512:(h + 1) * 512].rearrange("(kc p) n -> p kc n", p=128),
                         K.wb_upb[:, :, h * 512:(h + 1) * 512], 4, 512))
            jobs.append((K.w_out[:, h * 512:(h + 1) * 512].rearrange("(kc p) n -> p kc n", p=128),
                         K.wb_out[:, :, h * 512:(h + 1) * 512], 8, 512, h))
        for i, job in enumerate(jobs):
            src, dst, nk, w = job[:4]
            wt, b_wt = w32[i % 2]
            wo, b_wo = w16[i % 2]
            S.dma(lambda e, src=src, wt=wt, nk=nk, w=w: e.dma_start(out=wt[:, 0:nk, 0:w], in_=src), w=[b_wt])
            if len(job) == 5:
                gh = job[4]
                S.dve(lambda e, wt=wt, gh=gh: e.tensor_tensor(
                    out=wt[:], in0=wt[:], in1=K.gate_bc[:, gh * 512:(gh + 1) * 512].unsqueeze(1).to_broadcast([128, 8, 512]),
                    op=ALU.mult), r=[b_wt, K.b_gate], w=[b_wt])
            S.act(lambda e, wt=wt, wo=wo, nk=nk, w=w: e.activation(out=wo[:, 0:nk, 0:w], in_=wt[:, 0:nk, 0:w], func=AF.Copy),
                  r=[b_wt], w=[b_wo])
            S.dma(lambda e, dst=dst, wo=wo, nk=nk, w=w: e.dma_start(out=dst, in_=wo[:, 0:nk, 0:w]), r=[b_wo], q="gq")
        S.flush()


PI = math.pi


def sin_eval(K, out_ap, x_ap, scr, r, w, shift=0.0):
    S = K.S
    t, ti, tf, b_s = scr
    lim = 3.14159
    S.dve(lambda e: e.tensor_scalar(out=t, in0=x_ap, scalar1=1.0 / (2 * PI), scalar2=64.5 + shift / (2 * PI),
                                    op0=ALU.mult, op1=ALU.add), r=r, w=[b_s])
    S.dve(lambda e: e.tensor_copy(out=ti, in_=t), r=[b_s], w=[b_s])
    S.dve(lambda e: e.tensor_copy(out=tf, in_=ti), r=[b_s], w=[b_s])
    S.dve(lambda e: e.tensor_tensor(out=t, in0=t, in1=tf, op=ALU.subtract), r=[b_s], w=[b_s])
    S.dve(lambda e: e.tensor_scalar(out=t, in0=t, scalar1=0.0, scalar2=None, op0=ALU.is_lt), r=[b_s], w=[b_s])
    S.dve(lambda e: e.tensor_tensor(out=tf, in0=tf, in1=t, op=ALU.subtract), r=[b_s], w=[b_s])
    S.dve(lambda e: e.tensor_scalar(out=tf, in0=tf, scalar1=-2 * PI, scalar2=128 * PI + shift,
                                    op0=ALU.mult, op1=ALU.add), r=[b_s], w=[b_s])
    S.dve(lambda e: e.tensor_tensor(out=t, in0=x_ap, in1=tf, op=ALU.add), r=list(r) + [b_s], w=[b_s])
    S.dve(lambda e: e.tensor_scalar(out=t, in0=t, scalar1=lim, scalar2=-lim, op0=ALU.min, op1=ALU.max),
          r=[b_s], w=[b_s])
    S.act(lambda e: e.activation(out=out_ap, in_=t, func=AF.Sin), r=[b_s], w=w)


def s5_setup_a(K, st):
    nc, S = K.nc, K.S
    I32 = mybir.dt.int32
    if True:
        def sb(name, shape, dt):
            return K.sb(name, shape, dt, st)

        def ps(name, shape, dt):
            return K.ps(name, shape, dt, st)
        lam_in, b_lam = sb("s5_lam_in", [32, 256], F32)
        ar, b_ar = sb("s5_ar", [128, 32], F32)
        ai, b_ai = sb("s5_ai", [128, 32], F32)
        dt_, b_dt = sb("s5_dt", [128, 32], F32)
        zr, b_zr = sb("s5_zr", [128, 32], F32)
        zi, b_zi = sb("s5_zi", [128, 32], F32)
        sm = [sb(f"s5_sm{i}", [128, 32], F32) for i in range(8)]
        smi, _ = sb("s5_smi", [128, 32], I32)
        ZR, b_ZR = sb("s5_ZR", [128, 16, 32], F32)
        ZI, b_ZI = sb("s5_ZI", [128, 16, 32], F32)
        MAG, b_MAG = sb("s5_MAG", [128, 16, 32], F32)
        SN, b_SN = sb("s5_SN", [128, 16, 32], F32)
        CS, b_CS = sb("s5_CS", [128, 16, 32], F32)
        ER, b_ER = sb("s5_ER", [128, 16, 32], F32)
        EI, b_EI = sb("s5_EI", [128, 16, 32], F32)
        ALR, b_ALR = sb("s5_ALR", [128, 16, 32], F32)
        ALI, b_ALI = sb("s5_ALI", [128, 16, 32], F32)
        CA16, b_CA16 = sb("s5_CA16", [128, 16, 32], F32)
        CB16, b_CB16 = sb("s5_CB16", [128, 16, 32], F32)
        w1, b_w1 = sb("s5_w1", [128, 16, 32], F32)
        w2, b_w2 = sb("s5_w2", [128, 16, 32], F32)
        w3, b_w3 = sb("s5_w3", [128, 16, 32], F32)
        wi, _ = sb("s5_wi", [128, 16, 32], I32)
        sgn, b_sgn = sb("s5_sgn", [128, 1], F32)
        Bn, b_Bn = sb("s5_Bn", [128, 32, 16], F32)
        Bs, b_Bs = sb("s5_Bs", [128, 32, 16], F32)
        Pm, b_Pm = sb("s5_Pm", [128, 32, 128], BF16)
        Pp, b_Pp = sb("s5_Pp", [128, 32, 128], F32)
        Qm, b_Qm = sb("s5_Qm", [128, 32, 128], BF16)
        c_in, b_cin = sb("s5_cin", [128, 4, 256], F32)
        CrT, b_CrT = sb("s5_CrT", [128, 32, 16], F32)
        CiT, b_CiT = sb("s5_CiT", [128, 32, 16], F32)
        u1, b_u1 = sb("s5_u1", [128, 32, 16], F32)
        u2, b_u2 = sb("s5_u2", [128, 32, 16], F32)
        Dcol, b_Dcol = sb("s5_Dcol", [128, 32], F32)
        tmask, b_tmask = sb("s5_tmask", [128, 128], F32)
        tt, b_tt = sb("s5_tt", [128, 128], F32)
        ptp = [ps(f"s5_ptp{i}", [128, 128], F32) for i in range(2)]

        S.dma(lambda e: e.dma_start(out=lam_in[:], in_=K.s5_lam.rearrange("g a d p -> g (a d p)")), w=[b_lam])
        S.dma(lambda e: e.dma_start(out=dt_[:], in_=K.s5_ldt), w=[b_dt])
        S.dma(lambda e: e.dma_start(out=Bn[:], in_=K.s5_b), w=[b_Bn])
        S.dma(lambda e: e.dma_start(out=Bs[:], in_=K.s5_bs), w=[b_Bs])
        S.dma(lambda e: e.dma_start(out=c_in[:], in_=K.s5_c.rearrange("(s q) a d p -> q s (a d p)", q=128)), w=[b_cin])
        S.dma(lambda e: e.dma_start(out=Dcol[:], in_=K.s5_d), w=[b_Dcol])
        S.dma(lambda e: e.dma_start(out=tmask[:], in_=K.tmask), w=[b_tmask])
        S.dve(lambda e: e.memset(sgn[0:64, :], -1.0), w=[b_sgn])
        S.dve(lambda e: e.memset(sgn[64:128, :], 1.0), w=[b_sgn])
        for which, (dst, b_dst) in enumerate(((ar, b_ar), (ai, b_ai))):
            pt, b_pt = ptp[which]
            S.pe(lambda e, which=which, pt=pt: e.transpose(out=pt[:, 0:32], in_=lam_in[:, which * 128:(which + 1) * 128],
                                                           identity=K.id_f[0:32, 0:32]), r=[b_lam, K.b_idf], w=[b_pt])
            S.dve(lambda e, pt=pt, dst=dst: e.tensor_copy(out=dst[:], in_=pt[:, 0:32]), r=[b_pt], w=[b_dst])
        S.act(lambda e: e.activation(out=dt_[:], in_=dt_[:], func=AF.Exp), r=[b_dt], w=[b_dt])
        S.dve(lambda e: e.tensor_tensor(out=zr[:], in0=ar[:], in1=dt_[:], op=ALU.mult), r=[b_ar, b_dt], w=[b_zr])
        S.dve(lambda e: e.tensor_tensor(out=zi[:], in0=ai[:], in1=dt_[:], op=ALU.mult), r=[b_ai, b_dt], w=[b_zi])
        for ti_, tau in enumerate(range(-7, 9)):
            S.dve(lambda e, ti_=ti_, tau=tau: e.tensor_scalar(out=ZR[:, ti_, :], in0=zr[:], scalar1=float(tau), scalar2=None,
                                                              op0=ALU.mult), r=[b_zr], w=[b_ZR])
            S.dve(lambda e, ti_=ti_, tau=tau: e.tensor_scalar(out=ZI[:, ti_, :], in0=zi[:], scalar1=float(tau), scalar2=None,
                                                              op0=ALU.mult), r=[b_zi], w=[b_ZI])
        S.act(lambda e: e.activation(out=MAG[:], in_=ZR[:], func=AF.Exp), r=[b_ZR], w=[b_MAG])
        scr = (w1[:], wi[:], w2[:], b_w1)
        sin_eval(K, SN[:], ZI[:], scr, [b_ZI], [b_SN])
        sin_eval(K, CS[:], ZI[:], scr, [b_ZI], [b_CS], shift=PI / 2)
        S.dve(lambda e: e.tensor_tensor(out=ER[:], in0=MAG[:], in1=CS[:], op=ALU.mult), r=[b_MAG, b_CS], w=[b_ER])
        S.dve(lambda e: e.tensor_tensor(out=EI[:], in0=MAG[:], in1=SN[:], op=ALU.mult), r=[b_MAG, b_SN], w=[b_EI])
        (th, b_th), (sh, b_sh), (am1r, b_am1r), (am1i, b_am1i), (rl2, b_rl2), (kr, b_kr), (ki, b_ki), (tq, b_tq) = sm
        S.act(lambda e: e.activation(out=th[:], in_=zr[:], func=AF.Tanh, scale=0.5), r=[b_zr], w=[b_th])
        S.dve(lambda e: e.tensor_scalar(out=tq[:], in0=zi[:], scalar1=0.5, scalar2=None, op0=ALU.mult), r=[b_zi], w=[b_tq])
        scr2 = (am1r[:], smi[:], am1i[:], b_am1r)
        sin_eval(K, sh[:], tq[:], scr2, [b_tq], [b_sh])
        i1 = 8
        S.dve(lambda e: e.scalar_tensor_tensor(out=am1r[:], in0=MAG[:, i1, :], scalar=1.0, in1=th[:], op0=ALU.add, op1=ALU.mult),
              r=[b_MAG, b_th, b_sh], w=[b_am1r])
        S.dve(lambda e: e.tensor_tensor(out=am1r[:], in0=am1r[:], in1=CS[:, i1, :], op=ALU.mult), r=[b_am1r, b_CS], w=[b_am1r])
        S.dve(lambda e: e.tensor_tensor(out=tq[:], in0=sh[:], in1=sh[:], op=ALU.mult), r=[b_sh], w=[b_tq])
        S.dve(lambda e: e.scalar_tensor_tensor(out=am1r[:], in0=tq[:], scalar=-2.0, in1=am1r[:], op0=ALU.mult, op1=ALU.add),
              r=[b_tq, b_am1r], w=[b_am1r])
        S.dve(lambda e: e.tensor_copy(out=am1i[:], in_=EI[:, i1, :]), r=[b_EI, b_am1r], w=[b_am1i])
        S.dve(lambda e: e.tensor_tensor(out=rl2[:], in0=ar[:], in1=ar[:], op=ALU.mult), r=[b_ar], w=[b_rl2])
        S.dve(lambda e: e.tensor_tensor(out=tq[:], in0=ai[:], in1=ai[:], op=ALU.mult), r=[b_ai], w=[b_tq])
        S.dve(lambda e: e.tensor_tensor(out=rl2[:], in0=rl2[:], in1=tq[:], op=ALU.add), r=[b_rl2, b_tq], w=[b_rl2])
        S.dve(lambda e: e.reciprocal(out=rl2[:], in_=rl2[:]), r=[b_rl2], w=[b_rl2])
        S.dve(lambda e: e.tensor_tensor(out=kr[:], in0=am1r[:], in1=ar[:], op=ALU.mult), r=[b_am1r, b_ar], w=[b_kr])
        S.dve(lambda e: e.tensor_tensor(out=tq[:], in0=am1i[:], in1=ai[:], op=ALU.mult), r=[b_am1i, b_ai], w=[b_tq])
        S.dve(lambda e: e.tensor_tensor(out=kr[:], in0=kr[:], in1=tq[:], op=ALU.add), r=[b_kr, b_tq], w=[b_kr])
        S.dve(lambda e: e.tensor_tensor(out=kr[:], in0=kr[:], in1=rl2[:], op=ALU.mult), r=[b_kr, b_rl2], w=[b_kr])
        S.dve(lambda e: e.tensor_tensor(out=ki[:], in0=am1i[:], in1=ar[:], op=ALU.mult), r=[b_am1i, b_ar], w=[b_ki])
        S.dve(lambda e: e.tensor_tensor(out=tq[:], in0=am1r[:], in1=ai[:], op=ALU.mult), r=[b_am1r, b_ai], w=[b_tq])
        S.dve(lambda e: e.tensor_tensor(out=ki[:], in0=ki[:], in1=tq[:], op=ALU.subtract), r=[b_ki, b_tq], w=[b_ki])
        S.dve(lambda e: e.tensor_tensor(out=ki[:], in0=ki[:], in1=rl2[:], op=ALU.mult), r=[b_ki, b_rl2], w=[b_ki])
        krb = kr[:].unsqueeze(1).to_broadcast([128, 16, 32])
        kib = ki[:].unsqueeze(1).to_broadcast([128, 16, 32])
        S.dve(lambda e: e.tensor_tensor(out=ALR[:], in0=ER[:], in1=krb, op=ALU.mult), r=[b_ER, b_kr], w=[b_ALR])
        S.dve(lambda e: e.tensor_tensor(out=w3[:], in0=EI[:], in1=kib, op=ALU.mult), r=[b_EI, b_ki], w=[b_w3])
        S.dve(lambda e: e.tensor_tensor(out=ALR[:], in0=ALR[:], in1=w3[:], op=ALU.subtract), r=[b_ALR, b_w3], w=[b_ALR])
        S.dve(lambda e: e.tensor_tensor(out=ALI[:], in0=ER[:], in1=kib, op=ALU.mult), r=[b_ER, b_ki], w=[b_ALI])
        S.dve(lambda e: e.tensor_tensor(out=w3[:], in0=EI[:], in1=krb, op=ALU.mult), r=[b_EI, b_kr], w=[b_w3])
        S.dve(lambda e: e.tensor_tensor(out=ALI[:], in0=ALI[:], in1=w3[:], op=ALU.add), r=[b_ALI, b_w3], w=[b_ALI])
        S.dve(lambda e: e.tensor_scalar(out=ALI[:], in0=ALI[:], scalar1=sgn[:, 0:1], scalar2=None, op0=ALU.mult),
              r=[b_ALI, b_sgn], w=[b_ALI])
        for fam in range(2):
            dst, b_dst = (Pm, b_Pm) if fam == 0 else (Pp, b_Pp)
            for i in range(8):
                ti_ = (7 - i) if fam == 0 else (14 - i)
                ca = ALR[:, ti_, :].unsqueeze(2).to_broadcast([128, 32, 16])
                cb = ALI[:, ti_, :].unsqueeze(2).to_broadcast([128, 32, 16])
                S.dve(lambda e, ca=ca: e.tensor_tensor(out=u1[:], in0=Bn[:], in1=ca, op=ALU.mult), r=[b_Bn, b_ALR], w=[b_u1])
                S.dve(lambda e, cb=cb: e.tensor_tensor(out=u2[:], in0=Bs[:], in1=cb, op=ALU.mult), r=[b_Bs, b_ALI], w=[b_u2])
                S.dve(lambda e, dst=dst, i=i: e.tensor_tensor(out=dst[:, :, i * 16:(i + 1) * 16], in0=u1[:], in1=u2[:], op=ALU.add),
                      r=[b_u1, b_u2], w=[b_dst])
        for g in range(32):
            pt, b_pt = ptp[g % 2]
            S.pe(lambda e, g=g, pt=pt: e.transpose(out=pt[:], in_=Pp[:, g, :], identity=K.id_f[:]), r=[b_Pp, K.b_idf], w=[b_pt])
            S.dve(lambda e, g=g, pt=pt: e.tensor_copy(out=K.BA5[:, g, :], in_=pt[:]), r=[b_pt], w=[K.b_BA5])
        for sl in range(4):
            for part, (dst, b_dst) in enumerate(((CrT, b_CrT), (CiT, b_CiT))):
                pt, b_pt = ptp[(sl * 2 + part) % 2]
                S.pe(lambda e, sl=sl, part=part, pt=pt: e.transpose(out=pt[:], in_=c_in[:, sl, part * 128:(part + 1) * 128],
                                                                    identity=K.id_f[:]), r=[b_cin, K.b_idf], w=[b_pt])
                S.dve(lambda e, sl=sl, pt=pt, dst=dst: e.tensor_copy(
                    out=dst[:, sl * 8:(sl + 1) * 8, :], in_=pt[:].rearrange("p (g c) -> p g c", g=8)), r=[b_pt], w=[b_dst])
        S.dve(lambda e: e.tensor_copy(out=CA16[0:64], in_=ER[0:64]), r=[b_ER], w=[b_CA16])
        S.dve(lambda e: e.tensor_scalar(out=CA16[64:128], in0=EI[64:128], scalar1=-1.0, scalar2=None, op0=ALU.mult), r=[b_EI], w=[b_CA16])
        S.dve(lambda e: e.tensor_scalar(out=CB16[0:64], in0=EI[0:64], scalar1=-1.0, scalar2=None, op0=ALU.mult), r=[b_EI], w=[b_CB16])
        S.dve(lambda e: e.tensor_scalar(out=CB16[64:128], in0=ER[64:128], scalar1=-1.0, scalar2=None, op0=ALU.mult), r=[b_ER], w=[b_CB16])
        for fam in range(2):
            dst, b_dst = (Qm, b_Qm) if fam == 0 else (K.CA5, K.b_CA5)
            for j in range(8):
                ti_ = (7 + j) if fam == 0 else (8 + j)
                ca = CA16[:, ti_, :].unsqueeze(2).to_broadcast([128, 32, 16])
                cb = CB16[:, ti_, :].unsqueeze(2).to_broadcast([128, 32, 16])
                S.dve(lambda e, ca=ca: e.tensor_tensor(out=u1[:], in0=CrT[:], in1=ca, op=ALU.mult), r=[b_CrT, b_CA16], w=[b_u1])
                S.dve(lambda e, cb=cb: e.tensor_tensor(out=u2[:], in0=CiT[:], in1=cb, op=ALU.mult), r=[b_CiT, b_CB16], w=[b_u2])
                S.dve(lambda e, dst=dst, j=j: e.tensor_tensor(out=dst[:, :, j * 16:(j + 1) * 16], in0=u1[:], in1=u2[:], op=ALU.add),
                      r=[b_u1, b_u2], w=[b_dst])
        for g in range(32):
            pt, b_pt = ptp[g % 2]
            S.pe(lambda e, g=g, pt=pt: e.matmul(pt[:], lhsT=Pm[:, g, :], rhs=Qm[:, g, :], start=True, stop=True),
                 r=[b_Pm, b_Qm], w=[b_pt])
            S.dve(lambda e, pt=pt: e.tensor_tensor(out=tt[:], in0=pt[:], in1=tmask[:], op=ALU.mult), r=[b_pt, b_tmask], w=[b_tt])
            S.dve(lambda e, g=g: e.scalar_tensor_tensor(out=K.T5[:, g, :], in0=K.id_f[:], scalar=Dcol[:, g:g + 1], in1=tt[:],
                                                        op0=ALU.mult, op1=ALU.add), r=[K.b_idf, b_Dcol, b_tt], w=[K.b_T5])


def s5_setup_b(K):
    nc, S = K.nc, K.S
    I32 = mybir.dt.int32
    with ExitStack() as st:
        def sb(name, shape, dt):
            return K.sb(name, shape, dt, st)
        zin_, b_zin = sb("s5c_zin", [128, 3, 2048], F32)
        nv, b_nv = sb("s5c_nv", [128, 4], F32)
        sc, b_sc = sb("s5c_sc", [128, 4], F32)
        zrn, b_zrn = sb("s5c_zrn", [128, 2048], F32)
        phi, b_phi = sb("s5c_phi", [128, 2048], F32)
        mag, b_mag = sb("s5c_mag", [128, 2048], F32)
        th_, b_th_ = sb("s5c_th", [128, 2048], F32)
        sn, b_sn = sb("s5c_sn", [128, 2048], F32)
        cs, b_cs = sb("s5c_cs", [128, 2048], F32)
        x1, b_x1 = sb("s5c_x1", [128, 2048], F32)
        x2, b_x2 = sb("s5c_x2", [128, 2048], F32)
        xi, _ = sb("s5c_xi", [128, 2048], I32)
        tA, b_tA = sb("s5c_tA", [128, 32, 2, 64], BF16)
        tB, b_tB = sb("s5c_tB", [128, 32, 2, 64], BF16)
        eA, b_eA = sb("s5c_eA", [1, 32, 2, 64], F32)
        eB, b_eB = sb("s5c_eB", [1, 32, 2, 64], F32)
        S.dma(lambda e: e.dma_start(out=zin_[:], in_=K.s5_zrep), w=[b_zin])
        S.dma(lambda e: e.dma_start(out=nv[:], in_=K.nvec), w=[b_nv])
        S.dve(lambda e: e.tensor_scalar(out=sc[:, 0:1], in0=nv[:, 1:2], scalar1=-8.0, scalar2=None, op0=ALU.mult), r=[b_nv], w=[b_sc])
        S.dve(lambda e: e.tensor_scalar(out=sc[:, 1:2], in0=nv[:, 0:1], scalar1=8.0, scalar2=None, op0=ALU.mult), r=[b_nv], w=[b_sc])
        S.dve(lambda e: e.tensor_scalar(out=sc[:, 2:3], in0=nv[:, 1:2], scalar1=-1.0, scalar2=None, op0=ALU.mult), r=[b_nv], w=[b_sc])
        S.dve(lambda e: e.tensor_copy(out=sc[:, 3:4], in_=nv[:, 0:1]), r=[b_nv], w=[b_sc])
        S.act(lambda e: e.activation(out=zin_[:, 2, :], in_=zin_[:, 2, :], func=AF.Exp), r=[b_zin], w=[b_zin])
        S.dve(lambda e: e.tensor_tensor(out=zrn[:], in0=zin_[:, 0, :], in1=zin_[:, 2, :], op=ALU.mult), r=[b_zin], w=[b_zrn])
        S.dve(lambda e: e.scalar_tensor_tensor(out=x1[:], in0=zin_[:, 1, :], scalar=8.0, in1=zin_[:, 2, :], op0=ALU.mult, op1=ALU.mult),
              r=[b_zin], w=[b_x1])
        S.dve(lambda e: e.tensor_scalar(out=x2[:], in0=x1[:], scalar1=1.0 / (2 * PI), scalar2=64.5, op0=ALU.mult, op1=ALU.add), r=[b_x1], w=[b_x2])
        S.dve(lambda e: e.tensor_copy(out=xi[:], in_=x2[:]), r=[b_x2], w=[b_x2])
        S.dve(lambda e: e.tensor_copy(out=phi[:], in_=xi[:]), r=[b_x2], w=[b_phi])
        S.dve(lambda e: e.tensor_tensor(out=x2[:], in0=x2[:], in1=phi[:], op=ALU.subtract), r=[b_x2, b_phi], w=[b_x2])
        S.dve(lambda e: e.tensor_scalar(out=x2[:], in0=x2[:], scalar1=0.0, scalar2=None, op0=ALU.is_lt), r=[b_x2], w=[b_x2])
        S.dve(lambda e: e.tensor_tensor(out=phi[:], in0=phi[:], in1=x2[:], op=ALU.subtract), r=[b_x2, b_phi], w=[b_phi])
        S.dve(lambda e: e.tensor_scalar(out=phi[:], in0=phi[:], scalar1=-2 * PI, scalar2=128 * PI, op0=ALU.mult, op1=ALU.add), r=[b_phi], w=[b_phi])
        S.dve(lambda e: e.tensor_tensor(out=phi[:], in0=phi[:], in1=x1[:], op=ALU.add), r=[b_phi, b_x1], w=[b_phi])
        scr = (x1[:], xi[:], x2[:], b_x1)
        for which in range(2):
            S.act(lambda e, which=which: e.activation(out=mag[:], in_=zrn[:], func=AF.Exp, scale=sc[:, which:which + 1]),
                  r=[b_zrn, b_sc], w=[b_mag])
            S.dve(lambda e, which=which: e.tensor_scalar(out=th_[:], in0=phi[:], scalar1=sc[:, 2 + which:3 + which], scalar2=None,
                                                         op0=ALU.mult), r=[b_phi, b_sc], w=[b_th_])
            sin_eval(K, sn[:], th_[:], scr, [b_th_], [b_sn])
            sin_eval(K, cs[:], th_[:], scr, [b_th_], [b_cs], shift=PI / 2)
            S.dve(lambda e: e.tensor_tensor(out=cs[:], in0=cs[:], in1=mag[:], op=ALU.mult), r=[b_cs, b_mag], w=[b_cs])
            S.dve(lambda e: e.tensor_tensor(out=sn[:], in0=sn[:], in1=mag[:], op=ALU.mult), r=[b_sn, b_mag], w=[b_sn])
            cs3 = cs[:].rearrange("p (g q) -> p g q", g=32)
            sn3 = sn[:].rearrange("p (g q) -> p g q", g=32)
            S.dve(lambda e, cs3=cs3: e.tensor_copy(out=tA[:, :, 0, :], in_=cs3), r=[b_cs], w=[b_tA])
            S.dve(lambda e, cs3=cs3: e.tensor_copy(out=tA[:, :, 1, :], in_=cs3), r=[b_cs], w=[b_tA])
            S.dve(lambda e, sn3=sn3: e.tensor_scalar(out=tB[:, :, 0, :], in0=sn3, scalar1=-1.0, scalar2=None, op0=ALU.mult), r=[b_sn], w=[b_tB])
            S.dve(lambda e, sn3=sn3: e.tensor_copy(out=tB[:, :, 1, :], in_=sn3), r=[b_sn], w=[b_tB])
            S.dma(lambda e, which=which: e.dma_start(out=K.tab_d[2 * which], in_=tA[:].rearrange("p g r q -> p (g r q)")), r=[b_tA], q="gq")
            S.dma(lambda e, which=which: e.dma_start(out=K.tab_d[2 * which + 1], in_=tB[:].rearrange("p g r q -> p (g r q)")), r=[b_tB], q="gq")
        S.act(lambda e: e.activation(out=mag[0:1, :], in_=zrn[0:1, :], func=AF.Exp, scale=1024.0), r=[b_zrn], w=[b_mag])
        S.dve(lambda e: e.tensor_scalar(out=th_[0:1, :], in0=phi[0:1, :], scalar1=128.0, scalar2=None, op0=ALU.mult), r=[b_phi], w=[b_th_])
        scr1 = (x1[0:1, :], xi[0:1, :], x2[0:1, :], b_x1)
        sin_eval(K, sn[0:1, :], th_[0:1, :], scr1, [b_th_], [b_sn])
        sin_eval(K, cs[0:1, :], th_[0:1, :], scr1, [b_th_], [b_cs], shift=PI / 2)
        S.dve(lambda e: e.tensor_tensor(out=cs[0:1, :], in0=cs[0:1, :], in1=mag[0:1, :], op=ALU.mult), r=[b_cs, b_mag], w=[b_cs])
        S.dve(lambda e: e.tensor_tensor(out=sn[0:1, :], in0=sn[0:1, :], in1=mag[0:1, :], op=ALU.mult), r=[b_sn, b_mag], w=[b_sn])
        cs3 = cs[0:1, :].rearrange("p (g q) -> p g q", g=32)
        sn3 = sn[0:1, :].rearrange("p (g q) -> p g q", g=32)
        S.dve(lambda e: e.tensor_copy(out=eA[:, :, 0, :], in_=cs3), r=[b_cs], w=[b_eA])
        S.dve(lambda e: e.tensor_copy(out=eA[:, :, 1, :], in_=cs3), r=[b_cs], w=[b_eA])
        S.dve(lambda e: e.tensor_scalar(out=eB[:, :, 0, :], in0=sn3, scalar1=-1.0, scalar2=None, op0=ALU.mult), r=[b_sn], w=[b_eB])
        S.dve(lambda e: e.tensor_copy(out=eB[:, :, 1, :], in_=sn3), r=[b_sn], w=[b_eB])
        S.dma(lambda e: e.dma_start(out=K.tab_e[0:1, :], in_=eA[:].rearrange("p g r q -> p (g r q)")), r=[b_eA], q="gq")
        S.dma(lambda e: e.dma_start(out=K.tab_e[1:2, :], in_=eB[:].rearrange("p g r q -> p (g r q)")), r=[b_eB], q="gq")
        S.flush()
        for nm, src in (("T5", K.T5), ("BA5", K.BA5), ("CA5", K.CA5)):
            if nm in K.dbg_out:
                S.dma(lambda e, nm=nm, src=src: e.dma_start(out=K.dbg_out[nm], in_=src[:]), q="gq")
        if "tab" in K.dbg_out:
            S.dma(lambda e: e.dma_start(out=K.dbg_out["tab"], in_=K.tab_d), q="gq")
        if "tabe" in K.dbg_out:
            S.dma(lambda e: e.dma_start(out=K.dbg_out["tabe"], in_=K.tab_e), q="gq")
        S.flush()


def phase1(K):
    nc, S, NB = K.nc, K.S, K.NB
    with ExitStack() as st:
        def sb(name, shape, dt):
            return K.sb(name, shape, dt, st)

        def ps(name, shape, dt):
            return K.ps(name, shape, dt, st)
        xt = [sb(f"p1_x{i}", [128, 8, D], F32) for i in range(2)]
        hns = [sb(f"p1_hn{i}", [128, 8, D], BF16) for i in range(2)]
        hT = [sb(f"p1_hT{i}", [128, 8, 1024], BF16) for i in range(2)]
        junk, b_junk = sb("p1_junk", [128, D], BF16)
        ss, b_ss = sb("p1_ss", [128, 8], F32)
        rstd, b_rstd = sb("p1_rstd", [128, 8], F32)
        wf, b_wf = sb("p1_wf", [128, 8, 8], BF16)
        wu1, b_wu1 = sb("p1_wu", [128, 8, 512], BF16)
        Ut1 = [sb(f"p1_Ut{i}", [128, 32, 128], BF16) for i in range(2)]
        pu1 = [ps(f"p1_pu{i}", [128, 512], F32) for i in range(2)]
        bf_s, b_bf = sb("p1_bf", [128, 64], F32)
        lgs = [sb(f"p1_lg{i}", [128, 8, 8], F32) for i in range(2)]
        ppfs = [sb(f"p1_ppfs{i}", [128, 16], F32) for i in range(2)]
        car = [sb(f"p1_car{i}", [128, 8], F32) for i in range(2)]
        cT32, b_cT32 = sb("p1_cT32", [8, 1024], F32)
        cT16, b_cT16 = sb("p1_cT16", [8, 1024], BF16)
        gcol, b_gcol = sb("p1_gcol", [8, 1], F32)
        ptr = [ps(f"p1_ptr{i}", [128, 1024], BF16) for i in range(2)]
        pf, b_pf = ps("p1_pf", [128, 8, 8], F32)
        ppf, b_ppf = ps("p1_ppf", [128, 16], F32)
        pcT, b_pcT = ps("p1_pcT", [8, 1024], F32)

        S.dma(lambda e: e.dma_start(out=wf[:], in_=K.wb_in[:, :, C_F:C_F + 8]), w=[b_wf])
        S.dma(lambda e: e.dma_start(out=wu1[:], in_=K.wb_in[:, :, C_U:C_U + 512]), w=[b_wu1])
        S.dma(lambda e: e.dma_start(out=bf_s[:], in_=K.bf_rep), w=[b_bf])
        def stageA(s):
            x_t, b_x = xt[s % 2]
            hn, b_hn = hns[s % 2]
            S.dma(lambda e, s=s, x_t=x_t: e.dma_start(
                out=x_t[:], in_=K.x[s * 1024:(s + 1) * 1024, :].rearrange("(n j) d -> n j d", j=8)), w=[b_x])
            for j in range(8):
                S.act(lambda e, j=j, x_t=x_t: e.activation(out=junk[:], in_=x_t[:, j, :], func=AF.Square,
                                                           accum_out=ss[:, j:j + 1]), r=[b_x], w=[b_junk, b_ss])
            S.act(lambda e: e.activation(out=rstd[:], in_=ss[:], func=AF.Sqrt, scale=1.0 / D, bias=K.cst[:, 0:1]),
                  r=[b_ss, K.b_cst], w=[b_rstd])
            S.dve(lambda e: e.reciprocal(out=rstd[:], in_=rstd[:]), r=[b_rstd], w=[b_rstd])
            for j in range(8):
                if j % 2 == 0:
                    S.dve(lambda e, j=j, x_t=x_t: e.tensor_scalar(
                        out=hn[:, j, :], in0=x_t[:, j, :], scalar1=rstd[:, j:j + 1], scalar2=None, op0=ALU.mult),
                        r=[b_x, b_rstd], w=[b_hn])
                else:
                    S.act(lambda e, j=j, x_t=x_t: e.activation(
                        out=hn[:, j, :], in_=x_t[:, j, :], func=AF.Copy, scale=rstd[:, j:j + 1]),
                        r=[b_x, b_rstd], w=[b_hn])

        def stageB(s):
            hn, b_hn = hns[s % 2]
            h_t, b_h = hT[s % 2]
            for kc in range(8):
                pt, b_pt = ptr[kc % 2]
                for j in range(8):
                    S.pe(lambda e, kc=kc, j=j, pt=pt: e.transpose(
                        out=pt[:, j * 128:(j + 1) * 128], in_=hn[:, j, kc * 128:(kc + 1) * 128],
                        identity=K.id_b[:]), r=[b_hn, K.b_idb], w=[b_pt])
                S.act(lambda e, kc=kc, pt=pt, h_t=h_t: e.activation(
                    out=h_t[:, kc, :], in_=pt[:], func=AF.Identity,
                    bias=K.modc[:, kc:kc + 1], scale=K.acol[:, kc:kc + 1]),
                    r=[b_pt, K.b_modc, K.b_acol], w=[b_h])
            S.dma(lambda e, s=s, h_t=h_t: e.dma_start(out=K.hT_d[s], in_=h_t[:]), r=[b_h], q="gq")
            if "skip3" not in K.flags:
                U_t, b_U = Ut1[s % 2]
                for j in range(8):
                    pu_, b_pu = pu1[j % 2]
                    for kc in range(8):
                        S.pe(lambda e, j=j, kc=kc, h_t=h_t, pu_=pu_: e.matmul(
                            pu_[:], lhsT=h_t[:, kc, j * 128:(j + 1) * 128], rhs=wu1[:, kc, :],
                            start=(kc == 0), stop=(kc == 7)), r=[b_h, b_wu1], w=[b_pu])
                    pu3 = pu_[:].rearrange("p (g c) -> p g c", g=32)
                    if j % 2 == 0:
                        S.dve(lambda e, j=j, pu3=pu3, U_t=U_t: e.tensor_copy(out=U_t[:, :, j * 16:(j + 1) * 16], in_=pu3),
                              r=[b_pu], w=[b_U])
                    else:
                        S.act(lambda e, j=j, pu3=pu3, U_t=U_t: e.activation(out=U_t[:, :, j * 16:(j + 1) * 16], in_=pu3, func=AF.Copy),
                              r=[b_pu], w=[b_U])
                S.dma(lambda e, s=s, U_t=U_t: e.dma_start(out=K.ut_d[s], in_=U_t[:]), r=[b_U], q="gq")
            for j in range(8):
                for kc in range(8):
                    S.pe(lambda e, j=j, kc=kc, h_t=h_t: e.matmul(
                        pf[:, j, :], lhsT=h_t[:, kc, j * 128:(j + 1) * 128], rhs=wf[:, kc, :],
                        start=(kc == 0), stop=(kc == 7)), r=[b_h, b_wf], w=[b_pf])
            lg, b_lg = lgs[s % 2]
            pfs, b_pfs = ppfs[s % 2]
            pf2 = pf[:].rearrange("p j h -> p (j h)")
            lg2 = lg[:].rearrange("p j h -> p (j h)")
            S.dve(lambda e, lg2=lg2, pf2=pf2: e.tensor_tensor(out=lg2, in0=pf2, in1=bf_s[:], op=ALU.add), r=[b_pf, b_bf], w=[b_lg])
            S.act(lambda e, lg2=lg2: e.activation(out=lg2, in_=lg2, func=AF.Exp, scale=-1.0), r=[b_lg], w=[b_lg])
            S.act(lambda e, lg2=lg2: e.activation(out=lg2, in_=lg2, func=AF.Ln, bias=K.cst[:, 1:2], scale=1.0),
                  r=[b_lg, K.b_cst], w=[b_lg])
            for j in range(1, 8):
                S.dve(lambda e, j=j, lg=lg: e.tensor_tensor(out=lg[:, j, :], in0=lg[:, j, :], in1=lg[:, j - 1, :], op=ALU.add),
                      r=[b_lg], w=[b_lg])
            S.pe(lambda e, lg=lg: e.matmul(ppf[:, 0:8], lhsT=K.tri_fs[:], rhs=lg[:, 7, :], start=True, stop=True),
                 r=[K.b_trif, b_lg], w=[b_ppf])
            S.pe(lambda e, lg=lg: e.matmul(ppf[:, 8:16], lhsT=K.ones_f[:], rhs=lg[:, 7, :], start=True, stop=True),
                 r=[K.b_onesf, b_lg], w=[b_ppf])
            S.dve(lambda e, pfs=pfs: e.tensor_copy(out=pfs[:], in_=ppf[:]), r=[b_ppf], w=[b_pfs])
            if s % 2 == 1:
                A, B = s - 1, s
                (lgA, b_lgA), (lgB, b_lgB) = lgs
                (pA, b_pA), (pB, b_pB) = ppfs
                (cA, b_cA), (cB, b_cB) = car
                S.dve(lambda e: e.scalar_tensor_tensor(out=cA[:], in0=pB[:, 8:16], scalar=K.wj[:, 0:1], in1=K.Gcar[:],
                                                       op0=ALU.mult, op1=ALU.add), r=[b_pB, K.b_wj, K.b_Gcar], w=[b_cA])
                S.dve(lambda e: e.scalar_tensor_tensor(out=cB[:], in0=pA[:, 8:16], scalar=K.wj[:, 1:2], in1=K.Gcar[:],
                                                       op0=ALU.mult, op1=ALU.add), r=[b_pA, K.b_wj, K.b_Gcar], w=[b_cB])
                for (P_, lgX, b_lgX, pX, b_pX, cX, b_cX) in ((A, lgA, b_lgA, pA, b_pA, cA, b_cA), (B, lgB, b_lgB, pB, b_pB, cB, b_cB)):
                    S.dve(lambda e, P_=P_, pX=pX, cX=cX: e.tensor_tensor(out=K.Gend[:, P_, :], in0=pX[:, 8:16], in1=cX[:], op=ALU.add),
                          r=[b_pX, b_cX], w=[K.b_Gend])
                    S.dve(lambda e, pX=pX, cX=cX: e.tensor_tensor(out=pX[:, 0:8], in0=pX[:, 0:8], in1=cX[:], op=ALU.add),
                          r=[b_pX, b_cX], w=[b_pX])
                    S.dve(lambda e, P_=P_, lgX=lgX, pX=pX: e.tensor_tensor(
                        out=K.G[:, P_ * 8:(P_ + 1) * 8, :], in0=lgX[:], in1=pX[:, 0:8].unsqueeze(1).to_broadcast([128, 8, 8]),
                        op=ALU.add), r=[b_lgX, b_pX], w=[K.b_G])
                S.dve(lambda e: e.tensor_tensor(out=K.Gcar[:], in0=K.Gcar[:], in1=pA[:, 8:16], op=ALU.add),
                      r=[K.b_Gcar, b_pA], w=[K.b_Gcar])
                S.dve(lambda e: e.tensor_tensor(out=K.Gcar[:], in0=K.Gcar[:], in1=pB[:, 8:16], op=ALU.add),
                      r=[K.b_Gcar, b_pB], w=[K.b_Gcar])
                for j in range(8):
                    S.pe(lambda e, A=A, j=j: e.transpose(out=pcT[:, j * 128:(j + 1) * 128], in_=K.G[:, A * 8 + j, :],
                                                         identity=K.id_f[:]), r=[K.b_G, K.b_idf], w=[b_pcT])
                S.dve(lambda e: e.tensor_copy(out=cT32[:], in_=pcT[:]), r=[b_pcT], w=[b_cT32])
                S.dve(lambda e: e.tensor_copy(out=gcol[:], in_=cT32[:, 1023:1024]), r=[b_cT32], w=[b_gcol])
                S.dve(lambda e: e.tensor_scalar(out=cT16[:], in0=cT32[:], scalar1=gcol[:, 0:1], scalar2=-1.0,
                                                op0=ALU.subtract, op1=ALU.mult), r=[b_cT32, b_gcol], w=[b_cT16])
                S.dma(lambda e, A=A: e.dma_start(out=K.c_d[A // 2], in_=cT16[:]), r=[b_cT16], q="gq")
            if s == 0 and "hT" in K.dbg_out:
                S.dma(lambda e, h_t=h_t: e.dma_start(out=K.dbg_out["hT"], in_=h_t[:]), r=[b_h], q="gq")
        stageA(0)
        for s in range(NB):
            if s + 1 < NB:
                stageA(s + 1)
            stageB(s)
        if "G" in K.dbg_out:
            S.dma(lambda e: e.dma_start(out=K.dbg_out["G"], in_=K.G[:]), r=[K.b_G], q="gq")
        S.flush()


def phase2(K):
    nc, S, NB = K.nc, K.S, K.NB
    T = K.T
    with ExitStack() as st:
        def sb(name, shape, dt):
            return K.sb(name, shape, dt, st)

        def ps(name, shape, dt):
            return K.ps(name, shape, dt, st)
        NR = NB // 2
        QA, _ = sb("QA", [128, T // 2], BF16)
        QB, _ = sb("QB", [128, T // 2], BF16)
        KA, _ = sb("KA", [128, T], BF16)
        KB, _ = sb("KB", [128, T], BF16)
        V2, _ = sb("V2", [128, NB * 8, 192], BF16)
        b_Q = [S.buf(f"Q{s}") for s in range(NB // 2)]
        b_K = [S.buf(f"K{s}") for s in range(NB)]
        b_V = [S.buf(f"V{s}") for s in range(NB)]
        hTb = [sb(f"p2_hT{i}", [128, 8, 1024], BF16) for i in range(2)]
        wq, b_wq = sb("p2_wq", [128, 8, 128], BF16)
        wk, b_wk = sb("p2_wk", [128, 8, 128], BF16)
        wv, b_wv = sb("p2_wv", [128, 8, 128], BF16)
        cm5, b_cm5 = sb("p2_cm5", [128, 5, 512], BF16)
        mB, b_mB = sb("p2_mB", [128, 512], BF16)
        biasq = [sb(f"p2_biasq{i}", [128, NB * 8, 8], F32) for i in range(2)]
        pT = [sb(f"p2_pT{i}", [128, 512], BF16) for i in range(4)]
        rden = [sb(f"p2_rden{i}", [128, 512], F32) for i in range(2)]
        num = [sb(f"p2_num{i}", [128, 512], F32) for i in range(2)]
        yo = [sb(f"p2_yo{i}", [128, 512], BF16) for i in range(2)]
        pss = [ps(f"p2_s{i}", [128, 512], F32) for i in range(3)]
        po = [ps(f"p2_o{i}", [128, 512], F32) for i in range(2)]
        pb, b_pb = ps("p2_b", [128, 512], F32)
        pp = [ps(f"p2_p{i}", [128, 512], F32) for i in range(2)]

        for i, tile_ in enumerate((QA, QB, KA, KB)):
            for s in range(NB // 2 if i < 2 else NB):
                bb = b_Q[s] if i < 2 else b_K[s]
                eng = S.pool if (i + s) % 2 else S.dve
                eng(lambda e, tile_=tile_, s=s: e.memset(tile_[:, s * 1024:(s + 1) * 1024], 0.0), w=[bb])
        for s in range(NB):
            S.dve(lambda e, s=s: e.memset(KA[64:65, s * 1024:(s + 1) * 1024], 1.0), w=[b_K[s]])
            S.dve(lambda e, s=s: e.memset(KB[0:1, s * 1024:(s + 1) * 1024], 1.0), w=[b_K[s]])
            S.pool(lambda e, s=s: e.memset(V2[:, s * 8:(s + 1) * 8, 64:128], 0.0), w=[b_V[s]])
            S.pool(lambda e, s=s: e.memset(V2[:, s * 8:(s + 1) * 8, 64:65], 1.0), w=[b_V[s]])
        S.dma(lambda e: e.dma_start(out=cm5[:], in_=K.cm5), w=[b_cm5])
        S.dma(lambda e: e.dma_start(out=mB[:], in_=K.mB_in), w=[b_mB])
        mBc, _ = sb("p2_mBc", [128, 1], F32)
        S.dve(lambda e: e.tensor_copy(out=mBc[:], in_=mB[:, 0:1]), r=[b_mB], w=[b_mB])

        ti = 0
        for hp in range(4):
            S.dma(lambda e, hp=hp: e.dma_start(out=wq[:], in_=K.wb_in[:, :, C_Q + hp * 128:C_Q + (hp + 1) * 128]), w=[b_wq])
            S.dma(lambda e, hp=hp: e.dma_start(out=wk[:], in_=K.wb_in[:, :, C_K + hp * 128:C_K + (hp + 1) * 128]), w=[b_wk])
            S.dma(lambda e, hp=hp: e.dma_start(out=wv[:], in_=K.wb_in[:, :, C_V + hp * 128:C_V + (hp + 1) * 128]), w=[b_wv])
            for s in range(NB):
                h_t, b_h = hTb[s % 2]
                S.dma(lambda e, s=s, h_t=h_t: e.dma_start(out=h_t[:], in_=K.hT_d[s]), w=[b_h])
                own = (s % 2 == 0)
                rr = s // 2
                if own:
                    S.dma(lambda e, rr=rr, hp=hp: e.dma_start(out=QA[64:65, rr * 1024:(rr + 1) * 1024],
                                                              in_=K.c_d[rr, 2 * hp:2 * hp + 1, :]), w=[b_Q[rr]])
                    S.dma(lambda e, rr=rr, hp=hp: e.dma_start(out=QB[0:1, rr * 1024:(rr + 1) * 1024],
                                                              in_=K.c_d[rr, 2 * hp + 1:2 * hp + 2, :]), w=[b_Q[rr]])
                for which in ((0, 1) if own else (1,)):
                    wt, b_wt = (wq, b_wq) if which == 0 else (wk, b_wk)
                    tA, tB = (QA, QB) if which == 0 else (KA, KB)
                    bb = b_Q[rr] if which == 0 else b_K[s]
                    scl = 0.125 if which == 0 else 1.0
                    for half in range(2):
                        p_t, b_p = pp[ti % 2]
                        ti += 1
                        for kc in range(8):
                            S.pe(lambda e, kc=kc, half=half, p_t=p_t, wt=wt, h_t=h_t: e.matmul(
                                p_t[:], lhsT=wt[:, kc, :], rhs=h_t[:, kc, half * 512:(half + 1) * 512],
                                start=(kc == 0), stop=(kc == 7)), r=[b_wt, b_h], w=[b_p])
                        c0 = (rr if which == 0 else s) * 1024 + half * 512
                        S.act(lambda e, p_t=p_t, tA=tA, c0=c0, scl=scl: e.activation(
                            out=tA[0:64, c0:c0 + 512], in_=p_t[0:64, :], func=AF.Copy, scale=scl),
                            r=[b_p], w=[bb])
                        S.dve(lambda e, p_t=p_t, tB=tB, c0=c0, scl=scl: e.tensor_scalar(
                            out=tB[64:128, c0:c0 + 512], in0=p_t[64:128, :], scalar1=scl, scalar2=None, op0=ALU.mult),
                            r=[b_p], w=[bb])
                for jh in range(2):
                    p_t, b_p = pp[ti % 2]
                    ti += 1
                    for jj in range(4):
                        j = jh * 4 + jj
                        for kc in range(8):
                            S.pe(lambda e, kc=kc, j=j, jj=jj, p_t=p_t, h_t=h_t: e.matmul(
                                p_t[:, jj * 128:(jj + 1) * 128], lhsT=h_t[:, kc, j * 128:(j + 1) * 128], rhs=wv[:, kc, :],
                                start=(kc == 0), stop=(kc == 7)), r=[b_wv, b_h], w=[b_p])
                    kt0 = s * 8 + jh * 4
                    pv = p_t[:].rearrange("p (j c) -> p j c", j=4)
                    S.dve(lambda e, pv=pv, kt0=kt0: e.tensor_copy(out=V2[:, kt0:kt0 + 4, 0:64], in_=pv[:, :, 0:64]),
                          r=[b_p], w=[b_V[s]])
                    S.act(lambda e, pv=pv, kt0=kt0: e.activation(out=V2[:, kt0:kt0 + 4, 128:192], in_=pv[:, :, 64:128],
                                                                 func=AF.Copy), r=[b_p], w=[b_V[s]])
            tiles = []
            for sq in range(NB // 2):
                nkt = (2 * sq + 2) * 8
                for hq in range(2):
                    for hl in range(2):
                        for kt in range(nkt):
                            tiles.append((sq, hq, hl, kt, nkt))
            n_t_ = len(tiles)
            LA = 1 if "la1" in K.flags else 2
            bias_done = set()
            pending = []
            gctr = [0]

            def emit_qk(i):
                sq, hq, hl, kt, nkt = tiles[i]
                if sq not in bias_done:
                    bias_done.add(sq)
                    bq, b_bq = biasq[sq % 2]
                    S.dve(lambda e, sq=sq, nkt=nkt, bq=bq: e.tensor_tensor(
                        out=bq[:, 0:nkt, :], in0=K.G[:, 0:nkt, :],
                        in1=K.Gend[:, 2 * sq, :].unsqueeze(1).to_broadcast([128, nkt, 8]), op=ALU.subtract),
                        r=[K.b_G, K.b_Gend], w=[b_bq])
                    k0 = (2 * sq + 1) * 8
                    S.dve(lambda e, bq=bq, k0=k0: e.tensor_scalar(out=bq[:, k0:k0 + 8, :], in0=bq[:, k0:k0 + 8, :], scalar1=mBc[:, 0:1],
                                                                 scalar2=None, op0=ALU.add), r=[b_bq, b_mB], w=[b_bq])
                sk, jk = kt // 8, kt % 8
                q0 = sq * 1024 + hq * 512
                Qt = QA if hl == 0 else QB
                Kt = KA if hl == 0 else KB
                s_t, b_s = pss[i % 3]
                diag = (sk == 2 * sq)
                partner = (sk == 2 * sq + 1)
                S.pe(lambda e: e.matmul(s_t[:], lhsT=Kt[:, kt * 128:(kt + 1) * 128], rhs=Qt[:, q0:q0 + 512],
                                        start=True, stop=(not diag)), r=[b_K[sk], b_Q[sq]], w=[b_s])
                if diag:
                    a = min(max(jk - 4 * hq, 0), 4)
                    S.pe(lambda e: e.matmul(s_t[:], lhsT=K.id_b[:], rhs=cm5[:, a, :], start=False, stop=True),
                         r=[K.b_idb, b_cm5], w=[b_s])


            def emit_exp(i):
                sq, hq, hl, kt, nkt = tiles[i]
                h = 2 * hp + hl
                s_t, b_s = pss[i % 3]
                p_t, b_p = pT[i % 4]
                bq, b_bq = biasq[sq % 2]
                S.act(lambda e: e.activation(out=p_t[:], in_=s_t[:], func=AF.Exp, bias=bq[:, kt, h:h + 1], scale=1.0),
                      r=[b_s, b_bq], w=[b_p])

            def emit_pv(i):
                sq, hq, hl, kt, nkt = tiles[i]
                sk = kt // 8
                p_t, b_p = pT[i % 4]
                if kt == 0:
                    gctr[0] += 1
                g = gctr[0]
                o_t, b_o = po[g % 2]
                if hl == 0:
                    S.pe(lambda e: e.matmul(o_t[0:65, :], lhsT=V2[:, kt, 0:65], rhs=p_t[:],
                                            start=(kt == 0), stop=(kt == nkt - 1)), r=[b_V[sk], b_p], w=[b_o])
                else:
                    S.pe(lambda e: e.matmul(o_t[:], lhsT=V2[:, kt, 64:192], rhs=p_t[:],
                                            start=(kt == 0), stop=(kt == nkt - 1)), r=[b_V[sk], b_p], w=[b_o])
                if kt == nkt - 1:
                    n_t, b_n = num[g % 2]
                    rd, b_rd = rden[g % 2]
                    y_t, b_y = yo[hq]
                    if hl == 0:
                        S.dve(lambda e: e.reciprocal(out=rd[64:65, :], in_=o_t[64:65, :]), r=[b_o], w=[b_rd])
                        S.dve(lambda e: e.tensor_copy(out=n_t[0:64, :], in_=o_t[0:64, :]), r=[b_o], w=[b_n])
                    else:
                        S.dve(lambda e: e.reciprocal(out=rd[0:1, :], in_=o_t[0:1, :]), r=[b_o], w=[b_rd])
                        S.dve(lambda e: e.tensor_copy(out=n_t[64:128, :], in_=o_t[64:128, :]), r=[b_o], w=[b_n])

                    def stage2():
                        if hl == 0:
                            S.pe(lambda e: e.matmul(pb[0:64, :], lhsT=K.ones_f[64:65, 0:64], rhs=rd[64:65, :],
                                                    start=True, stop=True), r=[K.b_onesf, b_rd], w=[b_pb])
                            S.dve(lambda e: e.tensor_tensor(out=y_t[0:64, :], in0=n_t[0:64, :], in1=pb[0:64, :], op=ALU.mult),
                                  r=[b_n, b_pb], w=[b_y])
                        else:
                            S.pe(lambda e: e.matmul(pb[:], lhsT=K.ones_f[0:1, :], rhs=rd[0:1, :],
                                                    start=True, stop=True), r=[K.b_onesf, b_rd], w=[b_pb])
                            S.dve(lambda e: e.tensor_tensor(out=y_t[64:128, :], in0=n_t[64:128, :], in1=pb[64:128, :], op=ALU.mult),
                                  r=[b_n, b_pb], w=[b_y])
                            S.dma(lambda e, hp=hp: e.dma_start(out=K.yaT_d[sq, :, hp, hq * 512:(hq + 1) * 512], in_=y_t[:]),
                                  r=[b_y], q="sp")
                    pending.append((i + (0 if "nodefer" in K.flags else 3), stage2))

            for i in range(min(LA, n_t_)):
                emit_qk(i)
            for i in range(n_t_):
                emit_exp(i)
                if i + LA < n_t_:
                    emit_qk(i + LA)
                emit_pv(i)
                while pending and pending[0][0] <= i:
                    pending.pop(0)[1]()
            while pending:
                pending.pop(0)[1]()
        S.flush()
        if "yaT" in K.dbg_out:
            S.dma(lambda e: e.dma_start(out=K.dbg_out["yaT"], in_=K.yaT_d), q="gq")
            S.flush()


def phase3(K):
    nc, S, NB = K.nc, K.S, K.NB
    GC = 0.7978845608028654
    with ExitStack() as st:
        def sb(name, shape, dt):
            return K.sb(name, shape, dt, st)

        def ps(name, shape, dt):
            return K.ps(name, shape, dt, st)
        hTb, b_h = sb("p3_hT", [128, 8, 1024], BF16)
        yaT, b_ya = sb("p3_yaT", [128, 4, 512], BF16)
        wua4, b_wua = sb("p3_wupa", [128, 4, 1024], BF16)
        wub4, b_wub = sb("p3_wupb", [128, 4, 1024], BF16)
        wgl, b_wgl = sb("p3_wglu", [128, 4, 512], BF16)
        wbuf = [sb(f"p3_w{i}", [128, 8, 512], BF16) for i in range(4)]
        tabs = [sb(f"p3_tab{i}", [128, 4, 512], BF16) for i in range(2)]
        UtXp, b_UtXp = sb("p3_UtXp", [128, 32, 128], BF16)
        UT, b_UT = sb("p3_UT", [128, 32, 128], BF16)
        SY, b_SY = sb("p3_SY", [128, 32, 128], BF16)
        XpT, b_XpT = sb("p3_XpT", [128, 32, 128], BF16)
        ybT, b_ybT = sb("p3_ybT", [128, 4, 1024], BF16)
        tmp = [sb(f"p3_t{i}", [128, 512], F32) for i in range(6)]
        thb = [sb(f"p3_th{i}", [128, 512], BF16) for i in range(4)]
        Xin8, b_Xin = sb("p3_Xin8", [8, 512], BF16)
        totB8, b_totB = sb("p3_totB8", [8, 512], F32)
        rows = [sb(f"p3_row{i}", [8, 512], F32) for i in range(4)]
        xown8, b_xown = sb("p3_xown8", [8, 512], BF16)
        e128, b_e128 = sb("p3_e128", [8, 2, 512], F32)
        ecol, b_ecol = sb("p3_ecol", [128, 64], BF16)
        erws, b_erws = sb("p3_erws", [8, 1024], BF16)
        xj = [sb(f"p3_x{i}", [128, D], F32) for i in range(2)]
        junk, b_junk = sb("p3_junk", [128, D], BF16)
        gfin, b_gfin = sb("p3_gfin", [128, D], F32)
        bgl, b_bgl = sb("p3_bgl", [128, 4], F32)
        ss2, b_ss2 = sb("p3_ss2", [128, 2], F32)
        PB = [ps(f"p3_pb{i}", [128, 1024], BF16) for i in range(2)]
        PF = [ps(f"p3_pf{i}", [128, 512], F32) for i in range(5)]
        P8 = ps("p3_p8", [128, 512], F32)
        merged = UT[:].rearrange("p g q -> p (g q)").rearrange("p (k n) -> p k n", k=8)
        yag = XpT[:].rearrange("p g q -> p (g q)")[:, 0:2048].rearrange("p (k n) -> p k n", k=4)
        ybg = XpT[:].rearrange("p g q -> p (g q)")[:, 2048:4096].rearrange("p (k n) -> p k n", k=4)
        cnt = {"w": 0, "f": 0, "b": 0, "t": 0, "h": 0, "tab": 0, "x": 0}

        def nxt(key, pool_):
            i = cnt[key]
            cnt[key] += 1
            return pool_[i % len(pool_)]

        wsrc = {"u": K.wb_in[:, :, C_U:C_U + 512], "za": K.wb_in[:, :, C_ZA:C_ZA + 512], "zb": K.wb_in[:, :, C_ZB:C_ZB + 512],
                "ga0": K.wb_in[:, :, C_GA:C_GA + 512], "gb0": K.wb_in[:, :, C_GB:C_GB + 512],
                "ga1": K.wb_in[:, :, C_GA + 512:C_GA + 1024], "gb1": K.wb_in[:, :, C_GB + 512:C_GB + 1024],
                "o0": K.wb_out[:, :, 0:512], "o1": K.wb_out[:, :, 512:1024]}
        stages = []
        for r_ in range(NB // 2):
            for h_ in range(2):
                stages += [["za"], ["zb"], ["ga0", "gb0"], ["ga1", "gb1"], ["o0", "o1"]]
        wstate = {"stage": -1, "issued": 0, "tiles": {}}

        def _issue_stage(k):
            if k >= len(stages) or k < wstate["issued"]:
                return
            for kk in range(wstate["issued"], k + 1):
                for key in stages[kk]:
                    wt, b_wt = nxt("w", wbuf)
                    S.dma(lambda e, wt=wt, key=key: e.dma_start(out=wt[:], in_=wsrc[key]), w=[b_wt])
                    wstate["tiles"][(kk, key)] = (wt, b_wt)
            wstate["issued"] = k + 1

        def wstage(expect):
            wstate["stage"] += 1
            k = wstate["stage"]
            assert stages[k] == expect, (k, stages[k], expect)
            _issue_stage(k)
            _issue_stage(k + 1)
            return [wstate["tiles"].pop((k, key)) for key in stages[k]]

        S.dma(lambda e: e.dma_start(out=gfin[:], in_=K.gfin_rep), w=[b_gfin])
        S.dma(lambda e: e.dma_start(out=bgl[:], in_=K.bglu_col), w=[b_bgl])
        S.dma(lambda e: e.dma_start(out=wua4[:], in_=K.wb_upa), w=[b_wua])
        S.dve(lambda e: e.tensor_scalar(out=wua4[:], in0=wua4[:], scalar1=4.0, scalar2=None, op0=ALU.mult), r=[b_wua], w=[b_wua])
        S.dma(lambda e: e.dma_start(out=wub4[:], in_=K.wb_upb), w=[b_wub])
        S.dma(lambda e: e.dma_start(out=wgl[:], in_=K.wb_glu), w=[b_wgl])
        S.dve(lambda e: e.tensor_scalar(out=bgl[:], in0=bgl[:], scalar1=0.5, scalar2=None, op0=ALU.mult), r=[b_bgl], w=[b_bgl])
        S.dve(lambda e: e.memset(Xin8[:], 0.0), w=[b_Xin])
        S.dma(lambda e: e.dma_start(out=e128[:], in_=K.tab_e.rearrange("t (s q) -> s t q", s=8)), w=[b_e128])
        S.dma(lambda e: e.dma_start(out=ecol[:], in_=K.ecol_in), w=[b_ecol])
        S.dma(lambda e: e.dma_start(out=erws[:], in_=K.erow_in), w=[b_erws])

        def cscale(psrc, b_psrc, tabA, tabB, b_tab, dst, b_dst, extra_r=()):
            t1, b_t1 = nxt("t", tmp)
            t2, b_t2 = nxt("t", tmp)
            npart = dst.shape[0]
            S.dve(lambda e: e.tensor_tensor(out=t1[0:npart, :], in0=psrc, in1=tabA, op=ALU.mult),
                  r=[b_psrc, b_tab] + list(extra_r), w=[b_t1])
            p4 = psrc.rearrange("p (g r q) -> p g r q", g=4, r=2)
            b4 = tabB.rearrange("p (g r q) -> p g r q", g=4, r=2)
            t4 = t2[0:npart, :].rearrange("p (g r q) -> p g r q", g=4, r=2)
            S.dve(lambda e: e.tensor_tensor(out=t4[:, :, 0, :], in0=p4[:, :, 1, :], in1=b4[:, :, 0, :], op=ALU.mult),
                  r=[b_psrc, b_tab], w=[b_t2])
            S.dve(lambda e: e.tensor_tensor(out=t4[:, :, 1, :], in0=p4[:, :, 0, :], in1=b4[:, :, 1, :], op=ALU.mult),
                  r=[b_psrc, b_tab], w=[b_t2])
            S.dve(lambda e: e.tensor_tensor(out=dst, in0=t1[0:npart, :], in1=t2[0:npart, :], op=ALU.add),
                  r=[b_t1, b_t2], w=[b_dst])

        SY2 = SY[:].rearrange("p g q -> p (g q)")
        Xp2 = UtXp[:].rearrange("p g q -> p (g q)")
        w_ = K.wj[0:1, 0:1]
        v_ = K.wj[0:1, 1:2]

        def s5_front(pos):
            S.dma(lambda e: e.dma_start(out=UtXp[:], in_=K.ut_d[pos]), w=[b_UtXp])
            for g8 in range(4):
                pb_, b_pb = nxt("b", PB)
                for gg in range(8):
                    g = g8 * 8 + gg
                    S.pe(lambda e, g=g, gg=gg, pb_=pb_: e.transpose(out=pb_[:, gg * 128:(gg + 1) * 128], in_=UtXp[:, g, :],
                                                                    identity=K.id_b[:]), r=[b_UtXp, K.b_idb], w=[b_pb])
                src = pb_[:].rearrange("p (g q) -> p g q", g=8)
                if g8 % 2 == 0:
                    S.act(lambda e, g8=g8, src=src: e.activation(out=UT[:, g8 * 8:(g8 + 1) * 8, :], in_=src, func=AF.Copy),
                          r=[b_pb], w=[b_UT])
                else:
                    S.dve(lambda e, g8=g8, src=src: e.tensor_copy(out=UT[:, g8 * 8:(g8 + 1) * 8, :], in_=src), r=[b_pb], w=[b_UT])

        def s5_sums(p8, b_p8):
            def emit_pS(sl):
                pS, b_pS = nxt("f", PF)
                for gg in range(4):
                    g = sl * 4 + gg
                    S.pe(lambda e, g=g, gg=gg: e.matmul(pS[:, gg * 128:(gg + 1) * 128], lhsT=UT[:, g, :], rhs=K.BA5[:, g, :],
                                                        start=True, stop=True), r=[b_UT, K.b_BA5], w=[b_pS])
                return pS, b_pS
            cur = emit_pS(0)
            for sl in range(8):
                tb, b_tb = nxt("tab", tabs)
                S.dma(lambda e, sl=sl, tb=tb: e.dma_start(
                    out=tb[:, 0:2, :], in_=K.tab_d[0:2, :, sl * 512:(sl + 1) * 512].rearrange("t p q -> p t q")), w=[b_tb])
                nxt_ = emit_pS(sl + 1) if sl + 1 < 8 else None
                pS, b_pS = cur
                cscale(pS[:], b_pS, tb[:, 0, :], tb[:, 1, :], b_tb, SY2[:, sl * 512:(sl + 1) * 512], b_SY)
                S.pe(lambda e, sl=sl: e.matmul(p8[0:8, :], lhsT=ecol[:, sl * 8:(sl + 1) * 8], rhs=SY2[:, sl * 512:(sl + 1) * 512],
                                               start=(sl == 0), stop=(sl == 7)), r=[b_ecol, b_SY], w=[b_p8])
                cur = nxt_

        def block(r):
            A, Bp = 2 * r, 2 * r + 1
            S.dma(lambda e: e.dma_start(out=hTb[:], in_=K.hT_d[A]), w=[b_h])
            s5_front(Bp)
            p8, b_p8 = P8
            s5_sums(p8, b_p8)
            S.dve(lambda e, p8=p8: e.tensor_copy(out=totB8[:], in_=p8[0:8, :]), r=[b_p8], w=[b_totB])
            s5_front(A)
            if "p3a" in K.flags:
                return
            p8, b_p8 = P8
            s5_sums(p8, b_p8)
            (r1, b_r1), (r2, b_r2), (r3, b_r3), (r4, b_r4) = rows
            w8 = K.wj[0:8, 0:1]
            v8 = K.wj[0:8, 1:2]
            S.dve(lambda e, p8=p8: e.tensor_scalar(out=r1[:], in0=p8[0:8, :], scalar1=v8, scalar2=None, op0=ALU.mult),
                  r=[b_p8, K.b_wj], w=[b_r1])
            S.dve(lambda e: e.scalar_tensor_tensor(out=r1[:], in0=totB8[:], scalar=w8, in1=r1[:], op0=ALU.mult, op1=ALU.add),
                  r=[b_totB, b_r1, K.b_wj], w=[b_r1])
            S.dve(lambda e: e.tensor_scalar(out=r2[:], in0=totB8[:], scalar1=v8, scalar2=None, op0=ALU.mult),
                  r=[b_totB, K.b_wj], w=[b_r2])
            S.dve(lambda e, p8=p8: e.scalar_tensor_tensor(out=r2[:], in0=p8[0:8, :], scalar=w8, in1=r2[:], op0=ALU.mult, op1=ALU.add),
                  r=[b_p8, b_r2, K.b_wj], w=[b_r2])
            S.dve(lambda e: e.tensor_tensor(out=r1[:], in0=r1[:], in1=Xin8[:], op=ALU.add), r=[b_r1, b_Xin], w=[b_r1])
            cscale(r1[:], b_r1, e128[:, 0, :], e128[:, 1, :], b_e128, r3[:], b_r3)
            S.dve(lambda e: e.tensor_scalar(out=r4[:], in0=Xin8[:], scalar1=v8, scalar2=None, op0=ALU.mult),
                  r=[b_Xin, K.b_wj], w=[b_r4])
            S.dve(lambda e: e.scalar_tensor_tensor(out=xown8[:], in0=r3[:], scalar=w8, in1=r4[:], op0=ALU.mult, op1=ALU.add),
                  r=[b_r3, b_r4, K.b_wj], w=[b_xown])
            S.dve(lambda e: e.tensor_tensor(out=r2[:], in0=r2[:], in1=r3[:], op=ALU.add), r=[b_r2, b_r3], w=[b_r2])
            cscale(r2[:], b_r2, e128[:, 0, :], e128[:, 1, :], b_e128, Xin8[:], b_Xin)
            for sl in range(8):
                tb, b_tb = nxt("tab", tabs)
                S.dma(lambda e, sl=sl, tb=tb: e.dma_start(
                    out=tb[:, 2:4, :], in_=K.tab_d[2:4, :, sl * 512:(sl + 1) * 512].rearrange("t p q -> p t q")), w=[b_tb])
                pZ, b_pZ = nxt("f", PF)
                S.pe(lambda e, sl=sl, pZ=pZ: e.matmul(pZ[:], lhsT=K.tri_bs[:], rhs=SY2[:, sl * 512:(sl + 1) * 512],
                                                      start=True, stop=False), r=[K.b_trib, b_SY], w=[b_pZ])
                S.pe(lambda e, sl=sl, pZ=pZ: e.matmul(pZ[:], lhsT=erws[0:8, sl * 128:(sl + 1) * 128], rhs=xown8[0:8, :],
                                                      start=False, stop=True), r=[b_erws, b_xown], w=[b_pZ])
                cscale(pZ[:], b_pZ, tb[:, 2, :], tb[:, 3, :], b_tb, Xp2[:, sl * 512:(sl + 1) * 512], b_UtXp)
                if sl % 2 == 1 and "p3b" not in K.flags:
                    g8 = sl // 2
                    pb_, b_pb = nxt("b", PB)
                    for gg in range(8):
                        g = g8 * 8 + gg
                        S.pe(lambda e, g=g, gg=gg, pb_=pb_: e.transpose(out=pb_[:, gg * 128:(gg + 1) * 128], in_=UtXp[:, g, :],
                                                                        identity=K.id_b[:]), r=[b_UtXp, K.b_idb], w=[b_pb])
                    src = pb_[:].rearrange("p (g q) -> p g q", g=8)
                    if g8 % 2 == 0:
                        S.act(lambda e, g8=g8, src=src: e.activation(out=XpT[:, g8 * 8:(g8 + 1) * 8, :], in_=src, func=AF.Copy),
                              r=[b_pb], w=[b_XpT])
                    else:
                        S.dve(lambda e, g8=g8, src=src: e.tensor_copy(out=XpT[:, g8 * 8:(g8 + 1) * 8, :], in_=src), r=[b_pb], w=[b_XpT])
            if "p3b" in K.flags:
                return
            for sl in range(8):
                pY, b_pY = nxt("f", PF)
                for gg in range(4):
                    g = sl * 4 + gg
                    S.pe(lambda e, g=g, gg=gg, pY=pY: e.matmul(pY[:, gg * 128:(gg + 1) * 128], lhsT=UT[:, g, :], rhs=K.T5[:, g, :],
                                                               start=True, stop=False), r=[b_UT, K.b_T5], w=[b_pY])
                    S.pe(lambda e, g=g, gg=gg, pY=pY: e.matmul(pY[:, gg * 128:(gg + 1) * 128], lhsT=XpT[:, g, :], rhs=K.CA5[:, g, :],
                                                               start=False, stop=True), r=[b_XpT, K.b_CA5], w=[b_pY])
                xs, b_xs = nxt("t", tmp)
                x2, b_x2 = nxt("t", tmp)
                S.act(lambda e, pY=pY, xs=xs: e.activation(out=xs[:], in_=pY[:], func=AF.Copy), r=[b_pY], w=[b_xs])
                S.pool(lambda e, xs=xs, x2=x2: e.tensor_tensor(out=x2[:], in0=xs[:], in1=xs[:], op=ALU.mult), r=[b_xs], w=[b_x2])
                S.dve(lambda e, x2=x2: e.tensor_scalar(out=x2[:], in0=x2[:], scalar1=0.044715, scalar2=1.0, op0=ALU.mult, op1=ALU.add),
                      r=[b_x2], w=[b_x2])
                S.dve(lambda e, xs=xs, x2=x2: e.tensor_tensor(out=x2[:], in0=x2[:], in1=xs[:], op=ALU.mult), r=[b_x2, b_xs], w=[b_x2])
                S.act(lambda e, x2=x2: e.activation(out=x2[:], in_=x2[:], func=AF.Tanh, scale=GC), r=[b_x2], w=[b_x2])
                yg4 = SY2.rearrange("p (j g c) -> p j g c", j=8, g=32)
                for gg in range(4):
                    ygo = yg4[:, :, sl * 4 + gg, :]
                    S.dve(lambda e, ygo=ygo, xs=xs, x2=x2, gg=gg: e.scalar_tensor_tensor(
                        out=ygo, in0=x2[:, gg * 128:(gg + 1) * 128].rearrange("p (j c) -> p j c", j=8), scalar=1.0,
                        in1=xs[:, gg * 128:(gg + 1) * 128].rearrange("p (j c) -> p j c", j=8), op0=ALU.add, op1=ALU.mult),
                        r=[b_x2, b_xs], w=[b_SY])
            def emit_ybT():
                for fc in range(4):
                    pb_, b_pb = nxt("b", PB)
                    for j in range(8):
                        S.pe(lambda e, fc=fc, j=j, pb_=pb_: e.transpose(
                            out=pb_[:, j * 128:(j + 1) * 128], in_=SY2[:, j * 512 + fc * 128:j * 512 + (fc + 1) * 128],
                            identity=K.id_b[:]), r=[b_SY, K.b_idb], w=[b_pb])
                    if fc % 2 == 0:
                        S.act(lambda e, fc=fc, pb_=pb_: e.activation(out=ybT[:, fc, :], in_=pb_[:], func=AF.Copy), r=[b_pb], w=[b_ybT])
                    else:
                        S.dve(lambda e, fc=fc, pb_=pb_: e.tensor_copy(out=ybT[:, fc, :], in_=pb_[:]), r=[b_pb], w=[b_ybT])
                if r == 0 and "ybT" in K.dbg_out:
                    S.dma(lambda e: e.dma_start(out=K.dbg_out["ybT"], in_=ybT[:]), r=[b_ybT])
            if "p3c" in K.flags:
                emit_ybT()
                return
            def do_half(half):
                hs = slice(half * 512, (half + 1) * 512)
                S.dma(lambda e, r=r, hs=hs: e.dma_start(out=yaT[:], in_=K.yaT_d[r, :, :, hs]), w=[b_ya])
                ((wza, b_wza),) = wstage(["za"])
                for fc in range(4):
                    pz, b_pz = nxt("f", PF)
                    for kc in range(8):
                        S.pe(lambda e, fc=fc, kc=kc, pz=pz: e.matmul(pz[:], lhsT=wza[:, kc, fc * 128:(fc + 1) * 128], rhs=hTb[:, kc, hs],
                                                                     start=(kc == 0), stop=(kc == 7)), r=[b_wza, b_h], w=[b_pz])
                    th, b_th = nxt("h", thb)
                    t1, b_t1 = nxt("t", tmp)
                    S.act(lambda e, pz=pz, th=th: e.activation(out=th[:], in_=pz[:], func=AF.Tanh, scale=0.5), r=[b_pz], w=[b_th])
                    S.dve(lambda e, pz=pz, th=th, t1=t1: e.scalar_tensor_tensor(out=t1[:], in0=th[:], scalar=1.0, in1=pz[:],
                                                                                op0=ALU.add, op1=ALU.mult), r=[b_th, b_pz], w=[b_t1])
                    S.dve(lambda e, fc=fc, t1=t1: e.tensor_tensor(out=yag[:, fc, :], in0=t1[:], in1=yaT[:, fc, :], op=ALU.mult),
                          r=[b_t1, b_ya], w=[b_XpT])
                if half == 0:
                    emit_ybT()
                ((wzb, b_wzb),) = wstage(["zb"])
                for fc in range(4):
                    pg, b_pg = nxt("f", PF)
                    for kc in range(4):
                        S.pe(lambda e, fc=fc, kc=kc, pg=pg: e.matmul(pg[:], lhsT=wgl[:, kc, fc * 128:(fc + 1) * 128], rhs=ybT[:, kc, hs],
                                                                     start=(kc == 0), stop=(kc == 3)), r=[b_wgl, b_ybT], w=[b_pg])
                    pz, b_pz = nxt("f", PF)
                    for kc in range(8):
                        S.pe(lambda e, fc=fc, kc=kc, pz=pz: e.matmul(pz[:], lhsT=wzb[:, kc, fc * 128:(fc + 1) * 128], rhs=hTb[:, kc, hs],
                                                                     start=(kc == 0), stop=(kc == 7)), r=[b_wzb, b_h], w=[b_pz])
                    thg, b_thg = nxt("h", thb)
                    thz, b_thz = nxt("h", thb)
                    t1, b_t1 = nxt("t", tmp)
                    t2, b_t2 = nxt("t", tmp)
                    S.act(lambda e, fc=fc, pg=pg, thg=thg: e.activation(out=thg[:], in_=pg[:], func=AF.Tanh, scale=0.25,
                                                                        bias=bgl[:, fc:fc + 1]), r=[b_pg, b_bgl], w=[b_thg])
                    S.act(lambda e, pz=pz, thz=thz: e.activation(out=thz[:], in_=pz[:], func=AF.Tanh, scale=0.5), r=[b_pz], w=[b_thz])
                    S.dve(lambda e, pz=pz, thz=thz, t1=t1: e.scalar_tensor_tensor(out=t1[:], in0=thz[:], scalar=1.0, in1=pz[:],
                                                                                  op0=ALU.add, op1=ALU.mult), r=[b_thz, b_pz], w=[b_t1])
                    S.pool(lambda e, thg=thg, t2=t2: e.tensor_scalar(out=t2[:], in0=thg[:], scalar1=1.0, scalar2=None, op0=ALU.add),
                           r=[b_thg], w=[b_t2])
                    S.dve(lambda e, fc=fc, t2=t2: e.tensor_tensor(out=t2[:], in0=t2[:], in1=ybT[:, fc, hs], op=ALU.mult),
                          r=[b_t2, b_ybT], w=[b_t2])
                    S.dve(lambda e, fc=fc, t1=t1, t2=t2: e.tensor_tensor(out=ybg[:, fc, :], in0=t1[:], in1=t2[:], op=ALU.mult),
                          r=[b_t1, b_t2], w=[b_XpT])
                wg = {}

                def emit_G(fc):
                    gh, f4 = fc // 4, fc % 4
                    if f4 == 0:
                        wg[gh] = tuple(wstage(["ga%d" % gh, "gb%d" % gh]))
                    (wga, b_wga), (wgb, b_wgb) = wg[gh]
                    pga, b_pga = nxt("f", PF)
                    for kc in range(8):
                        S.pe(lambda e, kc=kc: e.matmul(pga[:], lhsT=wga[:, kc, f4 * 128:(f4 + 1) * 128], rhs=hTb[:, kc, hs],
                                                       start=(kc == 0), stop=(kc == 7)), r=[b_wga, b_h], w=[b_pga])
                    pgb, b_pgb = nxt("f", PF)
                    for kc in range(8):
                        S.pe(lambda e, kc=kc: e.matmul(pgb[:], lhsT=wgb[:, kc, f4 * 128:(f4 + 1) * 128], rhs=hTb[:, kc, hs],
                                                       start=(kc == 0), stop=(kc == 7)), r=[b_wgb, b_h], w=[b_pgb])
                    tha, b_tha = nxt("h", thb)
                    thb_, b_thb = nxt("h", thb)
                    S.act(lambda e: e.activation(out=tha[:], in_=pga[:], func=AF.Tanh, scale=0.5), r=[b_pga], w=[b_tha])
                    S.act(lambda e: e.activation(out=thb_[:], in_=pgb[:], func=AF.Tanh, scale=0.5), r=[b_pgb], w=[b_thb])
                    return tha, b_tha, thb_, b_thb

                def emit_U(fc, ths):
                    tha, b_tha, thb_, b_thb = ths
                    pua, b_pua = nxt("f", PF)
                    for kc in range(4):
                        S.pe(lambda e, kc=kc: e.matmul(pua[:], lhsT=wua4[:, kc, fc * 128:(fc + 1) * 128], rhs=yag[:, kc, :],
                                                       start=(kc == 0), stop=(kc == 3)), r=[b_wua, b_XpT], w=[b_pua])
                    pub, b_pub = nxt("f", PF)
                    for kc in range(4):
                        S.pe(lambda e, kc=kc: e.matmul(pub[:], lhsT=wub4[:, kc, fc * 128:(fc + 1) * 128], rhs=ybg[:, kc, :],
                                                       start=(kc == 0), stop=(kc == 3)), r=[b_wub, b_XpT], w=[b_pub])
                    m1, b_m1 = nxt("t", tmp)
                    m2, b_m2 = nxt("t", tmp)
                    S.dve(lambda e: e.scalar_tensor_tensor(out=m1[:], in0=tha[:], scalar=1.0, in1=pua[:],
                                                           op0=ALU.add, op1=ALU.mult), r=[b_tha, b_pua], w=[b_m1])
                    S.dve(lambda e: e.scalar_tensor_tensor(out=m2[:], in0=thb_[:], scalar=1.0, in1=pub[:],
                                                           op0=ALU.add, op1=ALU.mult), r=[b_thb, b_pub], w=[b_m2])
                    S.pool(lambda e: e.tensor_tensor(out=merged[:, fc, :], in0=m1[:], in1=m2[:], op=ALU.add),
                           r=[b_m1, b_m2], w=[b_UT])

                ths = {0: emit_G(0)}
                for fc in range(8):
                    if fc + 1 < 8:
                        ths[fc + 1] = emit_G(fc + 1)
                    emit_U(fc, ths.pop(fc))
                if "p3d" in K.flags:
                    return
                if r == 0 and half == 0:
                    if "merged" in K.dbg_out:
                        S.dma(lambda e: e.dma_start(out=K.dbg_out["merged"], in_=merged), r=[b_UT])
                    if "yag" in K.dbg_out:
                        S.dma(lambda e: e.dma_start(out=K.dbg_out["yag"], in_=yag), r=[b_XpT])
                    if "ybg" in K.dbg_out:
                        S.dma(lambda e: e.dma_start(out=K.dbg_out["ybg"], in_=ybg), r=[b_XpT])
                (wo0, b_wo0), (wo1, b_wo1) = wstage(["o0", "o1"])
                for jj in range(4):
                    j = half * 4 + jj
                    x_t, b_x = nxt("x", xj)
                    S.dma(lambda e, A=A, j=j, x_t=x_t: e.dma_start(
                        out=x_t[:], in_=K.x[A * 1024:(A + 1) * 1024, :].rearrange("(n j) d -> n j d", j=8)[:, j, :]), w=[b_x])
                    for fh in range(2):
                        wo, b_wo = (wo0, b_wo0) if fh == 0 else (wo1, b_wo1)
                        po, b_po = nxt("f", PF)
                        for kc in range(8):
                            S.pe(lambda e, jj=jj, kc=kc, po=po, wo=wo: e.matmul(po[:], lhsT=merged[:, kc, jj * 128:(jj + 1) * 128], rhs=wo[:, kc, :],
                                                                               start=(kc == 0), stop=(kc == 7)), r=[b_UT, b_wo], w=[b_po])
                        fs = slice(fh * 512, (fh + 1) * 512)
                        S.dve(lambda e, po=po, x_t=x_t, fs=fs: e.tensor_tensor(out=x_t[:, fs], in0=x_t[:, fs], in1=po[:], op=ALU.add),
                              r=[b_po, b_x], w=[b_x])
                    if "p3e" in K.flags:
                        continue
                    S.act(lambda e, x_t=x_t: e.activation(out=junk[:], in_=x_t[:], func=AF.Square, accum_out=ss2[:, 0:1]),
                          r=[b_x], w=[b_junk, b_ss2])
                    S.act(lambda e: e.activation(out=ss2[:, 1:2], in_=ss2[:, 0:1], func=AF.Sqrt, scale=1.0 / D, bias=K.cst[:, 0:1]),
                          r=[b_ss2, K.b_cst], w=[b_ss2])
                    S.dve(lambda e: e.reciprocal(out=ss2[:, 1:2], in_=ss2[:, 1:2]), r=[b_ss2], w=[b_ss2])
                    S.dve(lambda e, x_t=x_t: e.scalar_tensor_tensor(out=x_t[:], in0=x_t[:], scalar=ss2[:, 1:2], in1=gfin[:],
                                                                    op0=ALU.mult, op1=ALU.mult), r=[b_x, b_ss2, b_gfin], w=[b_x])
                    if "p3f" in K.flags:
                        continue
                    S.dma(lambda e, r=r, j=j, x_t=x_t: e.dma_start(
                        out=K.out[r * 1024:(r + 1) * 1024, :].rearrange("(n j) d -> n j d", j=8)[:, j, :], in_=x_t[:]),
                        r=[b_x], q="aq")
            for half in range(2):
                do_half(half)

        for r in range(NB // 2):
            block(r)
        S.flush()


def host_consts():
    bf = ml_dtypes.bfloat16
    n = np.arange(128)
    tri = (n[:, None] < n[None, :]).astype(np.float32)
    mle = (n[:, None] <= n[None, :]).astype(np.float32)
    mlt = tri
    cm5 = np.zeros((128, 5, 512), np.float32)
    for a in range(5):
        for i in range(4):
            cm5[:, a, i * 128:(i + 1) * 128] = -30000.0 * (1.0 - (mlt if i < a else mle))
    ic = np.arange(128) // 16
    tmask = (ic[None, :] >= ic[:, None]).astype(np.float32)
    nvec = np.stack([n, n + 1, n + 2, n + 3], axis=1).astype(np.float32)
    ecol = np.zeros((128, 8, 8), np.float32)
    erow = np.zeros((8, 8, 128), np.float32)
    for sl in range(8):
        ecol[:, sl, sl] = 1.0
        erow[sl, sl, :] = 1.0
    return {"ecol": ecol.reshape(128, 64).astype(bf), "erowsel": erow.reshape(8, 1024).astype(bf),
            "ident_b": np.eye(128, dtype=np.float32).astype(bf), "ident_f": np.eye(128, dtype=np.float32),
            "tri_f": tri, "cm5": cm5.astype(bf), "tmask": tmask, "nvec": nvec}


def core_inputs(inp, b, NB=8, jc=0):
    T = NB * 1024
    f = np.float32
    col = lambda v, k: np.ascontiguousarray(np.asarray(v, f).reshape(k, 128).T)
    rep = lambda v: np.ascontiguousarray(np.broadcast_to(np.asarray(v, f)[None, :], (128, v.shape[-1])))
    b_ada = np.asarray(inp["b_ada"][0], f)
    a_re, a_im = np.asarray(inp["a_re"][0], f), np.asarray(inp["a_im"][0], f)
    lam = np.stack([np.stack([a_re, a_re], 1), np.stack([a_im, a_im], 1)], 1)
    c_re, c_im = np.asarray(inp["c_re"][0], f).reshape(512, 64), np.asarray(inp["c_im"][0], f).reshape(512, 64)
    s5c = np.stack([np.stack([c_re, c_re], 1), np.stack([c_im, c_im], 1)], 1)
    b_re, b_im = np.asarray(inp["b_re"][0], f), np.asarray(inp["b_im"][0], f)
    s5b = np.concatenate([b_re.transpose(1, 0, 2), b_im.transpose(1, 0, 2)], 0)
    d = np.asarray(inp["d_skip"][0], f).reshape(32, 16)
    s5d = np.tile(d.T[None, :, :], (8, 1, 1)).reshape(128, 32)
    m = {
        "x": np.ascontiguousarray(np.asarray(inp["x"][b, :T], f).reshape(NB // 2, 2, 1024, D)[:, ::(-1 if jc else 1)].reshape(T, D)),
        "wj": np.ascontiguousarray(np.broadcast_to(np.array([float(jc), 1.0 - jc], f)[None, :], (128, 2))),
        "mB": np.full((128, 512), 0.0 if jc else -30000.0, f).astype(ml_dtypes.bfloat16),
        "cT": col(inp["c"][b], 8),
        "w_ada": np.ascontiguousarray(np.asarray(inp["w_ada"][0], f)),
        "bada_col": col(b_ada, 24),
        "bgate_rep": rep(b_ada[2 * D:]),
        "gn_col": col(inp["g_norm"][0], 8),
        "w_in": np.ascontiguousarray(np.asarray(inp["w_in"][0], f)),
        "bf_rep": np.ascontiguousarray(np.tile(np.asarray(inp["b_f"][0], f), 8)[None, :].repeat(128, 0)),
        "w_glu": np.ascontiguousarray(np.asarray(inp["w_glu"][0], f)),
        "bglu_col": col(inp["b_glu"][0], 4),
        "w_up_a": np.ascontiguousarray(np.asarray(inp["w_up_a"][0], f)),
        "w_up_b": np.ascontiguousarray(np.asarray(inp["w_up_b"][0], f)),
        "w_out": np.ascontiguousarray(np.asarray(inp["w_out"][0], f)),
        "gfin_rep": rep(np.asarray(inp["g_final"], f)),
        "s5_lam": np.ascontiguousarray(lam),
        "s5_ldt": rep(np.asarray(inp["log_dt"][0], f)),
        "s5_b": np.ascontiguousarray(s5b),
        "s5_c": np.ascontiguousarray(s5c),
        "s5_d": np.ascontiguousarray(s5d),
        "s5_bs": np.ascontiguousarray(np.concatenate([s5b[64:], s5b[:64]], 0)),
        "s5_zrep": np.ascontiguousarray(np.broadcast_to(np.stack([
            a_re.reshape(-1), a_im.reshape(-1),
            np.repeat(np.asarray(inp["log_dt"][0], f), 64)], 0)[None], (128, 3, 2048))),
    }
    m.update(host_consts())
    return m


_NC_CACHE = {}


def kernel(**inputs):
    NB = 8
    if NB not in _NC_CACHE:
        _NC_CACHE[NB] = build(NB)
    nc = _NC_CACHE[NB]
    in_maps = [core_inputs(inputs, c // 2, NB, c % 2) for c in range(NCORES)]
    res = run_bass_kernel_spmd(nc, in_maps, core_ids=list(range(NCORES)))
    out = np.empty((4, NB // 2, 2, 1024, D), np.float32)
    for c in range(NCORES):
        out[c // 2, :, c % 2] = np.asarray(res.results[c]["out"], np.float32).reshape(NB // 2, 1024, D)
    return out.reshape(4, NB * 1024, D)
```

```python
import math
import numpy as np
import ml_dtypes
from contextlib import ExitStack
import concourse.bass as bass
import concourse.mybir as mybir
from concourse.bass_utils import run_bass_kernel_spmd

F32 = mybir.dt.float32
BF16 = mybir.dt.bfloat16
AF = mybir.ActivationFunctionType
ALU = mybir.AluOpType

D = 1024
EPS = 1e-6
NCORES = 8
COMPUTE = ("pe", "act", "dve", "pool")
QUEUES = ("sp", "gq", "aq")
RING = 8


class Buf:
    __slots__ = ("name", "lw", "rd")

    def __init__(self, name):
        self.name = name
        self.lw = None
        self.rd = []


class Op:
    __slots__ = ("eng", "fn", "deps", "sig", "cnt", "ring", "seg")

    def __init__(self, eng, fn, seg):
        self.eng = eng
        self.fn = fn
        self.deps = []
        self.sig = False
        self.cnt = 0
        self.ring = None
        self.seg = seg


class Sched:
    def __init__(self, nc, sems):
        self.nc = nc
        self.sems = sems
        self.ops = []
        self.seg = 0
        self.eng_obj = {"pe": nc.tensor, "act": nc.scalar, "dve": nc.vector,
                        "pool": nc.gpsimd, "sp": nc.sync, "gq": nc.gpsimd, "aq": nc.scalar}
        self.cnt = {e: 0 for e in COMPUTE}
        self.qcnt = {q: 0 for q in QUEUES}
        self.waited = {}
        self.nops = 0

    def buf(self, name):
        return Buf(name)

    def add(self, eng, fn, r=(), w=()):
        op = Op(eng, fn, self.seg)
        seg = self.seg
        deps = op.deps
        for b in r:
            p = b.lw
            if p is not None and p.seg == seg:
                deps.append(p)
        for b in w:
            p = b.lw
            if p is not None and p.seg == seg:
                deps.append(p)
            for x in b.rd:
                if x.seg == seg:
                    deps.append(x)
        for b in r:
            b.rd.append(op)
        for b in w:
            b.lw = op
            b.rd = []
        self.ops.append(op)
        return op

    def pe(self, fn, r=(), w=()):
        return self.add("pe", fn, r, w)

    def act(self, fn, r=(), w=()):
        return self.add("act", fn, r, w)

    def dve(self, fn, r=(), w=()):
        return self.add("dve", fn, r, w)

    def pool(self, fn, r=(), w=()):
        return self.add("pool", fn, r, w)

    def dma(self, fn, r=(), w=(), q="sp"):
        return self.add(q, fn, r, w)

    @staticmethod
    def _stream(e):
        return "pool" if e == "gq" else ("act" if e == "aq" else e)

    def _wait(self, st, eng, key, val):
        wk = (st, key)
        if self.waited.get(wk, 0) >= val:
            return
        self.waited[wk] = val
        sem = self.sems[key[1]] if key[0] == "c" else self.sems[(key[1], key[2])]
        eng.wait_ge(sem, val)

    def flush(self):
        ops = self.ops
        last = {}
        for op in ops:
            if op.eng in COMPUTE:
                last[op.eng] = op
            for p in op.deps:
                if p.eng == "pe" and op.eng == "pe":
                    continue
                p.sig = True
        for op in last.values():
            op.sig = True
        for op in ops:
            if op.eng in COMPUTE:
                if op.sig:
                    self.cnt[op.eng] += 1
                    op.cnt = self.cnt[op.eng]
            else:
                k = self.qcnt[op.eng]
                self.qcnt[op.eng] += 1
                op.ring = (k % RING, 16 * (k // RING + 1), k)
        for op in ops:
            eng = self.eng_obj[op.eng]
            st = self._stream(op.eng)
            need = {}
            for p in op.deps:
                if p is op or (p.eng == "pe" and op.eng == "pe"):
                    continue
                if p.eng in COMPUTE:
                    key = ("c", p.eng)
                    val = p.cnt
                else:
                    key = ("q", p.eng, p.ring[0])
                    val = p.ring[1]
                if need.get(key, 0) < val:
                    need[key] = val
            if op.eng in QUEUES:
                r, v, k = op.ring
                if k >= RING:
                    key = ("q", op.eng, r)
                    if need.get(key, 0) < v - 16:
                        need[key] = v - 16
            for key, val in need.items():
                self._wait(st, eng, key, val)
            ins = op.fn(eng)
            if op.eng in COMPUTE:
                if op.sig:
                    ins.then_inc(self.sems[op.eng], 1)
            else:
                ins.then_inc(self.sems[(op.eng, op.ring[0])], 16)
        self.nops += len(ops)
        for st in ("pe", "act", "dve", "pool", "sp"):
            eng = self.eng_obj[st]
            for e in COMPUTE:
                if e != st and self.cnt[e] > 0:
                    self._wait(st, eng, ("c", e), self.cnt[e])
            for q in QUEUES:
                k = self.qcnt[q]
                for r in range(RING):
                    n = (k - r + RING - 1) // RING if k > r else 0
                    if n > 0:
                        self._wait(st, eng, ("q", q, r), 16 * n)
        self.ops = []
        self.seg += 1


C_Q, C_K, C_V, C_F, C_ZA, C_U, C_ZB, C_GA, C_GB = 0, 512, 1024, 1536, 1544, 2056, 2568, 3080, 4104
PW = 5128


class Ctx:
    pass


def build(NB=8, dbg=None, flags=()):
    T = NB * 1024
    nc = bass.Bass("TRN2", target_bir_lowering=False)
    K = Ctx()
    K.nc = nc
    K.NB = NB
    K.T = T
    K.dbg = dbg or ()
    K.flags = flags

    def din(name, shape, dt=F32):
        return nc.dram_tensor(name, list(shape), dt, kind="ExternalInput").ap()

    def dscr(name, shape, dt):
        return nc.dram_tensor(name, list(shape), dt, kind="Internal").ap()

    K.x = din("x", [T, D])
    K.cT = din("cT", [128, 8])
    K.w_ada = din("w_ada", [D, 3 * D])
    K.bada_col = din("bada_col", [128, 24])
    K.bgate_rep = din("bgate_rep", [128, D])
    K.gn_col = din("gn_col", [128, 8])
    K.w_in = din("w_in", [D, PW])
    K.bf_rep = din("bf_rep", [128, 64])
    K.w_glu = din("w_glu", [512, 512])
    K.bglu_col = din("bglu_col", [128, 4])
    K.w_up_a = din("w_up_a", [512, D])
    K.w_up_b = din("w_up_b", [512, D])
    K.w_out = din("w_out", [D, D])
    K.gfin_rep = din("gfin_rep", [128, D])
    K.s5_lam = din("s5_lam", [32, 2, 2, 64])
    K.s5_ldt = din("s5_ldt", [128, 32])
    K.s5_b = din("s5_b", [128, 32, 16])
    K.s5_c = din("s5_c", [512, 2, 2, 64])
    K.s5_d = din("s5_d", [128, 32])
    K.s5_bs = din("s5_bs", [128, 32, 16])
    K.s5_zrep = din("s5_zrep", [128, 3, 2048])
    K.ident_b = din("ident_b", [128, 128], BF16)
    K.ident_f = din("ident_f", [128, 128])
    K.tri_f = din("tri_f", [128, 128])
    K.cm5 = din("cm5", [128, 5, 512], BF16)
    K.tmask = din("tmask", [128, 128])
    K.nvec = din("nvec", [128, 4])
    K.ecol_in = din("ecol", [128, 64], BF16)
    K.erow_in = din("erowsel", [8, 1024], BF16)
    K.wj_in = din("wj", [128, 2])
    K.mB_in = din("mB", [128, 512], BF16)
    K.out = nc.dram_tensor("out", [T // 2, D], F32, kind="ExternalOutput").ap()
    K.hT_d = dscr("hT_d", [NB, 128, 8, 1024], BF16)
    K.yaT_d = dscr("yaT_d", [NB // 2, 128, 4, 1024], BF16)
    K.c_d = dscr("c_d", [NB // 2, 8, 1024], BF16)
    K.wb_in = dscr("wb_in", [128, 8, PW], BF16)
    K.wb_glu = dscr("wb_glu", [128, 4, 512], BF16)
    K.wb_upa = dscr("wb_upa", [128, 4, D], BF16)
    K.wb_upb = dscr("wb_upb", [128, 4, D], BF16)
    K.wb_out = dscr("wb_out", [128, 8, D], BF16)
    K.tab_d = dscr("tab_d", [4, 128, 4096], BF16)
    K.tab_e = dscr("tab_e", [2, 4096], F32)
    K.ut_d = dscr("ut_d", [NB, 128, 32, 128], BF16)
    K.dbg_out = {}
    for name, shape, dt in (dbg or ()):
        K.dbg_out[name] = nc.dram_tensor("dbg_" + name, list(shape), dt, kind="ExternalOutput").ap()

    with ExitStack() as es:
        sems = {}
        for e in COMPUTE:
            sems[e] = es.enter_context(nc.semaphore("s_" + e))
        for q in QUEUES:
            for r in range(RING):
                sems[(q, r)] = es.enter_context(nc.semaphore(f"s_{q}{r}"))
        S = Sched(nc, sems)
        K.S = S
        K.es = es

        def sb(name, shape, dt, stack=es):
            t = stack.enter_context(nc.sbuf_tensor(name, list(shape), dt))
            return t, S.buf(name)

        def ps(name, shape, dt, stack=es):
            t = stack.enter_context(nc.psum_tensor(name, list(shape), dt))
            return t, S.buf(name)
        K.sb = sb
        K.ps = ps

        K.cst, K.b_cst = sb("cst", [128, 8], F32)
        K.id_b, K.b_idb = sb("id_b", [128, 128], BF16)
        K.id_f, K.b_idf = sb("id_f", [128, 128], F32)
        K.tri_fs, K.b_trif = sb("tri_fs", [128, 128], F32)
        K.tri_bs, K.b_trib = sb("tri_bs", [128, 128], BF16)
        K.ones_f, K.b_onesf = sb("ones_f", [128, 128], F32)
        K.ones_b, K.b_onesb = sb("ones_b", [128, 128], BF16)
        K.modc, K.b_modc = sb("modc", [128, 24], F32)
        K.acol, K.b_acol = sb("acol", [128, 8], F32)
        K.gate_bc, K.b_gate = sb("gate_bc", [128, D], F32)
        K.G, K.b_G = sb("G", [128, NB * 8, 8], F32)
        K.Gend, K.b_Gend = sb("Gend", [128, NB, 8], F32)
        K.Gcar, K.b_Gcar = sb("Gcar", [128, 8], F32)
        K.wj, K.b_wj = sb("wj_s", [128, 2], F32)
        K.T5, K.b_T5 = sb("T5", [128, 32, 128], BF16)
        K.BA5, K.b_BA5 = sb("BA5", [128, 32, 128], BF16)
        K.CA5, K.b_CA5 = sb("CA5", [128, 32, 128], BF16)

        S.dve(lambda e: e.memset(K.cst[:, 0:1], EPS), w=[K.b_cst])
        S.dve(lambda e: e.memset(K.cst[:, 1:2], 1.0), w=[K.b_cst])
        S.dve(lambda e: e.memset(K.cst[:, 2:3], 0.0), w=[K.b_cst])
        S.dve(lambda e: e.memset(K.ones_f[:], 1.0), w=[K.b_onesf])
        S.dve(lambda e: e.memset(K.ones_b[:], 1.0), w=[K.b_onesb])
        S.dve(lambda e: e.memset(K.Gcar[:], 0.0), w=[K.b_Gcar])
        S.dma(lambda e: e.dma_start(out=K.wj[:], in_=K.wj_in), w=[K.b_wj])
        S.dma(lambda e: e.dma_start(out=K.id_b[:], in_=K.ident_b), w=[K.b_idb])
        S.dma(lambda e: e.dma_start(out=K.id_f[:], in_=K.ident_f), w=[K.b_idf])
        S.dma(lambda e: e.dma_start(out=K.tri_fs[:], in_=K.tri_f), w=[K.b_trif])
        S.dve(lambda e: e.tensor_copy(out=K.tri_bs[:], in_=K.tri_fs[:]), r=[K.b_trif], w=[K.b_trib])

        S.flush()
        stage_setup(K)
        if "skip3" not in K.flags or "s5tab" in K.flags:
            s5_setup_b(K)
        if "no12" not in K.flags:
            phase1(K)
            if "no2" not in K.flags:
                phase2(K)
        if "skip3" not in K.flags:
            phase3(K)
    return nc


def stage_setup(K):
    nc, S = K.nc, K.S
    with ExitStack() as st:
        def sb(name, shape, dt):
            return K.sb(name, shape, dt, st)

        def ps(name, shape, dt):
            return K.ps(name, shape, dt, st)
        cT_s, b_cT = sb("cT_s", [128, 8], F32)
        bada_s, b_bada = sb("bada_s", [128, 24], F32)
        gn_s, b_gn = sb("gn_s", [128, 8], F32)
        bg_s, b_bg = sb("bg_s", [128, D], F32)
        cTrep, b_cTrep = sb("cTrep", [128, 8, 128], F32)
        w32 = [sb(f"w32_{i}", [128, 8, 512], F32) for i in range(2)]
        w16 = [sb(f"w16_{i}", [128, 8, 512], BF16) for i in range(2)]
        pmod, b_pmod = ps("pmod", [128, 24], F32)
        pgate = [ps(f"pgate{i}", [128, 512], F32) for i in range(2)]

        S.dma(lambda e: e.dma_start(out=cT_s[:], in_=K.cT), w=[b_cT])
        S.dma(lambda e: e.dma_start(out=bada_s[:], in_=K.bada_col), w=[b_bada])
        S.dma(lambda e: e.dma_start(out=gn_s[:], in_=K.gn_col), w=[b_gn])
        S.dma(lambda e: e.dma_start(out=bg_s[:], in_=K.bgate_rep), w=[b_bg])
        for kc in range(8):
            S.dve(lambda e, kc=kc: e.tensor_copy(out=cTrep[:, kc, :],
                                                 in_=cT_s[:, kc:kc + 1].to_broadcast([128, 128])),
                  r=[b_cT], w=[b_cTrep])
        for piece in range(6):
            wt, b_wt = w32[piece % 2]
            S.dma(lambda e, piece=piece, wt=wt: e.dma_start(
                out=wt[:], in_=K.w_ada[:, piece * 512:(piece + 1) * 512].rearrange("(kc p) n -> p kc n", p=128)),
                w=[b_wt])
            if piece < 4:
                for fc in range(4):
                    col = piece * 4 + fc
                    for kc in range(8):
                        S.pe(lambda e, fc=fc, kc=kc, col=col, wt=wt: e.matmul(
                            pmod[:, col:col + 1], lhsT=wt[:, kc, fc * 128:(fc + 1) * 128],
                            rhs=cT_s[:, kc:kc + 1], start=(kc == 0), stop=(kc == 7)),
                            r=[b_wt, b_cT], w=[b_pmod])
            else:
                pg, b_pg = pgate[piece - 4]
                for kc in range(8):
                    S.pe(lambda e, kc=kc, wt=wt, pg=pg: e.matmul(
                        pg[:], lhsT=cTrep[:, kc, :], rhs=wt[:, kc, :], start=(kc == 0), stop=(kc == 7)),
                        r=[b_wt, b_cTrep], w=[b_pg])
                half = piece - 4
                S.dve(lambda e, pg=pg, half=half: e.tensor_tensor(
                    out=K.gate_bc[:, half * 512:(half + 1) * 512], in0=pg[:],
                    in1=bg_s[:, half * 512:(half + 1) * 512], op=ALU.add),
                    r=[b_pg, b_bg], w=[K.b_gate])
        S.dve(lambda e: e.tensor_scalar(out=K.gate_bc[:], in0=K.gate_bc[:], scalar1=0.0625, scalar2=None,
                                        op0=ALU.mult), r=[K.b_gate], w=[K.b_gate])
        S.dve(lambda e: e.tensor_tensor(out=K.modc[:, 0:16], in0=pmod[:, 0:16], in1=bada_s[:, 0:16], op=ALU.add),
              r=[b_pmod, b_bada], w=[K.b_modc])
        S.dve(lambda e: e.scalar_tensor_tensor(out=K.acol[:], in0=K.modc[:, 8:16], scalar=1.0, in1=gn_s[:],
                                               op0=ALU.add, op1=ALU.mult), r=[K.b_modc, b_gn], w=[K.b_acol])

        if "skip3" not in K.flags or "s5tab" in K.flags:
            s5_setup_a(K, st)
        jobs = []
        c0 = 0
        while c0 < PW:
            c1 = min(c0 + 512, PW)
            jobs.append((K.w_in[:, c0:c1].rearrange("(kc p) n -> p kc n", p=128), K.wb_in[:, :, c0:c1], 8, c1 - c0))
            c0 = c1
        jobs.append((K.w_glu.rearrange("(kc p) n -> p kc n", p=128), K.wb_glu, 4, 512))
        for h in range(2):
            jobs.append((K.w_up_a[:, h * 512:(h + 1) * 512].rearrange("(kc p) n -> p kc n", p=128),
                         K.wb_upa[:, :, h * 512:(h + 1) * 512], 4, 512))
            jobs.append((K.w_up_b[:, h * 512:(h + 1) * 512].rearrange("(kc p) n -> p kc n", p=128),
                         K.wb_upb[:, :, h * 512:(h + 1) * 512], 4, 512))
            jobs.append((K.w_out[:, h * 512:(h + 1) * 512].rearrange("(kc p) n -> p kc n", p=128),
                         K.wb_out[:, :, h * 512:(h + 1) * 512], 8, 512, h))
        for i, job in enumerate(jobs):
            src, dst, nk, w = job[:4]
            wt, b_wt = w32[i % 2]
            wo, b_wo = w16[i % 2]
            S.dma(lambda e, src=src, wt=wt, nk=nk, w=w: e.dma_start(out=wt[:, 0:nk, 0:w], in_=src), w=[b_wt])
            if len(job) == 5:
                gh = job[4]
                S.dve(lambda e, wt=wt, gh=gh: e.tensor_tensor(
                    out=wt[:], in0=wt[:], in1=K.gate_bc[:, gh * 512:(gh + 1) * 512].unsqueeze(1).to_broadcast([128, 8, 512]),
                    op=ALU.mult), r=[b_wt, K.b_gate], w=[b_wt])
            S.act(lambda e, wt=wt, wo=wo, nk=nk, w=w: e.activation(out=wo[:, 0:nk, 0:w], in_=wt[:, 0:nk, 0:w], func=AF.Copy),
                  r=[b_wt], w=[b_wo])
            S.dma(lambda e, dst=dst, wo=wo, nk=nk, w=w: e.dma_start(out=dst, in_=wo[:, 0:nk, 0:w]), r=[b_wo], q="gq")
        S.flush()


PI = math.pi


def sin_eval(K, out_ap, x_ap, scr, r, w, shift=0.0):
    S = K.S
    t, ti, tf, b_s = scr
    lim = 3.14159
    S.dve(lambda e: e.tensor_scalar(out=t, in0=x_ap, scalar1=1.0 / (2 * PI), scalar2=64.5 + shift / (2 * PI),
                                    op0=ALU.mult, op1=ALU.add), r=r, w=[b_s])
    S.dve(lambda e: e.tensor_copy(out=ti, in_=t), r=[b_s], w=[b_s])
    S.dve(lambda e: e.tensor_copy(out=tf, in_=ti), r=[b_s], w=[b_s])
    S.dve(lambda e: e.tensor_tensor(out=t, in0=t, in1=tf, op=ALU.subtract), r=[b_s], w=[b_s])
    S.dve(lambda e: e.tensor_scalar(out=t, in0=t, scalar1=0.0, scalar2=None, op0=ALU.is_lt), r=[b_s], w=[b_s])
    S.dve(lambda e: e.tensor_tensor(out=tf, in0=tf, in1=t, op=ALU.subtract), r=[b_s], w=[b_s])
    S.dve(lambda e: e.tensor_scalar(out=tf, in0=tf, scalar1=-2 * PI, scalar2=128 * PI + shift,
                                    op0=ALU.mult, op1=ALU.add), r=[b_s], w=[b_s])
    S.dve(lambda e: e.tensor_tensor(out=t, in0=x_ap, in1=tf, op=ALU.add), r=list(r) + [b_s], w=[b_s])
    S.dve(lambda e: e.tensor_scalar(out=t, in0=t, scalar1=lim, scalar2=-lim, op0=ALU.min, op1=ALU.max),
          r=[b_s], w=[b_s])
    S.act(lambda e: e.activation(out=out_ap, in_=t, func=AF.Sin), r=[b_s], w=w)


def s5_setup_a(K, st):
    nc, S = K.nc, K.S
    I32 = mybir.dt.int32
    if True:
        def sb(name, shape, dt):
            return K.sb(name, shape, dt, st)

        def ps(name, shape, dt):
            return K.ps(name, shape, dt, st)
        lam_in, b_lam = sb("s5_lam_in", [32, 256], F32)
        ar, b_ar = sb("s5_ar", [128, 32], F32)
        ai, b_ai = sb("s5_ai", [128, 32], F32)
        dt_, b_dt = sb("s5_dt", [128, 32], F32)
        zr, b_zr = sb("s5_zr", [128, 32], F32)
        zi, b_zi = sb("s5_zi", [128, 32], F32)
        sm = [sb(f"s5_sm{i}", [128, 32], F32) for i in range(8)]
        smi, _ = sb("s5_smi", [128, 32], I32)
        ZR, b_ZR = sb("s5_ZR", [128, 16, 32], F32)
        ZI, b_ZI = sb("s5_ZI", [128, 16, 32], F32)
        MAG, b_MAG = sb("s5_MAG", [128, 16, 32], F32)
        SN, b_SN = sb("s5_SN", [128, 16, 32], F32)
        CS, b_CS = sb("s5_CS", [128, 16, 32], F32)
        ER, b_ER = sb("s5_ER", [128, 16, 32], F32)
        EI, b_EI = sb("s5_EI", [128, 16, 32], F32)
        ALR, b_ALR = sb("s5_ALR", [128, 16, 32], F32)
        ALI, b_ALI = sb("s5_ALI", [128, 16, 32], F32)
        CA16, b_CA16 = sb("s5_CA16", [128, 16, 32], F32)
        CB16, b_CB16 = sb("s5_CB16", [128, 16, 32], F32)
        w1, b_w1 = sb("s5_w1", [128, 16, 32], F32)
        w2, b_w2 = sb("s5_w2", [128, 16, 32], F32)
        w3, b_w3 = sb("s5_w3", [128, 16, 32], F32)
        wi, _ = sb("s5_wi", [128, 16, 32], I32)
        sgn, b_sgn = sb("s5_sgn", [128, 1], F32)
        Bn, b_Bn = sb("s5_Bn", [128, 32, 16], F32)
        Bs, b_Bs = sb("s5_Bs", [128, 32, 16], F32)
        Pm, b_Pm = sb("s5_Pm", [128, 32, 128], BF16)
        Pp, b_Pp = sb("s5_Pp", [128, 32, 128], F32)
        Qm, b_Qm = sb("s5_Qm", [128, 32, 128], BF16)
        c_in, b_cin = sb("s5_cin", [128, 4, 256], F32)
        CrT, b_CrT = sb("s5_CrT", [128, 32, 16], F32)
        CiT, b_CiT = sb("s5_CiT", [128, 32, 16], F32)
        u1, b_u1 = sb("s5_u1", [128, 32, 16], F32)
        u2, b_u2 = sb("s5_u2", [128, 32, 16], F32)
        Dcol, b_Dcol = sb("s5_Dcol", [128, 32], F32)
        tmask, b_tmask = sb("s5_tmask", [128, 128], F32)
        tt, b_tt = sb("s5_tt", [128, 128], F32)
        ptp = [ps(f"s5_ptp{i}", [128, 128], F32) for i in range(2)]

        S.dma(lambda e: e.dma_start(out=lam_in[:], in_=K.s5_lam.rearrange("g a d p -> g (a d p)")), w=[b_lam])
        S.dma(lambda e: e.dma_start(out=dt_[:], in_=K.s5_ldt), w=[b_dt])
        S.dma(lambda e: e.dma_start(out=Bn[:], in_=K.s5_b), w=[b_Bn])
        S.dma(lambda e: e.dma_start(out=Bs[:], in_=K.s5_bs), w=[b_Bs])
        S.dma(lambda e: e.dma_start(out=c_in[:], in_=K.s5_c.rearrange("(s q) a d p -> q s (a d p)", q=128)), w=[b_cin])
        S.dma(lambda e: e.dma_start(out=Dcol[:], in_=K.s5_d), w=[b_Dcol])
        S.dma(lambda e: e.dma_start(out=tmask[:], in_=K.tmask), w=[b_tmask])
        S.dve(lambda e: e.memset(sgn[0:64, :], -1.0), w=[b_sgn])
        S.dve(lambda e: e.memset(sgn[64:128, :], 1.0), w=[b_sgn])
        for which, (dst, b_dst) in enumerate(((ar, b_ar), (ai, b_ai))):
            pt, b_pt = ptp[which]
            S.pe(lambda e, which=which, pt=pt: e.transpose(out=pt[:, 0:32], in_=lam_in[:, which * 128:(which + 1) * 128],
                                                           identity=K.id_f[0:32, 0:32]), r=[b_lam, K.b_idf], w=[b_pt])
            S.dve(lambda e, pt=pt, dst=dst: e.tensor_copy(out=dst[:], in_=pt[:, 0:32]), r=[b_pt], w=[b_dst])
        S.act(lambda e: e.activation(out=dt_[:], in_=dt_[:], func=AF.Exp), r=[b_dt], w=[b_dt])
        S.dve(lambda e: e.tensor_tensor(out=zr[:], in0=ar[:], in1=dt_[:], op=ALU.mult), r=[b_ar, b_dt], w=[b_zr])
        S.dve(lambda e: e.tensor_tensor(out=zi[:], in0=ai[:], in1=dt_[:], op=ALU.mult), r=[b_ai, b_dt], w=[b_zi])
        for ti_, tau in enumerate(range(-7, 9)):
            S.dve(lambda e, ti_=ti_, tau=tau: e.tensor_scalar(out=ZR[:, ti_, :], in0=zr[:], scalar1=float(tau), scalar2=None,
                                                              op0=ALU.mult), r=[b_zr], w=[b_ZR])
            S.dve(lambda e, ti_=ti_, tau=tau: e.tensor_scalar(out=ZI[:, ti_, :], in0=zi[:], scalar1=float(tau), scalar2=None,
                                                              op0=ALU.mult), r=[b_zi], w=[b_ZI])
        S.act(lambda e: e.activation(out=MAG[:], in_=ZR[:], func=AF.Exp), r=[b_ZR], w=[b_MAG])
        scr = (w1[:], wi[:], w2[:], b_w1)
        sin_eval(K, SN[:], ZI[:], scr, [b_ZI], [b_SN])
        sin_eval(K, CS[:], ZI[:], scr, [b_ZI], [b_CS], shift=PI / 2)
        S.dve(lambda e: e.tensor_tensor(out=ER[:], in0=MAG[:], in1=CS[:], op=ALU.mult), r=[b_MAG, b_CS], w=[b_ER])
        S.dve(lambda e: e.tensor_tensor(out=EI[:], in0=MAG[:], in1=SN[:], op=ALU.mult), r=[b_MAG, b_SN], w=[b_EI])
        (th, b_th), (sh, b_sh), (am1r, b_am1r), (am1i, b_am1i), (rl2, b_rl2), (kr, b_kr), (ki, b_ki), (tq, b_tq) = sm
        S.act(lambda e: e.activation(out=th[:], in_=zr[:], func=AF.Tanh, scale=0.5), r=[b_zr], w=[b_th])
        S.dve(lambda e: e.tensor_scalar(out=tq[:], in0=zi[:], scalar1=0.5, scalar2=None, op0=ALU.mult), r=[b_zi], w=[b_tq])
        scr2 = (am1r[:], smi[:], am1i[:], b_am1r)
        sin_eval(K, sh[:], tq[:], scr2, [b_tq], [b_sh])
        i1 = 8
        S.dve(lambda e: e.scalar_tensor_tensor(out=am1r[:], in0=MAG[:, i1, :], scalar=1.0, in1=th[:], op0=ALU.add, op1=ALU.mult),
              r=[b_MAG, b_th, b_sh], w=[b_am1r])
        S.dve(lambda e: e.tensor_tensor(out=am1r[:], in0=am1r[:], in1=CS[:, i1, :], op=ALU.mult), r=[b_am1r, b_CS], w=[b_am1r])
        S.dve(lambda e: e.tensor_tensor(out=tq[:], in0=sh[:], in1=sh[:], op=ALU.mult), r=[b_sh], w=[b_tq])
        S.dve(lambda e: e.scalar_tensor_tensor(out=am1r[:], in0=tq[:], scalar=-2.0, in1=am1r[:], op0=ALU.mult, op1=ALU.add),
              r=[b_tq, b_am1r], w=[b_am1r])
        S.dve(lambda e: e.tensor_copy(out=am1i[:], in_=EI[:, i1, :]), r=[b_EI, b_am1r], w=[b_am1i])
        S.dve(lambda e: e.tensor_tensor(out=rl2[:], in0=ar[:], in1=ar[:], op=ALU.mult), r=[b_ar], w=[b_rl2])
        S.dve(lambda e: e.tensor_tensor(out=tq[:], in0=ai[:], in1=ai[:], op=ALU.mult), r=[b_ai], w=[b_tq])
        S.dve(lambda e: e.tensor_tensor(out=rl2[:], in0=rl2[:], in1=tq[:], op=ALU.add), r=[b_rl2, b_tq], w=[b_rl2])
        S.dve(lambda e: e.reciprocal(out=rl2[:], in_=rl2[:]), r=[b_rl2], w=[b_rl2])
        S.dve(lambda e: e.tensor_tensor(out=kr[:], in0=am1r[:], in1=ar[:], op=ALU.mult), r=[b_am1r, b_ar], w=[b_kr])
        S.dve(lambda e: e.tensor_tensor(out=tq[:], in0=am1i[:], in1=ai[:], op=ALU.mult), r=[b_am1i, b_ai], w=[b_tq])
        S.dve(lambda e: e.tensor_tensor(out=kr[:], in0=kr[:], in1=tq[:], op=ALU.add), r=[b_kr, b_tq], w=[b_kr])
        S.dve(lambda e: e.tensor_tensor(out=kr[:], in0=kr[:], in1=rl2[:], op=ALU.mult), r=[b_kr, b_rl2], w=[b_kr])
        S.dve(lambda e: e.tensor_tensor(out=ki[:], in0=am1i[:], in1=ar[:], op=ALU.mult), r=[b_am1i, b_ar], w=[b_ki])
        S.dve(lambda e: e.tensor_tensor(out=tq[:], in0=am1r[:], in1=ai[:], op=ALU.mult), r=[b_am1r, b_ai], w=[b_tq])
        S.dve(lambda e: e.tensor_tensor(out=ki[:], in0=ki[:], in1=tq[:], op=ALU.subtract), r=[b_ki, b_tq], w=[b_ki])
        S.dve(lambda e: e.tensor_tensor(out=ki[:], in0=ki[:], in1=rl2[:], op=ALU.mult), r=[b_ki, b_rl2], w=[b_ki])
        krb = kr[:].unsqueeze(1).to_broadcast([128, 16, 32])
        kib = ki[:].unsqueeze(1).to_broadcast([128, 16, 32])
        S.dve(lambda e: e.tensor_tensor(out=ALR[:], in0=ER[:], in1=krb, op=ALU.mult), r=[b_ER, b_kr], w=[b_ALR])
        S.dve(lambda e: e.tensor_tensor(out=w3[:], in0=EI[:], in1=kib, op=ALU.mult), r=[b_EI, b_ki], w=[b_w3])
        S.dve(lambda e: e.tensor_tensor(out=ALR[:], in0=ALR[:], in1=w3[:], op=ALU.subtract), r=[b_ALR, b_w3], w=[b_ALR])
        S.dve(lambda e: e.tensor_tensor(out=ALI[:], in0=ER[:], in1=kib, op=ALU.mult), r=[b_ER, b_ki], w=[b_ALI])
        S.dve(lambda e: e.tensor_tensor(out=w3[:], in0=EI[:], in1=krb, op=ALU.mult), r=[b_EI, b_kr], w=[b_w3])
        S.dve(lambda e: e.tensor_tensor(out=ALI[:], in0=ALI[:], in1=w3[:], op=ALU.add), r=[b_ALI, b_w3], w=[b_ALI])
        S.dve(lambda e: e.tensor_scalar(out=ALI[:], in0=ALI[:], scalar1=sgn[:, 0:1], scalar2=None, op0=ALU.mult),
              r=[b_ALI, b_sgn], w=[b_ALI])
        for fam in range(2):
            dst, b_dst = (Pm, b_Pm) if fam == 0 else (Pp, b_Pp)
            for i in range(8):
                ti_ = (7 - i) if fam == 0 else (14 - i)
                ca = ALR[:, ti_, :].unsqueeze(2).to_broadcast([128, 32, 16])
                cb = ALI[:, ti_, :].unsqueeze(2).to_broadcast([128, 32, 16])
                S.dve(lambda e, ca=ca: e.tensor_tensor(out=u1[:], in0=Bn[:], in1=ca, op=ALU.mult), r=[b_Bn, b_ALR], w=[b_u1])
                S.dve(lambda e, cb=cb: e.tensor_tensor(out=u2[:], in0=Bs[:], in1=cb, op=ALU.mult), r=[b_Bs, b_ALI], w=[b_u2])
                S.dve(lambda e, dst=dst, i=i: e.tensor_tensor(out=dst[:, :, i * 16:(i + 1) * 16], in0=u1[:], in1=u2[:], op=ALU.add),
                      r=[b_u1, b_u2], w=[b_dst])
        for g in range(32):
            pt, b_pt = ptp[g % 2]
            S.pe(lambda e, g=g, pt=pt: e.transpose(out=pt[:], in_=Pp[:, g, :], identity=K.id_f[:]), r=[b_Pp, K.b_idf], w=[b_pt])
            S.dve(lambda e, g=g, pt=pt: e.tensor_copy(out=K.BA5[:, g, :], in_=pt[:]), r=[b_pt], w=[K.b_BA5])
        for sl in range(4):
            for part, (dst, b_dst) in enumerate(((CrT, b_CrT), (CiT, b_CiT))):
                pt, b_pt = ptp[(sl * 2 + part) % 2]
                S.pe(lambda e, sl=sl, part=part, pt=pt: e.transpose(out=pt[:], in_=c_in[:, sl, part * 128:(part + 1) * 128],
                                                                    identity=K.id_f[:]), r=[b_cin, K.b_idf], w=[b_pt])
                S.dve(lambda e, sl=sl, pt=pt, dst=dst: e.tensor_copy(
                    out=dst[:, sl * 8:(sl + 1) * 8, :], in_=pt[:].rearrange("p (g c) -> p g c", g=8)), r=[b_pt], w=[b_dst])
        S.dve(lambda e: e.tensor_copy(out=CA16[0:64], in_=ER[0:64]), r=[b_ER], w=[b_CA16])
        S.dve(lambda e: e.tensor_scalar(out=CA16[64:128], in0=EI[64:128], scalar1=-1.0, scalar2=None, op0=ALU.mult), r=[b_EI], w=[b_CA16])
        S.dve(lambda e: e.tensor_scalar(out=CB16[0:64], in0=EI[0:64], scalar1=-1.0, scalar2=None, op0=ALU.mult), r=[b_EI], w=[b_CB16])
        S.dve(lambda e: e.tensor_scalar(out=CB16[64:128], in0=ER[64:128], scalar1=-1.0, scalar2=None, op0=ALU.mult), r=[b_ER], w=[b_CB16])
        for fam in range(2):
            dst, b_dst = (Qm, b_Qm) if fam == 0 else (K.CA5, K.b_CA5)
            for j in range(8):
                ti_ = (7 + j) if fam == 0 else (8 + j)
                ca = CA16[:, ti_, :].unsqueeze(2).to_broadcast([128, 32, 16])
                cb = CB16[:, ti_, :].unsqueeze(2).to_broadcast([128, 32, 16])
                S.dve(lambda e, ca=ca: e.tensor_tensor(out=u1[:], in0=CrT[:], in1=ca, op=ALU.mult), r=[b_CrT, b_CA16], w=[b_u1])
                S.dve(lambda e, cb=cb: e.tensor_tensor(out=u2[:], in0=CiT[:], in1=cb, op=ALU.mult), r=[b_CiT, b_CB16], w=[b_u2])
                S.dve(lambda e, dst=dst, j=j: e.tensor_tensor(out=dst[:, :, j * 16:(j + 1) * 16], in0=u1[:], in1=u2[:], op=ALU.add),
                      r=[b_u1, b_u2], w=[b_dst])
        for g in range(32):
            pt, b_pt = ptp[g % 2]
            S.pe(lambda e, g=g, pt=pt: e.matmul(pt[:], lhsT=Pm[:, g, :], rhs=Qm[:, g, :], start=True, stop=True),
                 r=[b_Pm, b_Qm], w=[b_pt])
            S.dve(lambda e, pt=pt: e.tensor_tensor(out=tt[:], in0=pt[:], in1=tmask[:], op=ALU.mult), r=[b_pt, b_tmask], w=[b_tt])
            S.dve(lambda e, g=g: e.scalar_tensor_tensor(out=K.T5[:, g, :], in0=K.id_f[:], scalar=Dcol[:, g:g + 1], in1=tt[:],
                                                        op0=ALU.mult, op1=ALU.add), r=[K.b_idf, b_Dcol, b_tt], w=[K.b_T5])


def s5_setup_b(K):
    nc, S = K.nc, K.S
    I32 = mybir.dt.int32
    with ExitStack() as st:
        def sb(name, shape, dt):
            return K.sb(name, shape, dt, st)
        zin_, b_zin = sb("s5c_zin", [128, 3, 2048], F32)
        nv, b_nv = sb("s5c_nv", [128, 4], F32)
        sc, b_sc = sb("s5c_sc", [128, 4], F32)
        zrn, b_zrn = sb("s5c_zrn", [128, 2048], F32)
        phi, b_phi = sb("s5c_phi", [128, 2048], F32)
        mag, b_mag = sb("s5c_mag", [128, 2048], F32)
        th_, b_th_ = sb("s5c_th", [128, 2048], F32)
        sn, b_sn = sb("s5c_sn", [128, 2048], F32)
        cs, b_cs = sb("s5c_cs", [128, 2048], F32)
        x1, b_x1 = sb("s5c_x1", [128, 2048], F32)
        x2, b_x2 = sb("s5c_x2", [128, 2048], F32)
        xi, _ = sb("s5c_xi", [128, 2048], I32)
        tA, b_tA = sb("s5c_tA", [128, 32, 2, 64], BF16)
        tB, b_tB = sb("s5c_tB", [128, 32, 2, 64], BF16)
        eA, b_eA = sb("s5c_eA", [1, 32, 2, 64], F32)
        eB, b_eB = sb("s5c_eB", [1, 32, 2, 64], F32)
        S.dma(lambda e: e.dma_start(out=zin_[:], in_=K.s5_zrep), w=[b_zin])
        S.dma(lambda e: e.dma_start(out=nv[:], in_=K.nvec), w=[b_nv])
        S.dve(lambda e: e.tensor_scalar(out=sc[:, 0:1], in0=nv[:, 1:2], scalar1=-8.0, scalar2=None, op0=ALU.mult), r=[b_nv], w=[b_sc])
        S.dve(lambda e: e.tensor_scalar(out=sc[:, 1:2], in0=nv[:, 0:1], scalar1=8.0, scalar2=None, op0=ALU.mult), r=[b_nv], w=[b_sc])
        S.dve(lambda e: e.tensor_scalar(out=sc[:, 2:3], in0=nv[:, 1:2], scalar1=-1.0, scalar2=None, op0=ALU.mult), r=[b_nv], w=[b_sc])
        S.dve(lambda e: e.tensor_copy(out=sc[:, 3:4], in_=nv[:, 0:1]), r=[b_nv], w=[b_sc])
        S.act(lambda e: e.activation(out=zin_[:, 2, :], in_=zin_[:, 2, :], func=AF.Exp), r=[b_zin], w=[b_zin])
        S.dve(lambda e: e.tensor_tensor(out=zrn[:], in0=zin_[:, 0, :], in1=zin_[:, 2, :], op=ALU.mult), r=[b_zin], w=[b_zrn])
        S.dve(lambda e: e.scalar_tensor_tensor(out=x1[:], in0=zin_[:, 1, :], scalar=8.0, in1=zin_[:, 2, :], op0=ALU.mult, op1=ALU.mult),
              r=[b_zin], w=[b_x1])
        S.dve(lambda e: e.tensor_scalar(out=x2[:], in0=x1[:], scalar1=1.0 / (2 * PI), scalar2=64.5, op0=ALU.mult, op1=ALU.add), r=[b_x1], w=[b_x2])
        S.dve(lambda e: e.tensor_copy(out=xi[:], in_=x2[:]), r=[b_x2], w=[b_x2])
        S.dve(lambda e: e.tensor_copy(out=phi[:], in_=xi[:]), r=[b_x2], w=[b_phi])
        S.dve(lambda e: e.tensor_tensor(out=x2[:], in0=x2[:], in1=phi[:], op=ALU.subtract), r=[b_x2, b_phi], w=[b_x2])
        S.dve(lambda e: e.tensor_scalar(out=x2[:], in0=x2[:], scalar1=0.0, scalar2=None, op0=ALU.is_lt), r=[b_x2], w=[b_x2])
        S.dve(lambda e: e.tensor_tensor(out=phi[:], in0=phi[:], in1=x2[:], op=ALU.subtract), r=[b_x2, b_phi], w=[b_phi])
        S.dve(lambda e: e.tensor_scalar(out=phi[:], in0=phi[:], scalar1=-2 * PI, scalar2=128 * PI, op0=ALU.mult, op1=ALU.add), r=[b_phi], w=[b_phi])
        S.dve(lambda e: e.tensor_tensor(out=phi[:], in0=phi[:], in1=x1[:], op=ALU.add), r=[b_phi, b_x1], w=[b_phi])
        scr = (x1[:], xi[:], x2[:], b_x1)
        for which in range(2):
            S.act(lambda e, which=which: e.activation(out=mag[:], in_=zrn[:], func=AF.Exp, scale=sc[:, which:which + 1]),
                  r=[b_zrn, b_sc], w=[b_mag])
            S.dve(lambda e, which=which: e.tensor_scalar(out=th_[:], in0=phi[:], scalar1=sc[:, 2 + which:3 + which], scalar2=None,
                                                         op0=ALU.mult), r=[b_phi, b_sc], w=[b_th_])
            sin_eval(K, sn[:], th_[:], scr, [b_th_], [b_sn])
            sin_eval(K, cs[:], th_[:], scr, [b_th_], [b_cs], shift=PI / 2)
            S.dve(lambda e: e.tensor_tensor(out=cs[:], in0=cs[:], in1=mag[:], op=ALU.mult), r=[b_cs, b_mag], w=[b_cs])
            S.dve(lambda e: e.tensor_tensor(out=sn[:], in0=sn[:], in1=mag[:], op=ALU.mult), r=[b_sn, b_mag], w=[b_sn])
            cs3 = cs[:].rearrange("p (g q) -> p g q", g=32)
            sn3 = sn[:].rearrange("p (g q) -> p g q", g=32)
            S.dve(lambda e, cs3=cs3: e.tensor_copy(out=tA[:, :, 0, :], in_=cs3), r=[b_cs], w=[b_tA])
            S.dve(lambda e, cs3=cs3: e.tensor_copy(out=tA[:, :, 1, :], in_=cs3), r=[b_cs], w=[b_tA])
            S.dve(lambda e, sn3=sn3: e.tensor_scalar(out=tB[:, :, 0, :], in0=sn3, scalar1=-1.0, scalar2=None, op0=ALU.mult), r=[b_sn], w=[b_tB])
            S.dve(lambda e, sn3=sn3: e.tensor_copy(out=tB[:, :, 1, :], in_=sn3), r=[b_sn], w=[b_tB])
            S.dma(lambda e, which=which: e.dma_start(out=K.tab_d[2 * which], in_=tA[:].rearrange("p g r q -> p (g r q)")), r=[b_tA], q="gq")
            S.dma(lambda e, which=which: e.dma_start(out=K.tab_d[2 * which + 1], in_=tB[:].rearrange("p g r q -> p (g r q)")), r=[b_tB], q="gq")
        S.act(lambda e: e.activation(out=mag[0:1, :], in_=zrn[0:1, :], func=AF.Exp, scale=1024.0), r=[b_zrn], w=[b_mag])
        S.dve(lambda e: e.tensor_scalar(out=th_[0:1, :], in0=phi[0:1, :], scalar1=128.0, scalar2=None, op0=ALU.mult), r=[b_phi], w=[b_th_])
        scr1 = (x1[0:1, :], xi[0:1, :], x2[0:1, :], b_x1)
        sin_eval(K, sn[0:1, :], th_[0:1, :], scr1, [b_th_], [b_sn])
        sin_eval(K, cs[0:1, :], th_[0:1, :], scr1, [b_th_], [b_cs], shift=PI / 2)
        S.dve(lambda e: e.tensor_tensor(out=cs[0:1, :], in0=cs[0:1, :], in1=mag[0:1, :], op=ALU.mult), r=[b_cs, b_mag], w=[b_cs])
        S.dve(lambda e: e.tensor_tensor(out=sn[0:1, :], in0=sn[0:1, :], in1=mag[0:1, :], op=ALU.mult), r=[b_sn, b_mag], w=[b_sn])
        cs3 = cs[0:1, :].rearrange("p (g q) -> p g q", g=32)
        sn3 = sn[0:1, :].rearrange("p (g q) -> p g q", g=32)
        S.dve(lambda e: e.tensor_copy(out=eA[:, :, 0, :], in_=cs3), r=[b_cs], w=[b_eA])
        S.dve(lambda e: e.tensor_copy(out=eA[:, :, 1, :], in_=cs3), r=[b_cs], w=[b_eA])
        S.dve(lambda e: e.tensor_scalar(out=eB[:, :, 0, :], in0=sn3, scalar1=-1.0, scalar2=None, op0=ALU.mult), r=[b_sn], w=[b_eB])
        S.dve(lambda e: e.tensor_copy(out=eB[:, :, 1, :], in_=sn3), r=[b_sn], w=[b_eB])
        S.dma(lambda e: e.dma_start(out=K.tab_e[0:1, :], in_=eA[:].rearrange("p g r q -> p (g r q)")), r=[b_eA], q="gq")
        S.dma(lambda e: e.dma_start(out=K.tab_e[1:2, :], in_=eB[:].rearrange("p g r q -> p (g r q)")), r=[b_eB], q="gq")
        S.flush()
        for nm, src in (("T5", K.T5), ("BA5", K.BA5), ("CA5", K.CA5)):
            if nm in K.dbg_out:
                S.dma(lambda e, nm=nm, src=src: e.dma_start(out=K.dbg_out[nm], in_=src[:]), q="gq")
        if "tab" in K.dbg_out:
            S.dma(lambda e: e.dma_start(out=K.dbg_out["tab"], in_=K.tab_d), q="gq")
        if "tabe" in K.dbg_out:
            S.dma(lambda e: e.dma_start(out=K.dbg_out["tabe"], in_=K.tab_e), q="gq")
        S.flush()


def phase1(K):
    nc, S, NB = K.nc, K.S, K.NB
    with ExitStack() as st:
        def sb(name, shape, dt):
            return K.sb(name, shape, dt, st)

        def ps(name, shape, dt):
            return K.ps(name, shape, dt, st)
        xt = [sb(f"p1_x{i}", [128, 8, D], F32) for i in range(2)]
        hns = [sb(f"p1_hn{i}", [128, 8, D], BF16) for i in range(2)]
        hT = [sb(f"p1_hT{i}", [128, 8, 1024], BF16) for i in range(2)]
        junk, b_junk = sb("p1_junk", [128, D], BF16)
        ss, b_ss = sb("p1_ss", [128, 8], F32)
        rstd, b_rstd = sb("p1_rstd", [128, 8], F32)
        wf, b_wf = sb("p1_wf", [128, 8, 8], BF16)
        wu1, b_wu1 = sb("p1_wu", [128, 8, 512], BF16)
        Ut1 = [sb(f"p1_Ut{i}", [128, 32, 128], BF16) for i in range(2)]
        pu1 = [ps(f"p1_pu{i}", [128, 512], F32) for i in range(2)]
        bf_s, b_bf = sb("p1_bf", [128, 64], F32)
        lgs = [sb(f"p1_lg{i}", [128, 8, 8], F32) for i in range(2)]
        ppfs = [sb(f"p1_ppfs{i}", [128, 16], F32) for i in range(2)]
        car = [sb(f"p1_car{i}", [128, 8], F32) for i in range(2)]
        cT32, b_cT32 = sb("p1_cT32", [8, 1024], F32)
        cT16, b_cT16 = sb("p1_cT16", [8, 1024], BF16)
        gcol, b_gcol = sb("p1_gcol", [8, 1], F32)
        ptr = [ps(f"p1_ptr{i}", [128, 1024], BF16) for i in range(2)]
        pf, b_pf = ps("p1_pf", [128, 8, 8], F32)
        ppf, b_ppf = ps("p1_ppf", [128, 16], F32)
        pcT, b_pcT = ps("p1_pcT", [8, 1024], F32)

        S.dma(lambda e: e.dma_start(out=wf[:], in_=K.wb_in[:, :, C_F:C_F + 8]), w=[b_wf])
        S.dma(lambda e: e.dma_start(out=wu1[:], in_=K.wb_in[:, :, C_U:C_U + 512]), w=[b_wu1])
        S.dma(lambda e: e.dma_start(out=bf_s[:], in_=K.bf_rep), w=[b_bf])
        def stageA(s):
            x_t, b_x = xt[s % 2]
            hn, b_hn = hns[s % 2]
            S.dma(lambda e, s=s, x_t=x_t: e.dma_start(
                out=x_t[:], in_=K.x[s * 1024:(s + 1) * 1024, :].rearrange("(n j) d -> n j d", j=8)), w=[b_x])
            for j in range(8):
                S.act(lambda e, j=j, x_t=x_t: e.activation(out=junk[:], in_=x_t[:, j, :], func=AF.Square,
                                                           accum_out=ss[:, j:j + 1]), r=[b_x], w=[b_junk, b_ss])
            S.act(lambda e: e.activation(out=rstd[:], in_=ss[:], func=AF.Sqrt, scale=1.0 / D, bias=K.cst[:, 0:1]),
                  r=[b_ss, K.b_cst], w=[b_rstd])
            S.dve(lambda e: e.reciprocal(out=rstd[:], in_=rstd[:]), r=[b_rstd], w=[b_rstd])
            for j in range(8):
                if j % 2 == 0:
                    S.dve(lambda e, j=j, x_t=x_t: e.tensor_scalar(
                        out=hn[:, j, :], in0=x_t[:, j, :], scalar1=rstd[:, j:j + 1], scalar2=None, op0=ALU.mult),
                        r=[b_x, b_rstd], w=[b_hn])
                else:
                    S.act(lambda e, j=j, x_t=x_t: e.activation(
                        out=hn[:, j, :], in_=x_t[:, j, :], func=AF.Copy, scale=rstd[:, j:j + 1]),
                        r=[b_x, b_rstd], w=[b_hn])

        def stageB(s):
            hn, b_hn = hns[s % 2]
            h_t, b_h = hT[s % 2]
            for kc in range(8):
                pt, b_pt = ptr[kc % 2]
                for j in range(8):
                    S.pe(lambda e, kc=kc, j=j, pt=pt: e.transpose(
                        out=pt[:, j * 128:(j + 1) * 128], in_=hn[:, j, kc * 128:(kc + 1) * 128],
                        identity=K.id_b[:]), r=[b_hn, K.b_idb], w=[b_pt])
                S.act(lambda e, kc=kc, pt=pt, h_t=h_t: e.activation(
                    out=h_t[:, kc, :], in_=pt[:], func=AF.Identity,
                    bias=K.modc[:, kc:kc + 1], scale=K.acol[:, kc:kc + 1]),
                    r=[b_pt, K.b_modc, K.b_acol], w=[b_h])
            S.dma(lambda e, s=s, h_t=h_t: e.dma_start(out=K.hT_d[s], in_=h_t[:]), r=[b_h], q="gq")
            if "skip3" not in K.flags:
                U_t, b_U = Ut1[s % 2]
                for j in range(8):
                    pu_, b_pu = pu1[j % 2]
                    for kc in range(8):
                        S.pe(lambda e, j=j, kc=kc, h_t=h_t, pu_=pu_: e.matmul(
                            pu_[:], lhsT=h_t[:, kc, j * 128:(j + 1) * 128], rhs=wu1[:, kc, :],
                            start=(kc == 0), stop=(kc == 7)), r=[b_h, b_wu1], w=[b_pu])
                    pu3 = pu_[:].rearrange("p (g c) -> p g c", g=32)
                    if j % 2 == 0:
                        S.dve(lambda e, j=j, pu3=pu3, U_t=U_t: e.tensor_copy(out=U_t[:, :, j * 16:(j + 1) * 16], in_=pu3),
                              r=[b_pu], w=[b_U])
                    else:
                        S.act(lambda e, j=j, pu3=pu3, U_t=U_t: e.activation(out=U_t[:, :, j * 16:(j + 1) * 16], in_=pu3, func=AF.Copy),
                              r=[b_pu], w=[b_U])
                S.dma(lambda e, s=s, U_t=U_t: e.dma_start(out=K.ut_d[s], in_=U_t[:]), r=[b_U], q="gq")
            for j in range(8):
                for kc in range(8):
                    S.pe(lambda e, j=j, kc=kc, h_t=h_t: e.matmul(
                        pf[:, j, :], lhsT=h_t[:, kc, j * 128:(j + 1) * 128], rhs=wf[:, kc, :],
                        start=(kc == 0), stop=(kc == 7)), r=[b_h, b_wf], w=[b_pf])
            lg, b_lg = lgs[s % 2]
            pfs, b_pfs = ppfs[s % 2]
            pf2 = pf[:].rearrange("p j h -> p (j h)")
            lg2 = lg[:].rearrange("p j h -> p (j h)")
            S.dve(lambda e, lg2=lg2, pf2=pf2: e.tensor_tensor(out=lg2, in0=pf2, in1=bf_s[:], op=ALU.add), r=[b_pf, b_bf], w=[b_lg])
            S.act(lambda e, lg2=lg2: e.activation(out=lg2, in_=lg2, func=AF.Exp, scale=-1.0), r=[b_lg], w=[b_lg])
            S.act(lambda e, lg2=lg2: e.activation(out=lg2, in_=lg2, func=AF.Ln, bias=K.cst[:, 1:2], scale=1.0),
                  r=[b_lg, K.b_cst], w=[b_lg])
            for j in range(1, 8):
                S.dve(lambda e, j=j, lg=lg: e.tensor_tensor(out=lg[:, j, :], in0=lg[:, j, :], in1=lg[:, j - 1, :], op=ALU.add),
                      r=[b_lg], w=[b_lg])
            S.pe(lambda e, lg=lg: e.matmul(ppf[:, 0:8], lhsT=K.tri_fs[:], rhs=lg[:, 7, :], start=True, stop=True),
                 r=[K.b_trif, b_lg], w=[b_ppf])
            S.pe(lambda e, lg=lg: e.matmul(ppf[:, 8:16], lhsT=K.ones_f[:], rhs=lg[:, 7, :], start=True, stop=True),
                 r=[K.b_onesf, b_lg], w=[b_ppf])
            S.dve(lambda e, pfs=pfs: e.tensor_copy(out=pfs[:], in_=ppf[:]), r=[b_ppf], w=[b_pfs])
            if s % 2 == 1:
                A, B = s - 1, s
                (lgA, b_lgA), (lgB, b_lgB) = lgs
                (pA, b_pA), (pB, b_pB) = ppfs
                (cA, b_cA), (cB, b_cB) = car
                S.dve(lambda e: e.scalar_tensor_tensor(out=cA[:], in0=pB[:, 8:16], scalar=K.wj[:, 0:1], in1=K.Gcar[:],
                                                       op0=ALU.mult, op1=ALU.add), r=[b_pB, K.b_wj, K.b_Gcar], w=[b_cA])
                S.dve(lambda e: e.scalar_tensor_tensor(out=cB[:], in0=pA[:, 8:16], scalar=K.wj[:, 1:2], in1=K.Gcar[:],
                                                       op0=ALU.mult, op1=ALU.add), r=[b_pA, K.b_wj, K.b_Gcar], w=[b_cB])
                for (P_, lgX, b_lgX, pX, b_pX, cX, b_cX) in ((A, lgA, b_lgA, pA, b_pA, cA, b_cA), (B, lgB, b_lgB, pB, b_pB, cB, b_cB)):
                    S.dve(lambda e, P_=P_, pX=pX, cX=cX: e.tensor_tensor(out=K.Gend[:, P_, :], in0=pX[:, 8:16], in1=cX[:], op=ALU.add),
                          r=[b_pX, b_cX], w=[K.b_Gend])
                    S.dve(lambda e, pX=pX, cX=cX: e.tensor_tensor(out=pX[:, 0:8], in0=pX[:, 0:8], in1=cX[:], op=ALU.add),
                          r=[b_pX, b_cX], w=[b_pX])
                    S.dve(lambda e, P_=P_, lgX=lgX, pX=pX: e.tensor_tensor(
                        out=K.G[:, P_ * 8:(P_ + 1) * 8, :], in0=lgX[:], in1=pX[:, 0:8].unsqueeze(1).to_broadcast([128, 8, 8]),
                        op=ALU.add), r=[b_lgX, b_pX], w=[K.b_G])
                S.dve(lambda e: e.tensor_tensor(out=K.Gcar[:], in0=K.Gcar[:], in1=pA[:, 8:16], op=ALU.add),
                      r=[K.b_Gcar, b_pA], w=[K.b_Gcar])
                S.dve(lambda e: e.tensor_tensor(out=K.Gcar[:], in0=K.Gcar[:], in1=pB[:, 8:16], op=ALU.add),
                      r=[K.b_Gcar, b_pB], w=[K.b_Gcar])
                for j in range(8):
                    S.pe(lambda e, A=A, j=j: e.transpose(out=pcT[:, j * 128:(j + 1) * 128], in_=K.G[:, A * 8 + j, :],
                                                         identity=K.id_f[:]), r=[K.b_G, K.b_idf], w=[b_pcT])
                S.dve(lambda e: e.tensor_copy(out=cT32[:], in_=pcT[:]), r=[b_pcT], w=[b_cT32])
                S.dve(lambda e: e.tensor_copy(out=gcol[:], in_=cT32[:, 1023:1024]), r=[b_cT32], w=[b_gcol])
                S.dve(lambda e: e.tensor_scalar(out=cT16[:], in0=cT32[:], scalar1=gcol[:, 0:1], scalar2=-1.0,
                                                op0=ALU.subtract, op1=ALU.mult), r=[b_cT32, b_gcol], w=[b_cT16])
                S.dma(lambda e, A=A: e.dma_start(out=K.c_d[A // 2], in_=cT16[:]), r=[b_cT16], q="gq")
            if s == 0 and "hT" in K.dbg_out:
                S.dma(lambda e, h_t=h_t: e.dma_start(out=K.dbg_out["hT"], in_=h_t[:]), r=[b_h], q="gq")
        stageA(0)
        for s in range(NB):
            if s + 1 < NB:
                stageA(s + 1)
            stageB(s)
        if "G" in K.dbg_out:
            S.dma(lambda e: e.dma_start(out=K.dbg_out["G"], in_=K.G[:]), r=[K.b_G], q="gq")
        S.flush()


def phase2(K):
    nc, S, NB = K.nc, K.S, K.NB
    T = K.T
    with ExitStack() as st:
        def sb(name, shape, dt):
            return K.sb(name, shape, dt, st)

        def ps(name, shape, dt):
            return K.ps(name, shape, dt, st)
        NR = NB // 2
        QA, _ = sb("QA", [128, T // 2], BF16)
        QB, _ = sb("QB", [128, T // 2], BF16)
        KA, _ = sb("KA", [128, T], BF16)
        KB, _ = sb("KB", [128, T], BF16)
        V2, _ = sb("V2", [128, NB * 8, 192], BF16)
        b_Q = [S.buf(f"Q{s}") for s in range(NB // 2)]
        b_K = [S.buf(f"K{s}") for s in range(NB)]
        b_V = [S.buf(f"V{s}") for s in range(NB)]
        hTb = [sb(f"p2_hT{i}", [128, 8, 1024], BF16) for i in range(2)]
        wq, b_wq = sb("p2_wq", [128, 8, 128], BF16)
        wk, b_wk = sb("p2_wk", [128, 8, 128], BF16)
        wv, b_wv = sb("p2_wv", [128, 8, 128], BF16)
        cm5, b_cm5 = sb("p2_cm5", [128, 5, 512], BF16)
        mB, b_mB = sb("p2_mB", [128, 512], BF16)
        biasq = [sb(f"p2_biasq{i}", [128, NB * 8, 8], F32) for i in range(2)]
        pT = [sb(f"p2_pT{i}", [128, 512], BF16) for i in range(4)]
        rden = [sb(f"p2_rden{i}", [128, 512], F32) for i in range(2)]
        num = [sb(f"p2_num{i}", [128, 512], F32) for i in range(2)]
        yo = [sb(f"p2_yo{i}", [128, 512], BF16) for i in range(2)]
        pss = [ps(f"p2_s{i}", [128, 512], F32) for i in range(3)]
        po = [ps(f"p2_o{i}", [128, 512], F32) for i in range(2)]
        pb, b_pb = ps("p2_b", [128, 512], F32)
        pp = [ps(f"p2_p{i}", [128, 512], F32) for i in range(2)]

        for i, tile_ in enumerate((QA, QB, KA, KB)):
            for s in range(NB // 2 if i < 2 else NB):
                bb = b_Q[s] if i < 2 else b_K[s]
                eng = S.pool if (i + s) % 2 else S.dve
                eng(lambda e, tile_=tile_, s=s: e.memset(tile_[:, s * 1024:(s + 1) * 1024], 0.0), w=[bb])
        for s in range(NB):
            S.dve(lambda e, s=s: e.memset(KA[64:65, s * 1024:(s + 1) * 1024], 1.0), w=[b_K[s]])
            S.dve(lambda e, s=s: e.memset(KB[0:1, s * 1024:(s + 1) * 1024], 1.0), w=[b_K[s]])
            S.pool(lambda e, s=s: e.memset(V2[:, s * 8:(s + 1) * 8, 64:128], 0.0), w=[b_V[s]])
            S.pool(lambda e, s=s: e.memset(V2[:, s * 8:(s + 1) * 8, 64:65], 1.0), w=[b_V[s]])
        S.dma(lambda e: e.dma_start(out=cm5[:], in_=K.cm5), w=[b_cm5])
        S.dma(lambda e: e.dma_start(out=mB[:], in_=K.mB_in), w=[b_mB])
        mBc, _ = sb("p2_mBc", [128, 1], F32)
        S.dve(lambda e: e.tensor_copy(out=mBc[:], in_=mB[:, 0:1]), r=[b_mB], w=[b_mB])

        ti = 0
        for hp in range(4):
            S.dma(lambda e, hp=hp: e.dma_start(out=wq[:], in_=K.wb_in[:, :, C_Q + hp * 128:C_Q + (hp + 1) * 128]), w=[b_wq])
            S.dma(lambda e, hp=hp: e.dma_start(out=wk[:], in_=K.wb_in[:, :, C_K + hp * 128:C_K + (hp + 1) * 128]), w=[b_wk])
            S.dma(lambda e, hp=hp: e.dma_start(out=wv[:], in_=K.wb_in[:, :, C_V + hp * 128:C_V + (hp + 1) * 128]), w=[b_wv])
            for s in range(NB):
                h_t, b_h = hTb[s % 2]
                S.dma(lambda e, s=s, h_t=h_t: e.dma_start(out=h_t[:], in_=K.hT_d[s]), w=[b_h])
                own = (s % 2 == 0)
                rr = s // 2
                if own:
                    S.dma(lambda e, rr=rr, hp=hp: e.dma_start(out=QA[64:65, rr * 1024:(rr + 1) * 1024],
                                                              in_=K.c_d[rr, 2 * hp:2 * hp + 1, :]), w=[b_Q[rr]])
                    S.dma(lambda e, rr=rr, hp=hp: e.dma_start(out=QB[0:1, rr * 1024:(rr + 1) * 1024],
                                                              in_=K.c_d[rr, 2 * hp + 1:2 * hp + 2, :]), w=[b_Q[rr]])
                for which in ((0, 1) if own else (1,)):
                    wt, b_wt = (wq, b_wq) if which == 0 else (wk, b_wk)
                    tA, tB = (QA, QB) if which == 0 else (KA, KB)
                    bb = b_Q[rr] if which == 0 else b_K[s]
                    scl = 0.125 if which == 0 else 1.0
                    for half in range(2):
                        p_t, b_p = pp[ti % 2]
                        ti += 1
                        for kc in range(8):
                            S.pe(lambda e, kc=kc, half=half, p_t=p_t, wt=wt, h_t=h_t: e.matmul(
                                p_t[:], lhsT=wt[:, kc, :], rhs=h_t[:, kc, half * 512:(half + 1) * 512],
                                start=(kc == 0), stop=(kc == 7)), r=[b_wt, b_h], w=[b_p])
                        c0 = (rr if which == 0 else s) * 1024 + half * 512
                        S.act(lambda e, p_t=p_t, tA=tA, c0=c0, scl=scl: e.activation(
                            out=tA[0:64, c0:c0 + 512], in_=p_t[0:64, :], func=AF.Copy, scale=scl),
                            r=[b_p], w=[bb])
                        S.dve(lambda e, p_t=p_t, tB=tB, c0=c0, scl=scl: e.tensor_scalar(
                            out=tB[64:128, c0:c0 + 512], in0=p_t[64:128, :], scalar1=scl, scalar2=None, op0=ALU.mult),
                            r=[b_p], w=[bb])
                for jh in range(2):
                    p_t, b_p = pp[ti % 2]
                    ti += 1
                    for jj in range(4):
                        j = jh * 4 + jj
                        for kc in range(8):
                            S.pe(lambda e, kc=kc, j=j, jj=jj, p_t=p_t, h_t=h_t: e.matmul(
                                p_t[:, jj * 128:(jj + 1) * 128], lhsT=h_t[:, kc, j * 128:(j + 1) * 128], rhs=wv[:, kc, :],
                                start=(kc == 0), stop=(kc == 7)), r=[b_wv, b_h], w=[b_p])
                    kt0 = s * 8 + jh * 4
                    pv = p_t[:].rearrange("p (j c) -> p j c", j=4)
                    S.dve(lambda e, pv=pv, kt0=kt0: e.tensor_copy(out=V2[:, kt0:kt0 + 4, 0:64], in_=pv[:, :, 0:64]),
                          r=[b_p], w=[b_V[s]])
                    S.act(lambda e, pv=pv, kt0=kt0: e.activation(out=V2[:, kt0:kt0 + 4, 128:192], in_=pv[:, :, 64:128],
                                                                 func=AF.Copy), r=[b_p], w=[b_V[s]])
            tiles = []
            for sq in range(NB // 2):
                nkt = (2 * sq + 2) * 8
                for hq in range(2):
                    for hl in range(2):
                        for kt in range(nkt):
                            tiles.append((sq, hq, hl, kt, nkt))
            n_t_ = len(tiles)
            LA = 1 if "la1" in K.flags else 2
            bias_done = set()
            pending = []
            gctr = [0]

            def emit_qk(i):
                sq, hq, hl, kt, nkt = tiles[i]
                if sq not in bias_done:
                    bias_done.add(sq)
                    bq, b_bq = biasq[sq % 2]
                    S.dve(lambda e, sq=sq, nkt=nkt, bq=bq: e.tensor_tensor(
                        out=bq[:, 0:nkt, :], in0=K.G[:, 0:nkt, :],
                        in1=K.Gend[:, 2 * sq, :].unsqueeze(1).to_broadcast([128, nkt, 8]), op=ALU.subtract),
                        r=[K.b_G, K.b_Gend], w=[b_bq])
                    k0 = (2 * sq + 1) * 8
                    S.dve(lambda e, bq=bq, k0=k0: e.tensor_scalar(out=bq[:, k0:k0 + 8, :], in0=bq[:, k0:k0 + 8, :], scalar1=mBc[:, 0:1],
                                                                 scalar2=None, op0=ALU.add), r=[b_bq, b_mB], w=[b_bq])
                sk, jk = kt // 8, kt % 8
                q0 = sq * 1024 + hq * 512
                Qt = QA if hl == 0 else QB
                Kt = KA if hl == 0 else KB
                s_t, b_s = pss[i % 3]
                diag = (sk == 2 * sq)
                partner = (sk == 2 * sq + 1)
                S.pe(lambda e: e.matmul(s_t[:], lhsT=Kt[:, kt * 128:(kt + 1) * 128], rhs=Qt[:, q0:q0 + 512],
                                        start=True, stop=(not diag)), r=[b_K[sk], b_Q[sq]], w=[b_s])
                if diag:
                    a = min(max(jk - 4 * hq, 0), 4)
                    S.pe(lambda e: e.matmul(s_t[:], lhsT=K.id_b[:], rhs=cm5[:, a, :], start=False, stop=True),
                         r=[K.b_idb, b_cm5], w=[b_s])


            def emit_exp(i):
                sq, hq, hl, kt, nkt = tiles[i]
                h = 2 * hp + hl
                s_t, b_s = pss[i % 3]
                p_t, b_p = pT[i % 4]
                bq, b_bq = biasq[sq % 2]
                S.act(lambda e: e.activation(out=p_t[:], in_=s_t[:], func=AF.Exp, bias=bq[:, kt, h:h + 1], scale=1.0),
                      r=[b_s, b_bq], w=[b_p])

            def emit_pv(i):
                sq, hq, hl, kt, nkt = tiles[i]
                sk = kt // 8
                p_t, b_p = pT[i % 4]
                if kt == 0:
                    gctr[0] += 1
                g = gctr[0]
                o_t, b_o = po[g % 2]
                if hl == 0:
                    S.pe(lambda e: e.matmul(o_t[0:65, :], lhsT=V2[:, kt, 0:65], rhs=p_t[:],
                                            start=(kt == 0), stop=(kt == nkt - 1)), r=[b_V[sk], b_p], w=[b_o])
                else:
                    S.pe(lambda e: e.matmul(o_t[:], lhsT=V2[:, kt, 64:192], rhs=p_t[:],
                                            start=(kt == 0), stop=(kt == nkt - 1)), r=[b_V[sk], b_p], w=[b_o])
                if kt == nkt - 1:
                    n_t, b_n = num[g % 2]
                    rd, b_rd = rden[g % 2]
                    y_t, b_y = yo[hq]
                    if hl == 0:
                        S.dve(lambda e: e.reciprocal(out=rd[64:65, :], in_=o_t[64:65, :]), r=[b_o], w=[b_rd])
                        S.dve(lambda e: e.tensor_copy(out=n_t[0:64, :], in_=o_t[0:64, :]), r=[b_o], w=[b_n])
                    else:
                        S.dve(lambda e: e.reciprocal(out=rd[0:1, :], in_=o_t[0:1, :]), r=[b_o], w=[b_rd])
                        S.dve(lambda e: e.tensor_copy(out=n_t[64:128, :], in_=o_t[64:128, :]), r=[b_o], w=[b_n])

                    def stage2():
                        if hl == 0:
                            S.pe(lambda e: e.matmul(pb[0:64, :], lhsT=K.ones_f[64:65, 0:64], rhs=rd[64:65, :],
                                                    start=True, stop=True), r=[K.b_onesf, b_rd], w=[b_pb])
                            S.dve(lambda e: e.tensor_tensor(out=y_t[0:64, :], in0=n_t[0:64, :], in1=pb[0:64, :], op=ALU.mult),
                                  r=[b_n, b_pb], w=[b_y])
                        else:
                            S.pe(lambda e: e.matmul(pb[:], lhsT=K.ones_f[0:1, :], rhs=rd[0:1, :],
                                                    start=True, stop=True), r=[K.b_onesf, b_rd], w=[b_pb])
                            S.dve(lambda e: e.tensor_tensor(out=y_t[64:128, :], in0=n_t[64:128, :], in1=pb[64:128, :], op=ALU.mult),
                                  r=[b_n, b_pb], w=[b_y])
                            S.dma(lambda e, hp=hp: e.dma_start(out=K.yaT_d[sq, :, hp, hq * 512:(hq + 1) * 512], in_=y_t[:]),
                                  r=[b_y], q="sp")
                    pending.append((i + (0 if "nodefer" in K.flags else 3), stage2))

            for i in range(min(LA, n_t_)):
                emit_qk(i)
            for i in range(n_t_):
                emit_exp(i)
                if i + LA < n_t_:
                    emit_qk(i + LA)
                emit_pv(i)
                while pending and pending[0][0] <= i:
                    pending.pop(0)[1]()
            while pending:
                pending.pop(0)[1]()
        S.flush()
        if "yaT" in K.dbg_out:
            S.dma(lambda e: e.dma_start(out=K.dbg_out["yaT"], in_=K.yaT_d), q="gq")
            S.flush()


def phase3(K):
    nc, S, NB = K.nc, K.S, K.NB
    GC = 0.7978845608028654
    with ExitStack() as st:
        def sb(name, shape, dt):
            return K.sb(name, shape, dt, st)

        def ps(name, shape, dt):
            return K.ps(name, shape, dt, st)
        hTb, b_h = sb("p3_hT", [128, 8, 1024], BF16)
        yaT, b_ya = sb("p3_yaT", [128, 4, 512], BF16)
        wua4, b_wua = sb("p3_wupa", [128, 4, 1024], BF16)
        wub4, b_wub = sb("p3_wupb", [128, 4, 1024], BF16)
        wgl, b_wgl = sb("p3_wglu", [128, 4, 512], BF16)
        wbuf = [sb(f"p3_w{i}", [128, 8, 512], BF16) for i in range(4)]
        tabs = [sb(f"p3_tab{i}", [128, 4, 512], BF16) for i in range(2)]
        UtXp, b_UtXp = sb("p3_UtXp", [128, 32, 128], BF16)
        UT, b_UT = sb("p3_UT", [128, 32, 128], BF16)
        SY, b_SY = sb("p3_SY", [128, 32, 128], BF16)
        XpT, b_XpT = sb("p3_XpT", [128, 32, 128], BF16)
        ybT, b_ybT = sb("p3_ybT", [128, 4, 1024], BF16)
        tmp = [sb(f"p3_t{i}", [128, 512], F32) for i in range(6)]
        thb = [sb(f"p3_th{i}", [128, 512], BF16) for i in range(4)]
        Xin8, b_Xin = sb("p3_Xin8", [8, 512], BF16)
        totB8, b_totB = sb("p3_totB8", [8, 512], F32)
        rows = [sb(f"p3_row{i}", [8, 512], F32) for i in range(4)]
        xown8, b_xown = sb("p3_xown8", [8, 512], BF16)
        e128, b_e128 = sb("p3_e128", [8, 2, 512], F32)
        ecol, b_ecol = sb("p3_ecol", [128, 64], BF16)
        erws, b_erws = sb("p3_erws", [8, 1024], BF16)
        xj = [sb(f"p3_x{i}", [128, D], F32) for i in range(2)]
        junk, b_junk = sb("p3_junk", [128, D], BF16)
        gfin, b_gfin = sb("p3_gfin", [128, D], F32)
        bgl, b_bgl = sb("p3_bgl", [128, 4], F32)
        ss2, b_ss2 = sb("p3_ss2", [128, 2], F32)
        PB = [ps(f"p3_pb{i}", [128, 1024], BF16) for i in range(2)]
        PF = [ps(f"p3_pf{i}", [128, 512], F32) for i in range(5)]
        P8 = ps("p3_p8", [128, 512], F32)
        merged = UT[:].rearrange("p g q -> p (g q)").rearrange("p (k n) -> p k n", k=8)
        yag = XpT[:].rearrange("p g q -> p (g q)")[:, 0:2048].rearrange("p (k n) -> p k n", k=4)
        ybg = XpT[:].rearrange("p g q -> p (g q)")[:, 2048:4096].rearrange("p (k n) -> p k n", k=4)
        cnt = {"w": 0, "f": 0, "b": 0, "t": 0, "h": 0, "tab": 0, "x": 0}

        def nxt(key, pool_):
            i = cnt[key]
            cnt[key] += 1
            return pool_[i % len(pool_)]

        wsrc = {"u": K.wb_in[:, :, C_U:C_U + 512], "za": K.wb_in[:, :, C_ZA:C_ZA + 512], "zb": K.wb_in[:, :, C_ZB:C_ZB + 512],
                "ga0": K.wb_in[:, :, C_GA:C_GA + 512], "gb0": K.wb_in[:, :, C_GB:C_GB + 512],
                "ga1": K.wb_in[:, :, C_GA + 512:C_GA + 1024], "gb1": K.wb_in[:, :, C_GB + 512:C_GB + 1024],
                "o0": K.wb_out[:, :, 0:512], "o1": K.wb_out[:, :, 512:1024]}
        stages = []
        for r_ in range(NB // 2):
            for h_ in range(2):
                stages += [["za"], ["zb"], ["ga0", "gb0"], ["ga1", "gb1"], ["o0", "o1"]]
        wstate = {"stage": -1, "issued": 0, "tiles": {}}

        def _issue_stage(k):
            if k >= len(stages) or k < wstate["issued"]:
                return
            for kk in range(wstate["issued"], k + 1):
                for key in stages[kk]:
                    wt, b_wt = nxt("w", wbuf)
                    S.dma(lambda e, wt=wt, key=key: e.dma_start(out=wt[:], in_=wsrc[key]), w=[b_wt])
                    wstate["tiles"][(kk, key)] = (wt, b_wt)
            wstate["issued"] = k + 1

        def wstage(expect):
            wstate["stage"] += 1
            k = wstate["stage"]
            assert stages[k] == expect, (k, stages[k], expect)
            _issue_stage(k)
            _issue_stage(k + 1)
            return [wstate["tiles"].pop((k, key)) for key in stages[k]]

        S.dma(lambda e: e.dma_start(out=gfin[:], in_=K.gfin_rep), w=[b_gfin])
        S.dma(lambda e: e.dma_start(out=bgl[:], in_=K.bglu_col), w=[b_bgl])
        S.dma(lambda e: e.dma_start(out=wua4[:], in_=K.wb_upa), w=[b_wua])
        S.dve(lambda e: e.tensor_scalar(out=wua4[:], in0=wua4[:], scalar1=4.0, scalar2=None, op0=ALU.mult), r=[b_wua], w=[b_wua])
        S.dma(lambda e: e.dma_start(out=wub4[:], in_=K.wb_upb), w=[b_wub])
        S.dma(lambda e: e.dma_start(out=wgl[:], in_=K.wb_glu), w=[b_wgl])
        S.dve(lambda e: e.tensor_scalar(out=bgl[:], in0=bgl[:], scalar1=0.5, scalar2=None, op0=ALU.mult), r=[b_bgl], w=[b_bgl])
        S.dve(lambda e: e.memset(Xin8[:], 0.0), w=[b_Xin])
        S.dma(lambda e: e.dma_start(out=e128[:], in_=K.tab_e.rearrange("t (s q) -> s t q", s=8)), w=[b_e128])
        S.dma(lambda e: e.dma_start(out=ecol[:], in_=K.ecol_in), w=[b_ecol])
        S.dma(lambda e: e.dma_start(out=erws[:], in_=K.erow_in), w=[b_erws])

        def cscale(psrc, b_psrc, tabA, tabB, b_tab, dst, b_dst, extra_r=()):
            t1, b_t1 = nxt("t", tmp)
            t2, b_t2 = nxt("t", tmp)
            npart = dst.shape[0]
            S.dve(lambda e: e.tensor_tensor(out=t1[0:npart, :], in0=psrc, in1=tabA, op=ALU.mult),
                  r=[b_psrc, b_tab] + list(extra_r), w=[b_t1])
            p4 = psrc.rearrange("p (g r q) -> p g r q", g=4, r=2)
            b4 = tabB.rearrange("p (g r q) -> p g r q", g=4, r=2)
            t4 = t2[0:npart, :].rearrange("p (g r q) -> p g r q", g=4, r=2)
            S.dve(lambda e: e.tensor_tensor(out=t4[:, :, 0, :], in0=p4[:, :, 1, :], in1=b4[:, :, 0, :], op=ALU.mult),
                  r=[b_psrc, b_tab], w=[b_t2])
            S.dve(lambda e: e.tensor_tensor(out=t4[:, :, 1, :], in0=p4[:, :, 0, :], in1=b4[:, :, 1, :], op=ALU.mult),
                  r=[b_psrc, b_tab], w=[b_t2])
            S.dve(lambda e: e.tensor_tensor(out=dst, in0=t1[0:npart, :], in1=t2[0:npart, :], op=ALU.add),
                  r=[b_t1, b_t2], w=[b_dst])

        SY2 = SY[:].rearrange("p g q -> p (g q)")
        Xp2 = UtXp[:].rearrange("p g q -> p (g q)")
        w_ = K.wj[0:1, 0:1]
        v_ = K.wj[0:1, 1:2]

        def s5_front(pos):
            S.dma(lambda e: e.dma_start(out=UtXp[:], in_=K.ut_d[pos]), w=[b_UtXp])
            for g8 in range(4):
                pb_, b_pb = nxt("b", PB)
                for gg in range(8):
                    g = g8 * 8 + gg
                    S.pe(lambda e, g=g, gg=gg, pb_=pb_: e.transpose(out=pb_[:, gg * 128:(gg + 1) * 128], in_=UtXp[:, g, :],
                                                                    identity=K.id_b[:]), r=[b_UtXp, K.b_idb], w=[b_pb])
                src = pb_[:].rearrange("p (g q) -> p g q", g=8)
                if g8 % 2 == 0:
                    S.act(lambda e, g8=g8, src=src: e.activation(out=UT[:, g8 * 8:(g8 + 1) * 8, :], in_=src, func=AF.Copy),
                          r=[b_pb], w=[b_UT])
                else:
                    S.dve(lambda e, g8=g8, src=src: e.tensor_copy(out=UT[:, g8 * 8:(g8 + 1) * 8, :], in_=src), r=[b_pb], w=[b_UT])

        def s5_sums(p8, b_p8):
            def emit_pS(sl):
                pS, b_pS = nxt("f", PF)
                for gg in range(4):
                    g = sl * 4 + gg
                    S.pe(lambda e, g=g, gg=gg: e.matmul(pS[:, gg * 128:(gg + 1) * 128], lhsT=UT[:, g, :], rhs=K.BA5[:, g, :],
                                                        start=True, stop=True), r=[b_UT, K.b_BA5], w=[b_pS])
                return pS, b_pS
            cur = emit_pS(0)
            for sl in range(8):
                tb, b_tb = nxt("tab", tabs)
                S.dma(lambda e, sl=sl, tb=tb: e.dma_start(
                    out=tb[:, 0:2, :], in_=K.tab_d[0:2, :, sl * 512:(sl + 1) * 512].rearrange("t p q -> p t q")), w=[b_tb])
                nxt_ = emit_pS(sl + 1) if sl + 1 < 8 else None
                pS, b_pS = cur
                cscale(pS[:], b_pS, tb[:, 0, :], tb[:, 1, :], b_tb, SY2[:, sl * 512:(sl + 1) * 512], b_SY)
                S.pe(lambda e, sl=sl: e.matmul(p8[0:8, :], lhsT=ecol[:, sl * 8:(sl + 1) * 8], rhs=SY2[:, sl * 512:(sl + 1) * 512],
                                               start=(sl == 0), stop=(sl == 7)), r=[b_ecol, b_SY], w=[b_p8])
                cur = nxt_

        def block(r):
            A, Bp = 2 * r, 2 * r + 1
            S.dma(lambda e: e.dma_start(out=hTb[:], in_=K.hT_d[A]), w=[b_h])
            s5_front(Bp)
            p8, b_p8 = P8
            s5_sums(p8, b_p8)
            S.dve(lambda e, p8=p8: e.tensor_copy(out=totB8[:], in_=p8[0:8, :]), r=[b_p8], w=[b_totB])
            s5_front(A)
            if "p3a" in K.flags:
                return
            p8, b_p8 = P8
            s5_sums(p8, b_p8)
            (r1, b_r1), (r2, b_r2), (r3, b_r3), (r4, b_r4) = rows
            w8 = K.wj[0:8, 0:1]
            v8 = K.wj[0:8, 1:2]
            S.dve(lambda e, p8=p8: e.tensor_scalar(out=r1[:], in0=p8[0:8, :], scalar1=v8, scalar2=None, op0=ALU.mult),
                  r=[b_p8, K.b_wj], w=[b_r1])
            S.dve(lambda e: e.scalar_tensor_tensor(out=r1[:], in0=totB8[:], scalar=w8, in1=r1[:], op0=ALU.mult, op1=ALU.add),
                  r=[b_totB, b_r1, K.b_wj], w=[b_r1])
            S.dve(lambda e: e.tensor_scalar(out=r2[:], in0=totB8[:], scalar1=v8, scalar2=None, op0=ALU.mult),
                  r=[b_totB, K.b_wj], w=[b_r2])
            S.dve(lambda e, p8=p8: e.scalar_tensor_tensor(out=r2[:], in0=p8[0:8, :], scalar=w8, in1=r2[:], op0=ALU.mult, op1=ALU.add),
                  r=[b_p8, b_r2, K.b_wj], w=[b_r2])
            S.dve(lambda e: e.tensor_tensor(out=r1[:], in0=r1[:], in1=Xin8[:], op=ALU.add), r=[b_r1, b_Xin], w=[b_r1])
            cscale(r1[:], b_r1, e128[:, 0, :], e128[:, 1, :], b_e128, r3[:], b_r3)
            S.dve(lambda e: e.tensor_scalar(out=r4[:], in0=Xin8[:], scalar1=v8, scalar2=None, op0=ALU.mult),
                  r=[b_Xin, K.b_wj], w=[b_r4])
            S.dve(lambda e: e.scalar_tensor_tensor(out=xown8[:], in0=r3[:], scalar=w8, in1=r4[:], op0=ALU.mult, op1=ALU.add),
                  r=[b_r3, b_r4, K.b_wj], w=[b_xown])
            S.dve(lambda e: e.tensor_tensor(out=r2[:], in0=r2[:], in1=r3[:], op=ALU.add), r=[b_r2, b_r3], w=[b_r2])
            cscale(r2[:], b_r2, e128[:, 0, :], e128[:, 1, :], b_e128, Xin8[:], b_Xin)
            for sl in range(8):
                tb, b_tb = nxt("tab", tabs)
                S.dma(lambda e, sl=sl, tb=tb: e.dma_start(
                    out=tb[:, 2:4, :], in_=K.tab_d[2:4, :, sl * 512:(sl + 1) * 512].rearrange("t p q -> p t q")), w=[b_tb])
                pZ, b_pZ = nxt("f", PF)
                S.pe(lambda e, sl=sl, pZ=pZ: e.matmul(pZ[:], lhsT=K.tri_bs[:], rhs=SY2[:, sl * 512:(sl + 1) * 512],
                                                      start=True, stop=False), r=[K.b_trib, b_SY], w=[b_pZ])
                S.pe(lambda e, sl=sl, pZ=pZ: e.matmul(pZ[:], lhsT=erws[0:8, sl * 128:(sl + 1) * 128], rhs=xown8[0:8, :],
                                                      start=False, stop=True), r=[b_erws, b_xown], w=[b_pZ])
                cscale(pZ[:], b_pZ, tb[:, 2, :], tb[:, 3, :], b_tb, Xp2[:, sl * 512:(sl + 1) * 512], b_UtXp)
                if sl % 2 == 1 and "p3b" not in K.flags:
                    g8 = sl // 2
                    pb_, b_pb = nxt("b", PB)
                    for gg in range(8):
                        g = g8 * 8 + gg
                        S.pe(lambda e, g=g, gg=gg, pb_=pb_: e.transpose(out=pb_[:, gg * 128:(gg + 1) * 128], in_=UtXp[:, g, :],
                                                                        identity=K.id_b[:]), r=[b_UtXp, K.b_idb], w=[b_pb])
                    src = pb_[:].rearrange("p (g q) -> p g q", g=8)
                    if g8 % 2 == 0:
                        S.act(lambda e, g8=g8, src=src: e.activation(out=XpT[:, g8 * 8:(g8 + 1) * 8, :], in_=src, func=AF.Copy),
                              r=[b_pb], w=[b_XpT])
                    else:
                        S.dve(lambda e, g8=g8, src=src: e.tensor_copy(out=XpT[:, g8 * 8:(g8 + 1) * 8, :], in_=src), r=[b_pb], w=[b_XpT])
            if "p3b" in K.flags:
                return
            for sl in range(8):
                pY, b_pY = nxt("f", PF)
                for gg in range(4):
                    g = sl * 4 + gg
                    S.pe(lambda e, g=g, gg=gg, pY=pY: e.matmul(pY[:, gg * 128:(gg + 1) * 128], lhsT=UT[:, g, :], rhs=K.T5[:, g, :],
                                                               start=True, stop=False), r=[b_UT, K.b_T5], w=[b_pY])
                    S.pe(lambda e, g=g, gg=gg, pY=pY: e.matmul(pY[:, gg * 128:(gg + 1) * 128], lhsT=XpT[:, g, :], rhs=K.CA5[:, g, :],
                                                               start=False, stop=True), r=[b_XpT, K.b_CA5], w=[b_pY])
                xs, b_xs = nxt("t", tmp)
                x2, b_x2 = nxt("t", tmp)
                S.act(lambda e, pY=pY, xs=xs: e.activation(out=xs[:], in_=pY[:], func=AF.Copy), r=[b_pY], w=[b_xs])
                S.act(lambda e, pY=pY, x2=x2: e.activation(out=x2[:], in_=pY[:], func=AF.Square), r=[b_pY], w=[b_x2])
                S.dve(lambda e, x2=x2: e.tensor_scalar(out=x2[:], in0=x2[:], scalar1=0.044715, scalar2=1.0, op0=ALU.mult, op1=ALU.add),
                      r=[b_x2], w=[b_x2])
                S.dve(lambda e, xs=xs, x2=x2: e.tensor_tensor(out=x2[:], in0=x2[:], in1=xs[:], op=ALU.mult), r=[b_x2, b_xs], w=[b_x2])
                S.act(lambda e, x2=x2: e.activation(out=x2[:], in_=x2[:], func=AF.Tanh, scale=GC), r=[b_x2], w=[b_x2])
                yg4 = SY2.rearrange("p (j g c) -> p j g c", j=8, g=32)
                for gg in range(4):
                    ygo = yg4[:, :, sl * 4 + gg, :]
                    S.dve(lambda e, ygo=ygo, xs=xs, x2=x2, gg=gg: e.scalar_tensor_tensor(
                        out=ygo, in0=x2[:, gg * 128:(gg + 1) * 128].rearrange("p (j c) -> p j c", j=8), scalar=1.0,
                        in1=xs[:, gg * 128:(gg + 1) * 128].rearrange("p (j c) -> p j c", j=8), op0=ALU.add, op1=ALU.mult),
                        r=[b_x2, b_xs], w=[b_SY])
            def emit_ybT():
                for fc in range(4):
                    pb_, b_pb = nxt("b", PB)
                    for j in range(8):
                        S.pe(lambda e, fc=fc, j=j, pb_=pb_: e.transpose(
                            out=pb_[:, j * 128:(j + 1) * 128], in_=SY2[:, j * 512 + fc * 128:j * 512 + (fc + 1) * 128],
                            identity=K.id_b[:]), r=[b_SY, K.b_idb], w=[b_pb])
                    if fc % 2 == 0:
                        S.act(lambda e, fc=fc, pb_=pb_: e.activation(out=ybT[:, fc, :], in_=pb_[:], func=AF.Copy), r=[b_pb], w=[b_ybT])
                    else:
                        S.dve(lambda e, fc=fc, pb_=pb_: e.tensor_copy(out=ybT[:, fc, :], in_=pb_[:]), r=[b_pb], w=[b_ybT])
                if r == 0 and "ybT" in K.dbg_out:
                    S.dma(lambda e: e.dma_start(out=K.dbg_out["ybT"], in_=ybT[:]), r=[b_ybT])
            if "p3c" in K.flags:
                emit_ybT()
                return
            def do_half(half):
                hs = slice(half * 512, (half + 1) * 512)
                S.dma(lambda e, r=r, hs=hs: e.dma_start(out=yaT[:], in_=K.yaT_d[r, :, :, hs]), w=[b_ya])
                ((wza, b_wza),) = wstage(["za"])
                for fc in range(4):
                    pz, b_pz = nxt("f", PF)
                    for kc in range(8):
                        S.pe(lambda e, fc=fc, kc=kc, pz=pz: e.matmul(pz[:], lhsT=wza[:, kc, fc * 128:(fc + 1) * 128], rhs=hTb[:, kc, hs],
                                                                     start=(kc == 0), stop=(kc == 7)), r=[b_wza, b_h], w=[b_pz])
                    th, b_th = nxt("h", thb)
                    t1, b_t1 = nxt("t", tmp)
                    S.act(lambda e, pz=pz, th=th: e.activation(out=th[:], in_=pz[:], func=AF.Tanh, scale=0.5), r=[b_pz], w=[b_th])
                    S.dve(lambda e, pz=pz, th=th, t1=t1: e.scalar_tensor_tensor(out=t1[:], in0=th[:], scalar=1.0, in1=pz[:],
                                                                                op0=ALU.add, op1=ALU.mult), r=[b_th, b_pz], w=[b_t1])
                    S.dve(lambda e, fc=fc, t1=t1: e.tensor_tensor(out=yag[:, fc, :], in0=t1[:], in1=yaT[:, fc, :], op=ALU.mult),
                          r=[b_t1, b_ya], w=[b_XpT])
                if half == 0:
                    emit_ybT()
                ((wzb, b_wzb),) = wstage(["zb"])
                for fc in range(4):
                    pg, b_pg = nxt("f", PF)
                    for kc in range(4):
                        S.pe(lambda e, fc=fc, kc=kc, pg=pg: e.matmul(pg[:], lhsT=wgl[:, kc, fc * 128:(fc + 1) * 128], rhs=ybT[:, kc, hs],
                                                                     start=(kc == 0), stop=(kc == 3)), r=[b_wgl, b_ybT], w=[b_pg])
                    pz, b_pz = nxt("f", PF)
                    for kc in range(8):
                        S.pe(lambda e, fc=fc, kc=kc, pz=pz: e.matmul(pz[:], lhsT=wzb[:, kc, fc * 128:(fc + 1) * 128], rhs=hTb[:, kc, hs],
                                                                     start=(kc == 0), stop=(kc == 7)), r=[b_wzb, b_h], w=[b_pz])
                    thg, b_thg = nxt("h", thb)
                    thz, b_thz = nxt("h", thb)
                    t1, b_t1 = nxt("t", tmp)
                    t2, b_t2 = nxt("t", tmp)
                    S.act(lambda e, fc=fc, pg=pg, thg=thg: e.activation(out=thg[:], in_=pg[:], func=AF.Tanh, scale=0.25,
                                                                        bias=bgl[:, fc:fc + 1]), r=[b_pg, b_bgl], w=[b_thg])
                    S.act(lambda e, pz=pz, thz=thz: e.activation(out=thz[:], in_=pz[:], func=AF.Tanh, scale=0.5), r=[b_pz], w=[b_thz])
                    S.dve(lambda e, pz=pz, thz=thz, t1=t1: e.scalar_tensor_tensor(out=t1[:], in0=thz[:], scalar=1.0, in1=pz[:],
                                                                                  op0=ALU.add, op1=ALU.mult), r=[b_thz, b_pz], w=[b_t1])
                    S.dve(lambda e, fc=fc, thg=thg, t2=t2: e.scalar_tensor_tensor(out=t2[:], in0=thg[:], scalar=1.0, in1=ybT[:, fc, hs],
                                                                                  op0=ALU.add, op1=ALU.mult), r=[b_thg, b_ybT], w=[b_t2])
                    S.dve(lambda e, fc=fc, t1=t1, t2=t2: e.tensor_tensor(out=ybg[:, fc, :], in0=t1[:], in1=t2[:], op=ALU.mult),
                          r=[b_t1, b_t2], w=[b_XpT])
                wg = {}

                def emit_G(fc):
                    gh, f4 = fc // 4, fc % 4
                    if f4 == 0:
                        wg[gh] = tuple(wstage(["ga%d" % gh, "gb%d" % gh]))
                    (wga, b_wga), (wgb, b_wgb) = wg[gh]
                    pga, b_pga = nxt("f", PF)
                    for kc in range(8):
                        S.pe(lambda e, kc=kc: e.matmul(pga[:], lhsT=wga[:, kc, f4 * 128:(f4 + 1) * 128], rhs=hTb[:, kc, hs],
                                                       start=(kc == 0), stop=(kc == 7)), r=[b_wga, b_h], w=[b_pga])
                    pgb, b_pgb = nxt("f", PF)
                    for kc in range(8):
                        S.pe(lambda e, kc=kc: e.matmul(pgb[:], lhsT=wgb[:, kc, f4 * 128:(f4 + 1) * 128], rhs=hTb[:, kc, hs],
                                                       start=(kc == 0), stop=(kc == 7)), r=[b_wgb, b_h], w=[b_pgb])
                    tha, b_tha = nxt("h", thb)
                    thb_, b_thb = nxt("h", thb)
                    S.act(lambda e: e.activation(out=tha[:], in_=pga[:], func=AF.Tanh, scale=0.5), r=[b_pga], w=[b_tha])
                    S.act(lambda e: e.activation(out=thb_[:], in_=pgb[:], func=AF.Tanh, scale=0.5), r=[b_pgb], w=[b_thb])
                    return tha, b_tha, thb_, b_thb

                def emit_U(fc, ths):
                    tha, b_tha, thb_, b_thb = ths
                    pua, b_pua = nxt("f", PF)
                    for kc in range(4):
                        S.pe(lambda e, kc=kc: e.matmul(pua[:], lhsT=wua4[:, kc, fc * 128:(fc + 1) * 128], rhs=yag[:, kc, :],
                                                       start=(kc == 0), stop=(kc == 3)), r=[b_wua, b_XpT], w=[b_pua])
                    pub, b_pub = nxt("f", PF)
                    for kc in range(4):
                        S.pe(lambda e, kc=kc: e.matmul(pub[:], lhsT=wub4[:, kc, fc * 128:(fc + 1) * 128], rhs=ybg[:, kc, :],
                                                       start=(kc == 0), stop=(kc == 3)), r=[b_wub, b_XpT], w=[b_pub])
                    m1, b_m1 = nxt("t", tmp)
                    m2, b_m2 = nxt("t", tmp)
                    S.dve(lambda e: e.scalar_tensor_tensor(out=m1[:], in0=tha[:], scalar=1.0, in1=pua[:],
                                                           op0=ALU.add, op1=ALU.mult), r=[b_tha, b_pua], w=[b_m1])
                    S.dve(lambda e: e.scalar_tensor_tensor(out=m2[:], in0=thb_[:], scalar=1.0, in1=pub[:],
                                                           op0=ALU.add, op1=ALU.mult), r=[b_thb, b_pub], w=[b_m2])
                    S.pool(lambda e: e.tensor_tensor(out=merged[:, fc, :], in0=m1[:], in1=m2[:], op=ALU.add),
                           r=[b_m1, b_m2], w=[b_UT])

                ths = {0: emit_G(0)}
                for fc in range(8):
                    if fc + 1 < 8:
                        ths[fc + 1] = emit_G(fc + 1)
                    emit_U(fc, ths.pop(fc))
                if "p3d" in K.flags:
                    return
                if r == 0 and half == 0:
                    if "merged" in K.dbg_out:
                        S.dma(lambda e: e.dma_start(out=K.dbg_out["merged"], in_=merged), r=[b_UT])
                    if "yag" in K.dbg_out:
                        S.dma(lambda e: e.dma_start(out=K.dbg_out["yag"], in_=yag), r=[b_XpT])
                    if "ybg" in K.dbg_out:
                        S.dma(lambda e: e.dma_start(out=K.dbg_out["ybg"], in_=ybg), r=[b_XpT])
                (wo0, b_wo0), (wo1, b_wo1) = wstage(["o0", "o1"])
                for jj in range(4):
                    j = half * 4 + jj
                    x_t, b_x = nxt("x", xj)
                    S.dma(lambda e, A=A, j=j, x_t=x_t: e.dma_start(
                        out=x_t[:], in_=K.x[A * 1024:(A + 1) * 1024, :].rearrange("(n j) d -> n j d", j=8)[:, j, :]), w=[b_x])
                    for fh in range(2):
                        wo, b_wo = (wo0, b_wo0) if fh == 0 else (wo1, b_wo1)
                        po, b_po = nxt("f", PF)
                        for kc in range(8):
                            S.pe(lambda e, jj=jj, kc=kc, po=po, wo=wo: e.matmul(po[:], lhsT=merged[:, kc, jj * 128:(jj + 1) * 128], rhs=wo[:, kc, :],
                                                                               start=(kc == 0), stop=(kc == 7)), r=[b_UT, b_wo], w=[b_po])
                        fs = slice(fh * 512, (fh + 1) * 512)
                        S.dve(lambda e, po=po, x_t=x_t, fs=fs: e.tensor_tensor(out=x_t[:, fs], in0=x_t[:, fs], in1=po[:], op=ALU.add),
                              r=[b_po, b_x], w=[b_x])
                    if "p3e" in K.flags:
                        continue
                    S.act(lambda e, x_t=x_t: e.activation(out=junk[:], in_=x_t[:], func=AF.Square, accum_out=ss2[:, 0:1]),
                          r=[b_x], w=[b_junk, b_ss2])
                    S.act(lambda e: e.activation(out=ss2[:, 1:2], in_=ss2[:, 0:1], func=AF.Sqrt, scale=1.0 / D, bias=K.cst[:, 0:1]),
                          r=[b_ss2, K.b_cst], w=[b_ss2])
                    S.dve(lambda e: e.reciprocal(out=ss2[:, 1:2], in_=ss2[:, 1:2]), r=[b_ss2], w=[b_ss2])
                    S.dve(lambda e, x_t=x_t: e.scalar_tensor_tensor(out=x_t[:], in0=x_t[:], scalar=ss2[:, 1:2], in1=gfin[:],
                                                                    op0=ALU.mult, op1=ALU.mult), r=[b_x, b_ss2, b_gfin], w=[b_x])
                    if "p3f" in K.flags:
                        continue
                    S.dma(lambda e, r=r, j=j, x_t=x_t: e.dma_start(
                        out=K.out[r * 1024:(r + 1) * 1024, :].rearrange("(n j) d -> n j d", j=8)[:, j, :], in_=x_t[:]),
                        r=[b_x], q="aq")
            for half in range(2):
                do_half(half)

        for r in range(NB // 2):
            block(r)
        S.flush()


def host_consts():
    bf = ml_dtypes.bfloat16
    n = np.arange(128)
    tri = (n[:, None] < n[None, :]).astype(np.float32)
    mle = (n[:, None] <= n[None, :]).astype(np.float32)
    mlt = tri
    cm5 = np.zeros((128, 5, 512), np.float32)
    for a in range(5):
        for i in range(4):
            cm5[:, a, i * 128:(i + 1) * 128] = -30000.0 * (1.0 - (mlt if i < a else mle))
    ic = np.arange(128) // 16
    tmask = (ic[None, :] >= ic[:, None]).astype(np.float32)
    nvec = np.stack([n, n + 1, n + 2, n + 3], axis=1).astype(np.float32)
    ecol = np.zeros((128, 8, 8), np.float32)
    erow = np.zeros((8, 8, 128), np.float32)
    for sl in range(8):
        ecol[:, sl, sl] = 1.0
        erow[sl, sl, :] = 1.0
    return {"ecol": ecol.reshape(128, 64).astype(bf), "erowsel": erow.reshape(8, 1024).astype(bf),
            "ident_b": np.eye(128, dtype=np.float32).astype(bf), "ident_f": np.eye(128, dtype=np.float32),
            "tri_f": tri, "cm5": cm5.astype(bf), "tmask": tmask, "nvec": nvec}


def core_inputs(inp, b, NB=8, jc=0):
    T = NB * 1024
    f = np.float32
    col = lambda v, k: np.ascontiguousarray(np.asarray(v, f).reshape(k, 128).T)
    rep = lambda v: np.ascontiguousarray(np.broadcast_to(np.asarray(v, f)[None, :], (128, v.shape[-1])))
    b_ada = np.asarray(inp["b_ada"][0], f)
    a_re, a_im = np.asarray(inp["a_re"][0], f), np.asarray(inp["a_im"][0], f)
    lam = np.stack([np.stack([a_re, a_re], 1), np.stack([a_im, a_im], 1)], 1)
    c_re, c_im = np.asarray(inp["c_re"][0], f).reshape(512, 64), np.asarray(inp["c_im"][0], f).reshape(512, 64)
    s5c = np.stack([np.stack([c_re, c_re], 1), np.stack([c_im, c_im], 1)], 1)
    b_re, b_im = np.asarray(inp["b_re"][0], f), np.asarray(inp["b_im"][0], f)
    s5b = np.concatenate([b_re.transpose(1, 0, 2), b_im.transpose(1, 0, 2)], 0)
    d = np.asarray(inp["d_skip"][0], f).reshape(32, 16)
    s5d = np.tile(d.T[None, :, :], (8, 1, 1)).reshape(128, 32)
    m = {
        "x": np.ascontiguousarray(np.asarray(inp["x"][b, :T], f).reshape(NB // 2, 2, 1024, D)[:, ::(-1 if jc else 1)].reshape(T, D)),
        "wj": np.ascontiguousarray(np.broadcast_to(np.array([float(jc), 1.0 - jc], f)[None, :], (128, 2))),
        "mB": np.full((128, 512), 0.0 if jc else -30000.0, f).astype(ml_dtypes.bfloat16),
        "cT": col(inp["c"][b], 8),
        "w_ada": np.ascontiguousarray(np.asarray(inp["w_ada"][0], f)),
        "bada_col": col(b_ada, 24),
        "bgate_rep": rep(b_ada[2 * D:]),
        "gn_col": col(inp["g_norm"][0], 8),
        "w_in": np.ascontiguousarray(np.asarray(inp["w_in"][0], f)),
        "bf_rep": np.ascontiguousarray(np.tile(np.asarray(inp["b_f"][0], f), 8)[None, :].repeat(128, 0)),
        "w_glu": np.ascontiguousarray(np.asarray(inp["w_glu"][0], f)),
        "bglu_col": col(inp["b_glu"][0], 4),
        "w_up_a": np.ascontiguousarray(np.asarray(inp["w_up_a"][0], f)),
        "w_up_b": np.ascontiguousarray(np.asarray(inp["w_up_b"][0], f)),
        "w_out": np.ascontiguousarray(np.asarray(inp["w_out"][0], f)),
        "gfin_rep": rep(np.asarray(inp["g_final"], f)),
        "s5_lam": np.ascontiguousarray(lam),
        "s5_ldt": rep(np.asarray(inp["log_dt"][0], f)),
        "s5_b": np.ascontiguousarray(s5b),
        "s5_c": np.ascontiguousarray(s5c),
        "s5_d": np.ascontiguousarray(s5d),
        "s5_bs": np.ascontiguousarray(np.concatenate([s5b[64:], s5b[:64]], 0)),
        "s5_zrep": np.ascontiguousarray(np.broadcast_to(np.stack([
            a_re.reshape(-1), a_im.reshape(-1),
            np.repeat(np.asarray(inp["log_dt"][0], f), 64)], 0)[None], (128, 3, 2048))),
    }
    m.update(host_consts())
    return m


_NC_CACHE = {}


def kernel(**inputs):
    NB = 8
    if NB not in _NC_CACHE:
        _NC_CACHE[NB] = build(NB)
    nc = _NC_CACHE[NB]
    in_maps = [core_inputs(inputs, c // 2, NB, c % 2) for c in range(NCORES)]
    res = run_bass_kernel_spmd(nc, in_maps, core_ids=list(range(NCORES)))
    out = np.empty((4, NB // 2, 2, 1024, D), np.float32)
    for c in range(NCORES):
        out[c // 2, :, c % 2] = np.asarray(res.results[c]["out"], np.float32).reshape(NB // 2, 1024, D)
    return out.reshape(4, NB * 1024, D)
```
